# Optimizing a Trainium2 kernel written in Bass

```python
import math
import jax
import jax.numpy as jnp
from jax import lax
import numpy as np

D_MODEL = 2048
BATCH = 4
SEQ = 4096
DEPTH = 2

A_HEADS = 8
A_HEAD_DIM = 64
A_WIDTH = A_HEADS * A_HEAD_DIM
A_DECAY_LORA = 32
A_ICLR_LORA = 32
A_GATE_LORA = 96
A_GN_EPS = 64e-5
B_HEADS = 4
B_KEY_DIM = 64
B_VAL_DIM = 128
B_KEY_WIDTH = B_HEADS * B_KEY_DIM
B_WIDTH = B_HEADS * B_VAL_DIM
B_GATE_LORA = 16
B_GATE_NORMALIZER = 16.0
B_CHUNK = 64
C_HEADS = 4
C_HEAD_DIM = 128
C_WIDTH = C_HEADS * C_HEAD_DIM
C_CONV = 4
C_CHUNK = 64
N_BRANCH = 3
BRANCH_WIDTH = 512
D_FF = 5632
FFN_CONV = 3
NORM_EPS = 1e-5
L2_EPS = 1e-6
DEEPNORM_ALPHA = (2 * DEPTH) ** 0.25
DEEPNORM_BETA = (8 * DEPTH) ** -0.25

A_SPLITS = (A_WIDTH, A_WIDTH, A_WIDTH, A_DECAY_LORA, A_ICLR_LORA, A_GATE_LORA)
B_SPLITS = (B_KEY_WIDTH, B_KEY_WIDTH, B_WIDTH, B_GATE_LORA, B_WIDTH)
C_SPLITS = (3 * C_WIDTH, C_HEADS, C_HEADS, C_WIDTH)
A_IN = 3 * A_WIDTH + A_DECAY_LORA + A_ICLR_LORA + A_GATE_LORA
B_IN = 2 * B_KEY_WIDTH + B_WIDTH + B_GATE_LORA + B_WIDTH
C_IN = 3 * C_WIDTH + 2 * C_HEADS + C_WIDTH
N_IN = A_IN + B_IN + C_IN

kernel_name = "hybrid_rwkv7_gla_gdn_convffn_deepnorm"

F32 = jnp.float32


def split_cols(p, sizes):
    return jnp.split(p, [int(i) for i in np.cumsum(sizes)[:-1]], axis=-1)


def heads(t, n):
    return t.reshape(t.shape[:-1] + (n, t.shape[-1] // n))


def layer_norm(x, g, b):
    xf = x.astype(F32)
    mu = jnp.mean(xf, -1, keepdims=True)
    var = jnp.mean(jnp.square(xf - mu), -1, keepdims=True)
    return ((xf - mu) * lax.rsqrt(var + NORM_EPS)).astype(x.dtype) * g + b


def rms_norm(x, g):
    xf = x.astype(F32)
    return xf * lax.rsqrt(jnp.mean(jnp.square(xf), -1, keepdims=True) + NORM_EPS) * g


def l2norm(x):
    xf = x.astype(F32)
    return xf * lax.rsqrt(jnp.sum(jnp.square(xf), -1, keepdims=True) + L2_EPS)


def token_shift(x):
    return jnp.pad(x, ((0, 0), (1, 0), (0, 0)))[:, :-1]


def causal_dwconv(x, w):
    k = w.shape[0]
    return lax.conv_general_dilated(
        x, w[:, None, :], window_strides=(1,), padding=[(k - 1, 0)],
        dimension_numbers=('NWC', 'WIO', 'NWC'), feature_group_count=x.shape[-1])


def to_chunks(t, c):
    b, s, h, d = t.shape
    return t.reshape(b, s // c, c, h, d).transpose(0, 3, 1, 2, 4)


def to_chunks_scalar(t, c):
    b, s, h = t.shape
    return t.reshape(b, s // c, c, h).transpose(0, 3, 1, 2)


def from_chunks(t):
    b, h, n, c, d = t.shape
    return t.transpose(0, 2, 3, 1, 4).reshape(b, n * c, h, d)


def rwkv7_branch(pa, w0, w2, a0, a2, g2, k_k, k_a, r_k, lnx_w, lnx_b):
    bsz, seq = pa.shape[0], pa.shape[1]
    r, k, v, wl, al, gl = split_cols(pa, A_SPLITS)
    w = -jax.nn.softplus(-(w0 + jnp.tanh(wl) @ w2)) - 0.5
    decay = jnp.exp(-jnp.exp(w.astype(F32)))
    a = jax.nn.sigmoid(a0 + al @ a2)
    g = jax.nn.sigmoid(gl) @ g2
    kk = l2norm(heads(k * k_k, A_HEADS))
    k = k * (1 + (a - 1) * k_a)
    rh, kh, vh = heads(r, A_HEADS), heads(k, A_HEADS), heads(v, A_HEADS)
    ah, dh = heads(a, A_HEADS), heads(decay, A_HEADS)

    def step(state, inp):
        r_t, w_t, k_t, v_t, kk_t, a_t = inp
        sa = jnp.einsum('bhvk,bhk->bhv', state, -kk_t)
        state = (state * w_t[:, :, None, :]
                 + sa[..., None] * (kk_t * a_t)[:, :, None, :]
                 + v_t[..., None] * k_t[:, :, None, :])
        return state, jnp.einsum('bhvk,bhk->bhv', state, r_t)

    xs = tuple(jnp.moveaxis(t, 1, 0) for t in (rh, dh, kh, vh, kk, ah))
    state0 = jnp.zeros((bsz, A_HEADS, A_HEAD_DIM, A_HEAD_DIM), F32)
    _, o = lax.scan(step, state0, xs)
    o = jnp.moveaxis(o, 0, 1)
    mu = jnp.mean(o, -1, keepdims=True)
    var = jnp.mean(jnp.square(o - mu), -1, keepdims=True)
    on = ((o - mu) * lax.rsqrt(var + A_GN_EPS)).reshape(bsz, seq, A_WIDTH) * lnx_w + lnx_b
    bonus = (jnp.sum(rh * kh * r_k, -1, keepdims=True) * vh).reshape(bsz, seq, A_WIDTH)
    return ((on + bonus) * g).astype(pa.dtype)


def gla_chunked(q, k, v, gk):
    qc, kc, vc = (to_chunks(t.astype(F32), B_CHUNK) for t in (q, k, v))
    bcum = jnp.cumsum(to_chunks(gk, B_CHUNK), axis=-2)
    qd = qc * jnp.exp(bcum)
    kd = kc * jnp.exp(-bcum)
    causal = jnp.tril(jnp.ones((B_CHUNK, B_CHUNK), bool))
    att = jnp.where(causal, jnp.einsum('bhncd,bhnsd->bhncs', qd, kd), 0.0)
    intra = jnp.einsum('bhncs,bhnse->bhnce', att, vc)
    b_last = bcum[..., -1:, :]
    upd = jnp.einsum('bhncd,bhnce->bhnde', kc * jnp.exp(b_last - bcum), vc)
    chunk_decay = jnp.exp(b_last[..., 0, :])

    def step(state, inp):
        dec, u = inp
        return state * dec[..., None] + u, state

    bsz = q.shape[0]
    state0 = jnp.zeros((bsz, B_HEADS, B_KEY_DIM, B_VAL_DIM), F32)
    _, s_prev = lax.scan(step, state0, (jnp.moveaxis(chunk_decay, 2, 0), jnp.moveaxis(upd, 2, 0)))
    s_prev = jnp.moveaxis(s_prev, 0, 2)
    inter = jnp.einsum('bhncd,bhnde->bhnce', qd, s_prev)
    return from_chunks(intra + inter)


def gla_branch(pb, gk_w2, gk_b, norm_w):
    bsz, seq = pb.shape[0], pb.shape[1]
    q, k, v, gkl, g = split_cols(pb, B_SPLITS)
    gk = jax.nn.log_sigmoid((gkl @ gk_w2 + gk_b).astype(F32)) / B_GATE_NORMALIZER
    o = gla_chunked(heads(q, B_HEADS) * B_KEY_DIM ** -0.5, heads(k, B_HEADS),
                    heads(v, B_HEADS), heads(gk, B_HEADS))
    o = rms_norm(o, norm_w).reshape(bsz, seq, B_WIDTH)
    return (o * jax.nn.silu(g)).astype(pb.dtype)


def gated_delta_chunked(q, k, v, g, beta):
    qc, kc, vc = (to_chunks(t.astype(F32), C_CHUNK) for t in (q, k, v))
    bc = to_chunks_scalar(beta, C_CHUNK)
    gam = jnp.cumsum(to_chunks_scalar(g, C_CHUNK), axis=-1)
    diff = gam[..., :, None] - gam[..., None, :]
    incl = jnp.tril(jnp.ones((C_CHUNK, C_CHUNK), bool))
    strict = jnp.tril(jnp.ones((C_CHUNK, C_CHUNK), bool), -1)
    dec_incl = jnp.exp(jnp.where(incl, diff, -jnp.inf))
    dec_strict = jnp.exp(jnp.where(strict, diff, -jnp.inf))
    kb = kc * bc[..., None]
    m = jnp.einsum('bhnid,bhnjd->bhnij', kb, kc) * dec_strict
    eye = jnp.eye(C_CHUNK, dtype=F32)
    rhs = jnp.concatenate([vc * bc[..., None], kb * jnp.exp(gam)[..., None]], axis=-1)
    sol = lax.linalg.triangular_solve(eye + m, rhs, left_side=True, lower=True)
    u, w = sol[..., :C_HEAD_DIM], sol[..., C_HEAD_DIM:]
    qk = jnp.einsum('bhnid,bhnjd->bhnij', qc, kc) * dec_incl
    qg = qc * jnp.exp(gam)[..., None]
    kg = kc * jnp.exp(gam[..., -1:] - gam)[..., None]
    last = jnp.exp(gam[..., -1])

    def step(state, inp):
        u_i, w_i, qk_i, qg_i, kg_i, last_i = inp
        v_new = u_i - jnp.einsum('bhcd,bhde->bhce', w_i, state)
        o = (jnp.einsum('bhcd,bhde->bhce', qg_i, state)
             + jnp.einsum('bhcs,bhse->bhce', qk_i, v_new))
        state = state * last_i[..., None, None] + jnp.einsum('bhcd,bhce->bhde', kg_i, v_new)
        return state, o

    bsz = q.shape[0]
    xs = tuple(jnp.moveaxis(t, 2, 0) for t in (u, w, qk, qg, kg, last))
    state0 = jnp.zeros((bsz, C_HEADS, C_HEAD_DIM, C_HEAD_DIM), F32)
    _, o = lax.scan(step, state0, xs)
    return from_chunks(jnp.moveaxis(o, 0, 2))


def gdn_branch(pc, conv_w, a_log, dt_bias, norm_w):
    bsz, seq = pc.shape[0], pc.shape[1]
    qkv, a, b, z = split_cols(pc, C_SPLITS)
    qkv = jax.nn.silu(causal_dwconv(qkv, conv_w))
    q, k, v = jnp.split(qkv, 3, axis=-1)
    q = l2norm(heads(q, C_HEADS)) * C_HEAD_DIM ** -0.5
    k = l2norm(heads(k, C_HEADS))
    g = -jnp.exp(a_log.astype(F32)) * jax.nn.softplus(a.astype(F32) + dt_bias)
    beta = jax.nn.sigmoid(b.astype(F32))
    o = gated_delta_chunked(q, k, heads(v, C_HEADS), g, beta)
    o = rms_norm(o, norm_w) * jax.nn.silu(heads(z, C_HEADS).astype(F32))
    return o.reshape(bsz, seq, C_WIDTH).astype(pc.dtype)


def setup_inputs(seed: int = 0) -> dict:
    key = jax.random.key(seed)
    ks = iter(jax.random.split(key, 40))

    def nrm(shape, scale):
        return scale * jax.random.normal(next(ks), shape, F32)

    def unif(shape, lo, hi):
        return jax.random.uniform(next(ks), shape, F32, lo, hi)

    L, D = DEPTH, D_MODEL
    dt = jnp.exp(unif((L, C_HEADS), math.log(1e-3), math.log(1e-1)))
    return {
        'x': nrm((BATCH, SEQ, D), 1.0),
        'ln_in_g': 1.0 + nrm((D,), 0.02),
        'ln_in_b': nrm((D,), 0.02),
        'w_in': nrm((L, D, N_IN), D ** -0.5),
        'mu_a': unif((L, A_IN), 0.0, 1.0),
        'a_w0': -1.0 + nrm((L, A_WIDTH), 0.5),
        'a_w2': nrm((L, A_DECAY_LORA, A_WIDTH), 0.5 * A_DECAY_LORA ** -0.5),
        'a_a0': nrm((L, A_WIDTH), 0.5),
        'a_a2': nrm((L, A_ICLR_LORA, A_WIDTH), 0.5 * A_ICLR_LORA ** -0.5),
        'a_g2': nrm((L, A_GATE_LORA, A_WIDTH), A_GATE_LORA ** -0.5),
        'a_kk': 0.85 + nrm((L, A_WIDTH), 0.05),
        'a_ka': 1.0 + nrm((L, A_WIDTH), 0.05),
        'a_rk': nrm((L, A_HEADS, A_HEAD_DIM), 0.1),
        'a_lnx_w': 1.0 + nrm((L, A_WIDTH), 0.02),
        'a_lnx_b': nrm((L, A_WIDTH), 0.02),
        'b_gk_w2': nrm((L, B_GATE_LORA, B_KEY_WIDTH), B_GATE_LORA ** -0.5),
        'b_gk_b': nrm((L, B_KEY_WIDTH), 0.1),
        'b_norm_w': 1.0 + nrm((L, B_VAL_DIM), 0.02),
        'c_conv_w': nrm((L, C_CONV, 3 * C_WIDTH), C_CONV ** -0.5),
        'c_a_log': jnp.log(unif((L, C_HEADS), 1.0, 16.0)),
        'c_dt_bias': dt + jnp.log(-jnp.expm1(-dt)),
        'c_norm_w': 1.0 + nrm((L, C_HEAD_DIM), 0.02),
        'w_gate': nrm((L, N_BRANCH, D, D), D ** -0.5),
        'w_branch': nrm((L, N_BRANCH, BRANCH_WIDTH, D), BRANCH_WIDTH ** -0.5),
        'w_out': nrm((L, D, D), DEEPNORM_BETA * D ** -0.5),
        'ln1_g': 1.0 + nrm((L, D), 0.02),
        'ln1_b': nrm((L, D), 0.02),
        'w_up': nrm((L, D, 2 * D_FF), D ** -0.5),
        'ffn_conv_w': nrm((L, FFN_CONV, 2 * D_FF), FFN_CONV ** -0.5),
        'ffn_conv_b': nrm((L, 2 * D_FF), 0.02),
        'w_down': nrm((L, D_FF, D), DEEPNORM_BETA * D_FF ** -0.5),
        'ln2_g': 1.0 + nrm((L, D), 0.02),
        'ln2_b': nrm((L, D), 0.02),
    }


def reference(x, ln_in_g, ln_in_b, w_in, mu_a, a_w0, a_w2, a_a0, a_a2, a_g2, a_kk, a_ka,
              a_rk, a_lnx_w, a_lnx_b, b_gk_w2, b_gk_b, b_norm_w, c_conv_w, c_a_log,
              c_dt_bias, c_norm_w, w_gate, w_branch, w_out, ln1_g, ln1_b, w_up,
              ffn_conv_w, ffn_conv_b, w_down, ln2_g, ln2_b):
    h = layer_norm(x, ln_in_g, ln_in_b)
    for l in range(DEPTH):
        p = h @ w_in[l]
        pa, pb, pc = split_cols(p, (A_IN, B_IN, C_IN))
        pa = pa + (token_shift(pa) - pa) * mu_a[l]
        ya = rwkv7_branch(pa, a_w0[l], a_w2[l], a_a0[l], a_a2[l], a_g2[l], a_kk[l],
                          a_ka[l], a_rk[l], a_lnx_w[l], a_lnx_b[l])
        yb = gla_branch(pb, b_gk_w2[l], b_gk_b[l], b_norm_w[l])
        yc = gdn_branch(pc, c_conv_w[l], c_a_log[l], c_dt_bias[l], c_norm_w[l])
        merged = jnp.zeros_like(h)
        for i, y in enumerate((ya, yb, yc)):
            merged = merged + jax.nn.sigmoid(h @ w_gate[l, i]) * (y @ w_branch[l, i])
        h = layer_norm(DEEPNORM_ALPHA * h + merged @ w_out[l], ln1_g[l], ln1_b[l])
        u = causal_dwconv(h @ w_up[l], ffn_conv_w[l]) + ffn_conv_b[l]
        gate, up = jnp.split(u, 2, axis=-1)
        h = layer_norm(DEEPNORM_ALPHA * h + (jax.nn.silu(gate) * up) @ w_down[l],
                       ln2_g[l], ln2_b[l])
    return h
```

```python
import numpy as np
import concourse.bass as bass
import concourse.mybir as mybir
from concourse.bass_utils import run_bass_kernel_spmd

F32 = mybir.dt.float32
BF16 = mybir.dt.bfloat16
AF = mybir.ActivationFunctionType
ALU = mybir.AluOpType

EPOCH = 20000
NDMA = 16
SAME_ENGINE_SYNC = True
SCHED = True

D = 2048
KC = 16
TT = 512
CH = 64
NCH = TT // CH
DFF = 5632
NIN = 5304
ALPHA = 4.0 ** 0.25
A_OFF, B_OFF, C_OFF = 0, 1696, 3248


class Buf:
    __slots__ = ("t", "lw", "rd", "name", "psum", "pe_rt")

    def __init__(self, t, name="", psum=False):
        self.t = t
        self.lw = None
        self.rd = {}
        self.name = name
        self.psum = psum
        self.pe_rt = None

    def __getitem__(self, k):
        return self.t[k]


class KB:
    def __init__(self, nc):
        self.nc = nc
        self.engs = ("pe", "act", "dve", "pool", "sp")
        self.prog = {e: [] for e in self.engs}
        self.cnt = {e: 0 for e in self.engs}
        self.sems = {e: [] for e in self.engs}
        self.waited = {}
        self.dma_sems = [nc.alloc_semaphore(f"dq{j}") for j in range(2 * NDMA)]
        self.dma_val = [0] * (2 * NDMA)
        self.dma_rr = {"sp": 0, "pool": 0, "act": 0}
        self.out_tokens = []
        self.nbuf = 0
        self.pending = None

    def sb(self, shape, dtype=F32, name=None):
        self.nbuf += 1
        name = (name or f"sb{self.nbuf}") + "_s"
        return Buf(self.nc.alloc_sbuf_tensor(name, list(shape), dtype), name)

    def ps(self, shape, dtype=F32, name=None):
        self.nbuf += 1
        name = name or f"ps{self.nbuf}"
        return Buf(self.nc.alloc_psum_tensor(name, list(shape), dtype), name, psum=True)

    def dram(self, name, shape, dtype=F32, kind="Internal"):
        t = self.nc.dram_tensor(name, list(shape), dtype, kind=kind)
        return Buf(t.ap(), name)

    def _sem(self, E, ep):
        while len(self.sems[E]) <= ep:
            self.sems[E].append(self.nc.alloc_semaphore(f"s_{E}_{len(self.sems[E])}"))
        return self.sems[E][ep]

    def _resolve(self, E, deps):
        need_c = {}
        need_d = {}
        for tok in deps:
            if tok is None:
                continue
            if tok[0] == "c":
                _, W, g = tok
                if W == E and (E == "pe" or not SAME_ENGINE_SYNC):
                    continue
                if self.waited.get((E, W), -1) >= g:
                    continue
                if need_c.get(W, -1) < g:
                    need_c[W] = g
            else:
                _, j, v = tok
                if self.waited.get((E, "d", j), 0) >= v:
                    continue
                if need_d.get(j, 0) < v:
                    need_d[j] = v
        waits = []
        for W, g in need_c.items():
            self.waited[(E, W)] = g
            ep, v = divmod(g, EPOCH)
            waits.append((self._sem(W, ep), v + 1))
        for j, v in need_d.items():
            self.waited[(E, "d", j)] = v
            waits.append((self.dma_sems[j], v))
        return waits

    def _deps(self, reads, writes, E=None):
        deps = []
        for b in reads:
            if b.lw is not None:
                deps.append(b.lw)
            if b.psum:
                for key, tok in b.rd.items():
                    if key != E:
                        deps.append(tok)
        for b in writes:
            if b.lw is not None:
                deps.append(b.lw)
            deps.extend(b.rd.values())
        return deps

    def begin_sched(self):
        assert self.pending is None
        self.pending = []

    def end_sched(self):
        pend = self.pending
        self.pending = None
        if not pend:
            return
        lastw = {}
        readers = {}
        n = len(pend)
        level = [0] * n
        succ = [[] for _ in range(n)]
        for i, (kind, args, reads, writes) in enumerate(pend):
            deps = set()
            for b in reads:
                if id(b) in lastw:
                    deps.add(lastw[id(b)])
            for b in writes:
                if id(b) in lastw:
                    deps.add(lastw[id(b)])
                for r in readers.get(id(b), ()):
                    deps.add(r)
            deps.discard(i)
            lv = 0
            for d in deps:
                succ[d].append(i)
                if level[d] + 1 > lv:
                    lv = level[d] + 1
            level[i] = lv
            for b in reads:
                readers.setdefault(id(b), []).append(i)
            for b in writes:
                lastw[id(b)] = i
                readers[id(b)] = []
        height = [0] * n
        for i in range(n - 1, -1, -1):
            h = 0
            for j in succ[i]:
                if height[j] + 1 > h:
                    h = height[j] + 1
            height[i] = h
        order = sorted(range(n), key=lambda i: (level[i], -height[i], i))
        for i in order:
            kind, args, reads, writes = pend[i]
            if kind == "op":
                self.op(*args)
            else:
                self.dma(*args)

    def op(self, E, fn, reads=(), writes=(), rt=None):
        if self.pending is not None:
            self.pending.append(("op", (E, fn, reads, writes, rt), list(reads), list(writes)))
            return None
        deps = self._deps(reads, writes, E)
        force = []
        if E == "pe" and rt is not None:
            for b in writes:
                if b.psum:
                    if b.pe_rt is not None and b.pe_rt != rt and b.lw is not None and b.lw[1] == "pe":
                        force.append(b.lw)
                    b.pe_rt = rt
        waits = self._resolve(E, deps)
        for tok in force:
            g = tok[2]
            if self.waited.get(("pe", "pe"), -1) < g:
                self.waited[("pe", "pe")] = g
                ep, v = divmod(g, EPOCH)
                waits.append((self._sem("pe", ep), v + 1))
        g = self.cnt[E]
        self.cnt[E] += 1
        ep, v = divmod(g, EPOCH)
        tok = ("c", E, g)
        self.prog[E].append((waits, fn, self._sem(E, ep), 1))
        for b in reads:
            b.rd[E] = tok
        for b in writes:
            b.lw = tok
            b.rd = {}
        return tok

    def dma(self, Q, out, in_, reads=(), writes=(), is_output=False):
        if self.pending is not None:
            self.pending.append(("dma", (Q, out, in_, reads, writes, is_output), list(reads), list(writes)))
            return None
        deps = self._deps(reads, writes)
        base = NDMA if Q == "pool" else 0
        j = base + self.dma_rr[Q]
        self.dma_rr[Q] = (self.dma_rr[Q] + 1) % NDMA
        if self.dma_val[j] > 0:
            deps.append(("d", j, self.dma_val[j]))
        waits = self._resolve(Q, deps)
        self.dma_val[j] += 16
        tok = ("d", j, self.dma_val[j])
        self.prog[Q].append((waits, lambda e: e.dma_start(out=out, in_=in_), self.dma_sems[j], 16))
        for b in reads:
            b.rd[("d", j)] = tok
        for b in writes:
            b.lw = tok
            b.rd = {}
        if is_output:
            self.out_tokens.append(tok)
        return tok

    def barrier_dma(self):
        toks = [("d", j, v) for j, v in enumerate(self.dma_val) if v > 0]
        for E in self.engs:
            for (h, v) in self._resolve(E, toks):
                self.prog[E].append(([(h, v)], None, None, 0))

    def finish(self):
        toks = list(self.out_tokens) + [("d", j, v) for j, v in enumerate(self.dma_val) if v > 0]
        for (h, v) in self._resolve("sp", toks):
            self.prog["sp"].append(([(h, v)], None, None, 0))

    def emit(self):
        nc = self.nc
        self.finish()
        with nc.Block() as block:
            decs = dict(sp=block.sync, act=block.scalar, dve=block.vector,
                        pool=block.gpsimd, pe=block.tensor)
            for E in self.engs:
                prog = self.prog[E]

                def body(eng, prog=prog):
                    for waits, fn, h, inc in prog:
                        for (wh, wv) in waits:
                            eng.wait_ge(wh, wv)
                        if fn is not None:
                            fn(eng).then_inc(h, inc)

                decs[E](body)

    def mm(self, out, lhsT, rhs, start, stop, R, W):
        rt = (lhsT.base_partition(), lhsT.partition_size())
        return self.op("pe", lambda e: e.matmul(out, lhsT, rhs, start=start, stop=stop), R, W, rt=rt)

    def tr(self, out, in_, ident, R, W):
        rt = (in_.base_partition(), in_.partition_size())
        return self.op("pe", lambda e: e.transpose(out, in_, ident), R, W, rt=rt)

    def act(self, out, in_, func, R, W, bias=None, scale=None):
        kw = {}
        if bias is not None:
            kw["bias"] = bias
        if scale is not None:
            kw["scale"] = scale
        return self.op("act", lambda e: e.activation(out, in_, func, **kw), R, W)

    def tt(self, E, out, in0, in1, op, R, W):
        return self.op(E, lambda e: e.tensor_tensor(out, in0, in1, op), R, W)

    def ts(self, E, out, in0, s1, op0, R, W, s2=None, op1=None):
        if op1 is None:
            return self.op(E, lambda e: e.tensor_scalar(out, in0, s1, None, op0), R, W)
        return self.op(E, lambda e: e.tensor_scalar(out, in0, s1, s2, op0, op1), R, W)

    def stt(self, E, out, in0, scalar, in1, op0, op1, R, W):
        E = "dve"
        return self.op(E, lambda e: e.scalar_tensor_tensor(out, in0, scalar, in1, op0, op1), R, W)

    def copy(self, E, out, in_, R, W):
        if E == "act":
            return self.op(E, lambda e: e.copy(out, in_), R, W)
        return self.op(E, lambda e: e.tensor_copy(out, in_), R, W)

    def memset(self, E, ap, val, W):
        return self.op(E, lambda e: e.memset(ap, val), (), W)

    def recip(self, out, in_, R, W):
        return self.op("dve", lambda e: e.reciprocal(out, in_), R, W)


class ParMap:
    def __init__(self):
        self.off = {}
        self.n = 0

    def add(self, name, ncols):
        self.off[name] = self.n
        self.n += ncols
        return self.off[name]


def _param_map():
    pm = ParMap()
    pm.add("lning", 16)
    pm.add("lninb", 16)
    for l in range(2):
        p = f"L{l}_"
        for nm, n in (("mu_r", 4), ("mu_k", 4), ("mu_v", 4), ("mu_w", 1), ("mu_a", 1), ("mu_g", 1),
                      ("w0", 4), ("a0", 4), ("kk", 4), ("ka", 4), ("rk", 4), ("lnxw", 4), ("lnxb", 4),
                      ("gkb", 2), ("bnw", 1), ("ccw", 48), ("cnw", 1),
                      ("ln1g", 16), ("ln1b", 16), ("ln2g", 16), ("ln2b", 16),
                      ("fcw", 264), ("fcb", 88), ("dtb", 4), ("alog", 4)):
            pm.add(p + nm, n)
    return pm


PM = _param_map()


def _cols(vec):
    vec = np.asarray(vec, np.float32).reshape(-1)
    n = (len(vec) + 127) // 128
    out = np.zeros((n * 128,), np.float32)
    out[:len(vec)] = vec
    return out.reshape(n, 128).T


def pack_params(inp):
    par = np.zeros((128, PM.n), np.float32)

    def put(name, arr):
        o = PM.off[name]
        par[:arr.shape[0], o:o + arr.shape[1]] = arr

    put("lning", _cols(inp["ln_in_g"]))
    put("lninb", _cols(inp["ln_in_b"]))
    for l in range(2):
        p = f"L{l}_"
        mu = inp["mu_a"][l]
        put(p + "mu_r", _cols(mu[0:512]))
        put(p + "mu_k", _cols(mu[512:1024]))
        put(p + "mu_v", _cols(mu[1024:1536]))
        put(p + "mu_w", _cols(mu[1536:1568]))
        put(p + "mu_a", _cols(mu[1568:1600]))
        put(p + "mu_g", _cols(mu[1600:1696]))
        put(p + "w0", _cols(inp["a_w0"][l]))
        put(p + "a0", _cols(inp["a_a0"][l]))
        put(p + "kk", _cols(inp["a_kk"][l]))
        put(p + "ka", _cols(inp["a_ka"][l]))
        put(p + "rk", _cols(inp["a_rk"][l].reshape(-1)))
        put(p + "lnxw", _cols(inp["a_lnx_w"][l]))
        put(p + "lnxb", _cols(inp["a_lnx_b"][l]))
        put(p + "gkb", _cols(inp["b_gk_b"][l]))
        put(p + "bnw", _cols(inp["b_norm_w"][l]))
        cw = inp["c_conv_w"][l]
        put(p + "ccw", np.concatenate([_cols(cw[j]) for j in range(4)], axis=1))
        put(p + "cnw", _cols(inp["c_norm_w"][l]))
        put(p + "ln1g", _cols(inp["ln1_g"][l]))
        put(p + "ln1b", _cols(inp["ln1_b"][l]))
        put(p + "ln2g", _cols(inp["ln2_g"][l]))
        put(p + "ln2b", _cols(inp["ln2_b"][l]))
        fw = inp["ffn_conv_w"][l]
        put(p + "fcw", np.concatenate([_cols(fw[j]) for j in range(3)], axis=1))
        put(p + "fcb", _cols(inp["ffn_conv_b"][l]))
        put(p + "dtb", np.broadcast_to(inp["c_dt_bias"][l][None, :], (128, 4)))
        put(p + "alog", np.broadcast_to(inp["c_a_log"][l][None, :], (128, 4)))
    return par


SM_W2, SM_A2, SM_G2, SM_GK = 0, 512, 1024, 1536
SM_N = 1792


def pack_small(inp):
    sm = np.zeros((2, 128, SM_N), np.float32)
    for l in range(2):
        sm[l, :32, SM_W2:SM_W2 + 512] = inp["a_w2"][l]
        sm[l, :32, SM_A2:SM_A2 + 512] = inp["a_a2"][l]
        sm[l, :96, SM_G2:SM_G2 + 512] = inp["a_g2"][l]
        sm[l, :16, SM_GK:SM_GK + 256] = inp["b_gk_w2"][l]
    return sm


C_ID, C_ONES, C_BD, C_SU, C_IU, C_G3, C_RM = 0, 128, 256, 384, 448, 512, 896
C_N = 1408


def make_consts():
    c = np.zeros((128, C_N), np.float32)
    c[:, C_ID:C_ID + 128] = np.eye(128)
    c[:, C_ONES:C_ONES + 128] = 1.0
    bd = np.zeros((128, 128), np.float32)
    bd[:64, :64] = 1.0
    bd[64:, 64:] = 1.0
    c[:, C_BD:C_BD + 128] = bd
    r = np.arange(64)[:, None]
    q = np.arange(64)[None, :]
    su = (r < q).astype(np.float32)
    iu = (r <= q).astype(np.float32)
    c[:64, C_SU:C_SU + 64] = su
    c[64:, C_SU:C_SU + 64] = su
    c[:64, C_IU:C_IU + 64] = iu
    c[64:, C_IU:C_IU + 64] = iu
    g3 = np.concatenate([iu, su, iu, iu, su, iu], axis=1)
    c[:64, C_G3:C_G3 + 384] = g3
    c[64:, C_G3:C_G3 + 384] = g3
    rm = np.ones((512,), np.float32)
    rm[::64] = 0.0
    c[:, C_RM:C_RM + 512] = rm[None, :]
    return c


def win_blocks():
    blocks = {}
    blocks["A_lora"] = [(A_OFF + 1536, 160)]
    for p in range(4):
        blocks[f"A_pair{p}"] = [(A_OFF + p * 128, 128), (A_OFF + 512 + p * 128, 128),
                                (A_OFF + 1024 + p * 128, 128)]
    blocks["B_lora"] = [(B_OFF + 1024, 16)]
    for p in range(2):
        blocks[f"B_pair{p}a"] = [(B_OFF + p * 128, 128), (B_OFF + 256 + p * 128, 128)]
        blocks[f"B_pair{p}b"] = [(B_OFF + 512 + p * 256, 256), (B_OFF + 1040 + p * 256, 256)]
    blocks["C_ab"] = [(C_OFF + 1536, 8)]
    for h in range(4):
        blocks[f"C_head{h}"] = [(C_OFF + h * 128, 128), (C_OFF + 512 + h * 128, 128),
                                (C_OFF + 1024 + h * 128, 128), (C_OFF + 1544 + h * 128, 128)]
    return blocks


class Prog:
    pass


class _Stop(Exception):
    pass


def build_program(NT=8, NL=2, debug_taps=None, stop=None):
    nc = bass.Bass("TRN2", target_bir_lowering=False)
    k = KB(nc)
    P = Prog()
    taps = {}

    x_d = nc.dram_tensor("x", [NT * TT, D], F32, kind="ExternalInput").ap()
    out_d = nc.dram_tensor("out", [NT * TT, D], F32, kind="ExternalOutput").ap()
    w_in_d = nc.dram_tensor("w_in", [2, D, NIN], F32, kind="ExternalInput").ap()
    w_gate_d = nc.dram_tensor("w_gate", [2, 3, D, D], F32, kind="ExternalInput").ap()
    w_branch_d = nc.dram_tensor("w_branch", [2, 3, 512, D], F32, kind="ExternalInput").ap()
    w_out_d = nc.dram_tensor("w_out", [2, D, D], F32, kind="ExternalInput").ap()
    w_up_d = nc.dram_tensor("w_up", [2, D, 2 * DFF], F32, kind="ExternalInput").ap()
    w_down_d = nc.dram_tensor("w_down", [2, DFF, D], F32, kind="ExternalInput").ap()
    par_d = nc.dram_tensor("par", [128, PM.n], F32, kind="ExternalInput").ap()
    sm_d = nc.dram_tensor("sm", [2, 128, SM_N], F32, kind="ExternalInput").ap()
    cst_d = nc.dram_tensor("cst", [128, C_N], F32, kind="ExternalInput").ap()
    outb = Buf(out_d, "out")

    def tap(name, ap, shape, reads):
        if debug_taps is None or name not in debug_taps or name in taps:
            return
        t = nc.dram_tensor("tap_" + name, list(shape), F32, kind="ExternalOutput").ap()
        taps[name] = t
        k.dma("sp", t, ap, reads=reads, writes=[Buf(t, name)], is_output=True)

    cst = k.sb([128, C_N], F32, "cst")
    par = k.sb([128, PM.n], F32, "par")
    sm = k.sb([128, SM_N], F32, "sm")
    hT = [k.sb([128, TT], F32, f"hT{i}") for i in range(KC)]
    hTb = k.sb([128, KC, TT], BF16, "hTb")
    yT = [k.sb([128, 4, TT], BF16, f"yT{i}") for i in range(3)]
    NAR = 26
    AR_ = [k.sb([128, 516], F32, f"ar{i}") for i in range(NAR)]
    wbuf = [k.sb([128, KC, 512], BF16, f"wbuf{i}") for i in range(2)]
    wsel = [0]
    wbr_buf = [k.sb([128, 4, 512], BF16, f"wbr{i}") for i in range(2)]
    wbr_sel = [0]
    bank = [k.ps([128, 512], F32, f"bank{i}") for i in range(8)]
    TK = [None] + [k.sb([64, 512], F32, f"tk{i}") for i in range(1, 5)]
    tiny = [k.sb([128, 128], F32, f"tiny{i}") for i in range(7)]
    TB = [k.sb([64, 512], BF16, f"tb{i}") for i in range(11)]
    stA = [[k.sb([128, 64], F32, f"stA{l}_{p}") for p in range(4)] for l in range(NL)]
    stB = [[k.sb([128, 128], F32, f"stB{l}_{p}") for p in range(2)] for l in range(NL)]
    stC = [[k.sb([128, 128], F32, f"stC{l}_{h}") for h in range(4)] for l in range(NL)]
    stAb = [[k.sb([128, 64], BF16, f"stAb{l}_{p}") for p in range(4)] for l in range(NL)]
    stBb = [[k.sb([128, 128], BF16, f"stBb{l}_{p}") for p in range(2)] for l in range(NL)]
    stCb = [[k.sb([128, 128], BF16, f"stCb{l}_{h}") for h in range(4)] for l in range(NL)]
    carA = [k.sb([128, 16], F32, f"carA{l}") for l in range(NL)]
    carC = [k.sb([128, 12, 3], F32, f"carC{l}") for l in range(NL)]
    carF = [k.sb([128, 88, 2], F32, f"carF{l}") for l in range(NL)]
    lay = [k.sb([128, 8], F32, f"lay{l}") for l in range(NL)]

    ident = cst[:, C_ID:C_ID + 128]
    ones = cst[:, C_ONES:C_ONES + 128]
    bd64 = cst[:, C_BD:C_BD + 128]

    def pc(name, j=0, rows=128):
        o = PM.off[name] + j
        return par[0:rows, o:o + 1]

    k.dma("sp", cst[:, :], cst_d, writes=[cst])
    k.dma("sp", par[:, :], par_d, writes=[par])

    WB = win_blocks()
    win_b = []
    gate_b, branch_b, wout_b, wup_b, wdown_b = [], [], [], [], []
    for l in range(NL):
        d = {}
        for bname, segs in WB.items():
            n = sum(s[1] for s in segs)
            t = nc.dram_tensor(f"wb_in{l}_{bname}", [D, n], BF16, kind="Internal").ap()
            bufs = []
            o = 0
            for (sc, sn) in segs:
                b = Buf(t[:, o:o + sn], f"{bname}{o}")
                k.dma("pool", t[:, o:o + sn], w_in_d[l, :, sc:sc + sn], writes=[b])
                bufs.append(b)
                o += sn
            d[bname] = (bufs, t, n)
        win_b.append(d)
        g = []
        for i in range(3):
            for cb in range(4):
                t = nc.dram_tensor(f"wb_g{l}_{i}_{cb}", [D, 512], BF16, kind="Internal").ap()
                b = Buf(t, f"g{l}{i}{cb}")
                k.dma("pool", t, w_gate_d[l, i, :, cb * 512:(cb + 1) * 512], writes=[b])
                g.append((b, t))
        gate_b.append(g)
        br = []
        for i in range(3):
            t = nc.dram_tensor(f"wb_br{l}_{i}", [512, D], BF16, kind="Internal").ap()
            b = Buf(t, f"br{l}{i}")
            k.dma("pool", t, w_branch_d[l, i], writes=[b])
            br.append((b, t))
        branch_b.append(br)
        wo = []
        for cb in range(4):
            t = nc.dram_tensor(f"wb_o{l}_{cb}", [D, 512], BF16, kind="Internal").ap()
            b = Buf(t, f"o{l}{cb}")
            k.dma("pool", t, w_out_d[l, :, cb * 512:(cb + 1) * 512], writes=[b])
            wo.append((b, t))
        wout_b.append(wo)
        wu = []
        for j in range(22):
            t = nc.dram_tensor(f"wb_u{l}_{j}", [D, 512], BF16, kind="Internal").ap()
            b0 = Buf(t[:, 0:256], f"u{l}{j}a")
            b1 = Buf(t[:, 256:512], f"u{l}{j}b")
            k.dma("pool", t[:, 0:256], w_up_d[l, :, j * 256:(j + 1) * 256], writes=[b0])
            k.dma("pool", t[:, 256:512], w_up_d[l, :, DFF + j * 256:DFF + (j + 1) * 256], writes=[b1])
            wu.append(([b0, b1], t))
        wup_b.append(wu)
        wd = []
        for cb in range(16):
            t = nc.dram_tensor(f"wb_d{l}_{cb}", [DFF, 128], BF16, kind="Internal").ap()
            b = Buf(t, f"d{l}{cb}")
            k.dma("pool", t, w_down_d[l, :, cb * 128:(cb + 1) * 128], writes=[b])
            wd.append((b, t))
        wdown_b.append(wd)

    k.barrier_dma()

    def load_w(bufs, t_ap, nrows, ncols):
        wb = wbuf[wsel[0] % 2]
        wsel[0] += 1
        kc = nrows // 128
        k.dma("sp", wb[:, 0:kc, 0:ncols], t_ap.rearrange("(kc p) n -> p kc n", p=128),
              reads=bufs, writes=[wb])
        return wb

    bsel = [0]

    def next_bank():
        b = bank[bsel[0] % 2]
        bsel[0] += 1
        return b

    def proj(wb, c0, M, ps_ap, psb):
        for kc in range(KC):
            k.mm(ps_ap, wb[:, kc, c0:c0 + M], hTb[:, kc, :], kc == 0, kc == KC - 1,
                 R=[wb, hTb], W=[psb])

    def layer_norm(gname, bname):
        mean = AR_[0]
        rstd = AR_[1]
        sq = [AR_[2], AR_[3]]
        pm_, pv_ = bank[2], bank[3]
        for kc in range(KC):
            k.mm(pm_[:, :], ones, hT[kc][:, :], kc == 0, kc == KC - 1, R=[hT[kc], cst], W=[pm_])
        k.act(mean[:, 0:TT], pm_[:, :], AF.Copy, R=[pm_], W=[mean], scale=1.0 / D)
        for kc in range(KC):
            k.tt("dve" if kc % 2 == 0 else "pool", hT[kc][:, :], hT[kc][:, :], mean[:, 0:TT], ALU.subtract,
                 R=[hT[kc], mean], W=[hT[kc]])
        for kc in range(KC):
            s = sq[kc % 2]
            k.act(s[:, 0:TT], hT[kc][:, :], AF.Square, R=[hT[kc]], W=[s])
            k.mm(pv_[:, :], ones, s[:, 0:TT], kc == 0, kc == KC - 1, R=[s, cst], W=[pv_])
        k.ts("dve", rstd[:, 0:TT], pv_[:, :], 1.0 / D, ALU.mult, R=[pv_], W=[rstd], s2=1e-5, op1=ALU.add)
        k.act(rstd[:, 0:TT], rstd[:, 0:TT], AF.Sqrt, R=[rstd], W=[rstd])
        k.recip(rstd[:, 0:TT], rstd[:, 0:TT], R=[rstd], W=[rstd])
        for kc in range(KC):
            k.tt("dve" if kc % 2 == 0 else "pool", hT[kc][:, :], hT[kc][:, :], rstd[:, 0:TT], ALU.mult,
                 R=[hT[kc], rstd], W=[hT[kc]])
            k.ts("dve", hT[kc][:, :], hT[kc][:, :], pc(gname, kc), ALU.mult, R=[hT[kc], par], W=[hT[kc]],
                 s2=pc(bname, kc), op1=ALU.add)
            k.copy("act", hTb[:, kc, :], hT[kc][:, :], R=[hT[kc]], W=[hTb])

    def rstd_from(ps_ap, psb, out_t, eps, scale, rows=128, n=TT):
        k.ts("dve", out_t[0:rows, 0:n], ps_ap, scale, ALU.mult, R=[psb], W=[out_t], s2=eps, op1=ALU.add)
        k.act(out_t[0:rows, 0:n], out_t[0:rows, 0:n], AF.Sqrt, R=[out_t], W=[out_t])
        k.recip(out_t[0:rows, 0:n], out_t[0:rows, 0:n], R=[out_t], W=[out_t])

    for l in range(NL):
        for b in stA[l] + stB[l] + stC[l] + stAb[l] + stBb[l] + stCb[l] + [carA[l], carC[l], carF[l]]:
            k.memset("pool", b.t[:], 0.0, W=[b])
        k.act(lay[l][:, 0:4], par[:, PM.off[f"L{l}_alog"]:PM.off[f"L{l}_alog"] + 4], AF.Exp, R=[par], W=[lay[l]])
        k.ts("dve", lay[l][:, 0:4], lay[l][:, 0:4], -1.0, ALU.mult, R=[lay[l]], W=[lay[l]])

    def mixer_A(l, t):
        Lp = f"L{l}_"
        wl_t, al_t, gl_t = AR_[2], AR_[3], AR_[4]
        W = AR_[5]

        def shift_mix(psb, ps_ap, rows, mu_name, mu_j, car_idx, out_t):
            k.act(W[0:rows, 1:513], ps_ap, AF.Copy, R=[psb], W=[W])
            k.copy("dve", W[0:rows, 0:1], carA[l][0:rows, car_idx:car_idx + 1], R=[carA[l]], W=[W])
            k.copy("pool", carA[l][0:rows, car_idx:car_idx + 1], W[0:rows, 512:513], R=[W], W=[carA[l]])
            mu = pc(Lp + mu_name, mu_j, rows)
            k.tt("dve", out_t[0:rows, 0:TT], W[0:rows, 0:512], W[0:rows, 1:513], ALU.subtract, R=[W], W=[out_t])
            k.stt("dve", out_t[0:rows, 0:TT], out_t[0:rows, 0:TT], mu, W[0:rows, 1:513], ALU.mult, ALU.add,
                  R=[out_t, W, par], W=[out_t])

        bufs, tap_, n = win_b[l]["A_lora"]
        wb = load_w(bufs, tap_, D, n)
        for (c0, M, nm, ci, dst) in ((0, 32, "mu_w", 12, wl_t), (32, 32, "mu_a", 13, al_t), (64, 96, "mu_g", 14, gl_t)):
            pb = next_bank()
            proj(wb, c0, M, pb[0:M, :], pb)
            shift_mix(pb, pb[0:M, :], M, nm, 0, ci, dst)
        k.act(wl_t[0:32, 0:TT], wl_t[0:32, 0:TT], AF.Tanh, R=[wl_t], W=[wl_t])
        k.act(gl_t[0:96, 0:TT], gl_t[0:96, 0:TT], AF.Sigmoid, R=[gl_t], W=[gl_t])
        chk("A1")
        if t == 0 or NL > 1:
            k.dma("sp", sm[:, :], sm_d[l], writes=[sm])

        for p in range(4):
            r_t, kx_t, v_t = AR_[6], AR_[7], AR_[8]
            bufs, tap_, n = win_b[l][f"A_pair{p}"]
            wb = load_w(bufs, tap_, D, n)
            for (c0, nm, ci, dst) in ((0, "mu_r", p, r_t), (128, "mu_k", 4 + p, kx_t), (256, "mu_v", 8 + p, v_t)):
                pb = next_bank()
                proj(wb, c0, 128, pb[:, :], pb)
                shift_mix(pb, pb[:, :], 128, nm, p, ci, dst)
            if p == 0:
                tap(f"A_r_l{l}", r_t[:, 0:TT], [128, TT], [r_t])
            chk("A1a")
            ld_t, ai_t, g_t = AR_[9], AR_[10], AR_[11]
            cs = slice(p * 128, (p + 1) * 128)
            pb = bank[2]
            k.mm(pb[:, :], sm[0:32, SM_W2 + p * 128:SM_W2 + (p + 1) * 128], wl_t[0:32, 0:TT], True, True,
                 R=[sm, wl_t], W=[pb])
            k.act(ld_t[:, 0:TT], pb[:, :], AF.Sigmoid, R=[pb, par], W=[ld_t], bias=pc(Lp + "w0", p))
            chk("A1b1")
            k.ts("pool", ld_t[:, 0:TT], ld_t[:, 0:TT], -float(np.exp(-0.5)), ALU.mult, R=[ld_t], W=[ld_t])
            chk("A1b2")
            pb = bank[3]
            k.mm(pb[:, :], sm[0:32, SM_A2 + p * 128:SM_A2 + (p + 1) * 128], al_t[0:32, 0:TT], True, True,
                 R=[sm, al_t], W=[pb])
            k.act(ai_t[:, 0:TT], pb[:, :], AF.Sigmoid, R=[pb, par], W=[ai_t], bias=pc(Lp + "a0", p))
            chk("A1b3")
            pb = bank[2]
            k.mm(pb[:, :], sm[0:96, SM_G2 + p * 128:SM_G2 + (p + 1) * 128], gl_t[0:96, 0:TT], True, True,
                 R=[sm, gl_t], W=[pb])
            k.act(g_t[:, 0:TT], pb[:, :], AF.Copy, R=[pb], W=[g_t])
            chk("A1b")
            kkn_t, tmp_t, rs_t = AR_[12], AR_[13], AR_[14]
            k.ts("pool", kkn_t[:, 0:TT], kx_t[:, 0:TT], pc(Lp + "kk", p), ALU.mult, R=[kx_t, par], W=[kkn_t])
            k.act(tmp_t[:, 0:TT], kkn_t[:, 0:TT], AF.Square, R=[kkn_t], W=[tmp_t])
            pb = bank[3]
            k.mm(pb[:, :], bd64, tmp_t[:, 0:TT], True, True, R=[cst, tmp_t], W=[pb])
            rstd_from(pb[:, :], pb, rs_t, 1e-6, 1.0)
            k.tt("dve", kkn_t[:, 0:TT], kkn_t[:, 0:TT], rs_t[:, 0:TT], ALU.mult, R=[kkn_t, rs_t], W=[kkn_t])
            chk("A1c")
            kp_t = AR_[15]
            k.ts("dve", tmp_t[:, 0:TT], ai_t[:, 0:TT], -1.0, ALU.add, R=[ai_t, par], W=[tmp_t],
                 s2=pc(Lp + "ka", p), op1=ALU.mult)
            k.stt("dve", kp_t[:, 0:TT], tmp_t[:, 0:TT], 1.0, kx_t[:, 0:TT], ALU.add, ALU.mult,
                  R=[tmp_t, kx_t], W=[kp_t])
            bon_t = AR_[16]
            k.stt("dve", tmp_t[:, 0:TT], r_t[:, 0:TT], pc(Lp + "rk", p), kp_t[:, 0:TT], ALU.mult, ALU.mult,
                  R=[r_t, kp_t, par], W=[tmp_t])
            pb = bank[2]
            k.mm(pb[:, :], bd64, tmp_t[:, 0:TT], True, True, R=[cst, tmp_t], W=[pb])
            k.tt("dve", bon_t[:, 0:TT], pb[:, :], v_t[:, 0:TT], ALU.mult, R=[pb, v_t], W=[bon_t])
            chk("A1d")
            bc_t, ep_t, en_t, ex_t = AR_[13], AR_[14], AR_[17], AR_[18]
            k.op("dve", lambda e, bc_t=bc_t, ld_t=ld_t: e.tensor_tensor_scan(
                bc_t[:, 0:TT], cst[:, C_RM:C_RM + TT], ld_t[:, 0:TT], 0.0, ALU.mult, ALU.add),
                [cst, ld_t], [bc_t])
            k.tt("pool", ex_t[:, 0:TT], bc_t[:, 0:TT], ld_t[:, 0:TT], ALU.subtract, R=[bc_t, ld_t], W=[ex_t])
            k.act(ex_t[:, 0:TT], ex_t[:, 0:TT], AF.Exp, R=[ex_t], W=[ex_t])
            k.act(ep_t[:, 0:TT], bc_t[:, 0:TT], AF.Exp, R=[bc_t], W=[ep_t])
            k.act(en_t[:, 0:TT], bc_t[:, 0:TT], AF.Exp, R=[bc_t], W=[en_t], scale=-1.0)
            chk("A1e")
            def bfv(tl):
                return tl[:, 0:256].bitcast(BF16)

            def v3(ap_):
                return ap_.rearrange("p (n c) -> p n c", c=CH)

            bA, bR, bB, bK = AR_[19], AR_[22], AR_[20], AR_[23]
            ARa, ARr, BKb, BKk = bfv(bA), bfv(bR), bfv(bB), bfv(bK)
            HBt, HBt2 = AR_[21], AR_[24]
            bt32, kt32 = AR_[9], AR_[25]
            k.stt("dve", ARa, kkn_t[:, 0:TT], -1.0, ex_t[:, 0:TT], ALU.mult, ALU.mult, R=[kkn_t, ex_t], W=[bA])
            k.tt("pool", ARr, r_t[:, 0:TT], ep_t[:, 0:TT], ALU.mult, R=[r_t, ep_t], W=[bR])
            k.tt("dve", bt32[:, 0:TT], kkn_t[:, 0:TT], ai_t[:, 0:TT], ALU.mult, R=[kkn_t, ai_t], W=[bt32])
            k.tt("dve", bt32[:, 0:TT], bt32[:, 0:TT], en_t[:, 0:TT], ALU.mult, R=[bt32, en_t], W=[bt32])
            k.copy("act", BKb, bt32[:, 0:TT], R=[bt32], W=[bB])
            k.tt("pool", kt32[:, 0:TT], kp_t[:, 0:TT], en_t[:, 0:TT], ALU.mult, R=[kp_t, en_t], W=[kt32])
            k.copy("act", BKk, kt32[:, 0:TT], R=[kt32], W=[bK])
            el_t = tiny[0]
            k.copy("dve", el_t[:, 0:NCH], ep_t[:, CH - 1:TT:CH], R=[ep_t], W=[el_t])
            elb = el_t[:, 0:NCH].unsqueeze(2).to_broadcast([128, NCH, CH])
            k.tt("dve", v3(HBt[:, 0:TT]), v3(bt32[:, 0:TT]), elb, ALU.mult, R=[bt32, el_t], W=[HBt])
            k.tt("pool", v3(HBt2[:, 0:TT]), v3(kt32[:, 0:TT]), elb, ALU.mult, R=[kt32, el_t], W=[HBt2])
            po = bank[7]
            st = stA[l][p]
            stb = stAb[l][p]
            chk("A2")
            Q32 = TK[2]
            Qb, Pb, Q2b, P2b, Accb = TB[0], TB[1], TB[2], TB[3], TB[4]
            su8 = cst[0:64, C_SU:C_SU + 64].unsqueeze(1).to_broadcast([64, 8, 64])
            id8 = ident[0:64, 0:64].unsqueeze(1).to_broadcast([64, 8, 64])

            def pre(n):
                cs_ = slice(n * CH, (n + 1) * CH)
                pt = bank[6]
                k.tr(pt[0:64, 0:128], v_t[:, cs_], ident, R=[v_t, cst], W=[pt])
                k.tr(pt[0:64, 128:256], HBt[:, cs_], ident, R=[HBt, cst], W=[pt])
                k.tr(pt[0:64, 256:384], HBt2[:, cs_], ident, R=[HBt2, cst], W=[pt])
                tk = TB[5 + n % 2]
                k.act(tk[0:64, 0:384], pt[0:64, 0:384], AF.Copy, R=[pt], W=[tk])
                pg = bank[4]
                for hh in range(2):
                    bs = slice(64 * hh, 64 * hh + 64)
                    o = hh * 192
                    k.mm(pg[0:64, o:o + 64], BKb[bs, cs_], ARr[bs, cs_], True, True, R=[bB, bR], W=[pg])
                    k.mm(pg[0:64, o + 64:o + 128], BKk[bs, cs_], ARa[bs, cs_], True, True, R=[bK, bA], W=[pg])
                    k.mm(pg[0:64, o + 128:o + 192], BKk[bs, cs_], ARr[bs, cs_], True, True, R=[bK, bR], W=[pg])
                gm = TB[7 + n % 2]
                k.tt("dve", gm[0:64, 0:384], pg[0:64, 0:384], cst[0:64, C_G3:C_G3 + 384], ALU.mult, R=[pg, cst], W=[gm])

            def dep(n, m0):
                cs_ = slice(n * CH, (n + 1) * CH)
                tk = TB[5 + n % 2]
                gm = TB[7 + n % 2]
                px = bank[5]
                for hh in range(2):
                    bs = slice(64 * hh, 64 * hh + 64)
                    hs = slice(hh * 64, hh * 64 + 64)
                    o = hh * 192
                    k.mm(px[0:64, hs], ARa[bs, cs_], stb[bs, :], True, False, R=[bA, stb], W=[px])
                    k.mm(px[0:64, hs], gm[0:64, o + 64:o + 128], tk[0:64, hs], False, True, R=[gm, tk], W=[px])
                X = TB[9]
                k.act(X[0:64, 0:128], px[0:64, 0:128], AF.Copy, R=[px], W=[X])
                pu = bank[3]
                for hh in range(2):
                    hs = slice(hh * 64, hh * 64 + 64)
                    ms = slice((m0 + hh) * 64, (m0 + hh) * 64 + 64)
                    k.mm(pu[0:64, hs], Accb[0:64, ms], X[0:64, hs], True, True, R=[Accb, X], W=[pu])
                U = TB[10]
                k.copy("dve", U[0:64, 0:128], pu[0:64, 0:128], R=[pu], W=[U])
                for hh in range(2):
                    bs = slice(64 * hh, 64 * hh + 64)
                    hs = slice(hh * 64, hh * 64 + 64)
                    o = hh * 192
                    k.mm(po[bs, cs_], stb[bs, :], ARr[bs, cs_], True, False, R=[stb, bR], W=[po])
                    k.mm(po[bs, cs_], U[0:64, hs], gm[0:64, o:o + 64], False, False, R=[U, gm], W=[po])
                    k.mm(po[bs, cs_], tk[0:64, hs], gm[0:64, o + 128:o + 192], False, True, R=[tk, gm], W=[po])
                pst = bank[2]
                for hh in range(2):
                    bs = slice(64 * hh, 64 * hh + 64)
                    hs = slice(hh * 64, hh * 64 + 64)
                    k.mm(pst[bs, 0:64], tk[0:64, 128 + hh * 64:128 + hh * 64 + 64], U[0:64, hs], True, False,
                         R=[tk, U], W=[pst])
                    k.mm(pst[bs, 0:64], tk[0:64, 256 + hh * 64:256 + hh * 64 + 64], tk[0:64, hs], False, True,
                         R=[tk], W=[pst])
                P.dbg.append(("A", l, t, p, n, k.cnt["dve"]))
                k.stt("dve", st[:, :], st[:, :], el_t[:, n:n + 1], pst[:, 0:64], ALU.mult, ALU.add,
                      R=[st, el_t, pst], W=[st])
                k.act(stb[:, :], st[:, :], AF.Copy, R=[st], W=[stb])

            for hf in range(2):
                pq = bank[4]
                for hh in range(2):
                    bs = slice(64 * hh, 64 * hh + 64)
                    for c4 in range(4):
                        n = hf * 4 + c4
                        cs_ = slice(n * CH, (n + 1) * CH)
                        o = (c4 * 2 + hh) * 64
                        k.mm(pq[0:64, o:o + 64], BKb[bs, cs_], ARa[bs, cs_], True, True, R=[bB, bA], W=[pq])
                k.tt("dve", v3(Q32[0:64, :]), v3(pq[0:64, :]), su8, ALU.mult, R=[pq, cst], W=[Q32])
                k.copy("act", Qb[0:64, :], Q32[0:64, :], R=[Q32], W=[Qb])
                pp = bank[5]
                for m in range(8):
                    ms = slice(m * 64, m * 64 + 64)
                    k.tr(pp[0:64, ms], Q32[0:64, ms], ident[0:64, 0:64], R=[Q32, cst], W=[pp])
                k.act(Pb[0:64, :], pp[0:64, :], AF.Copy, R=[pp], W=[Pb])
                k.tt("pool", v3(Accb[0:64, :]), v3(Q32[0:64, :]), id8, ALU.add, R=[Q32, cst], W=[Accb])
                cq, cp, nq, np_ = Qb, Pb, Q2b, P2b
                for step in range(5):
                    ps1 = bank[6]
                    for m in range(8):
                        ms = slice(m * 64, m * 64 + 64)
                        k.mm(ps1[0:64, ms], cq[0:64, ms], cp[0:64, ms], True, True, R=[cq, cp], W=[ps1])
                    k.act(np_[0:64, :], ps1[0:64, :], AF.Copy, R=[ps1], W=[np_])
                    if step < 4:
                        ps2 = bank[4]
                        for m in range(8):
                            ms = slice(m * 64, m * 64 + 64)
                            k.mm(ps2[0:64, ms], cp[0:64, ms], cq[0:64, ms], True, True, R=[cq, cp], W=[ps2])
                        k.copy("dve", nq[0:64, :], ps2[0:64, :], R=[ps2], W=[nq])
                    pa_ = bank[5]
                    for m in range(8):
                        ms = slice(m * 64, m * 64 + 64)
                        k.mm(pa_[0:64, ms], np_[0:64, ms], Accb[0:64, ms], True, True, R=[np_, Accb], W=[pa_])
                    k.tt("dve", Accb[0:64, :], Accb[0:64, :], pa_[0:64, :], ALU.add, R=[Accb, pa_], W=[Accb])
                    cq, cp, nq, np_ = nq, np_, cq, cp
                chk("A5")
                pre(hf * 4)
                for c4 in range(4):
                    n = hf * 4 + c4
                    if c4 + 1 < 4:
                        pre(n + 1)
                    dep(n, c4 * 2)
            o_t, cen_t = AR_[9], AR_[10]
            k.act(o_t[:, 0:TT], po[:, :], AF.Copy, R=[po], W=[o_t])
            if p == 0:
                tap(f"A_o_l{l}", o_t[:, 0:TT], [128, TT], [o_t])
            pb = bank[2]
            k.mm(pb[:, :], bd64, o_t[:, 0:TT], True, True, R=[cst, o_t], W=[pb])
            k.stt("dve", cen_t[:, 0:TT], pb[:, :], -1.0 / 64, o_t[:, 0:TT], ALU.mult, ALU.add, R=[pb, o_t], W=[cen_t])
            k.act(tmp_t[:, 0:TT], cen_t[:, 0:TT], AF.Square, R=[cen_t], W=[tmp_t])
            pb = bank[3]
            k.mm(pb[:, :], bd64, tmp_t[:, 0:TT], True, True, R=[cst, tmp_t], W=[pb])
            rstd_from(pb[:, :], pb, rs_t, 64e-5, 1.0 / 64)
            k.tt("dve", cen_t[:, 0:TT], cen_t[:, 0:TT], rs_t[:, 0:TT], ALU.mult, R=[cen_t, rs_t], W=[cen_t])
            k.ts("dve", cen_t[:, 0:TT], cen_t[:, 0:TT], pc(Lp + "lnxw", p), ALU.mult, R=[cen_t, par], W=[cen_t],
                 s2=pc(Lp + "lnxb", p), op1=ALU.add)
            k.tt("pool", cen_t[:, 0:TT], cen_t[:, 0:TT], bon_t[:, 0:TT], ALU.add, R=[cen_t, bon_t], W=[cen_t])
            k.tt("dve", yT[0][:, p, :], cen_t[:, 0:TT], g_t[:, 0:TT], ALU.mult, R=[cen_t, g_t], W=[yT[0]])

    def mixer_B(l, t):
        Lp = f"L{l}_"
        gkl_t = AR_[2]
        bufs, tap_, n = win_b[l]["B_lora"]
        wb = load_w(bufs, tap_, D, n)
        pb = next_bank()
        proj(wb, 0, 16, pb[0:16, :], pb)
        k.act(gkl_t[0:16, 0:TT], pb[0:16, :], AF.Copy, R=[pb], W=[gkl_t])
        for p in range(2):
            q_t, k_t = AR_[3], AR_[4]
            v_t = [AR_[5], AR_[6]]
            g_t = [AR_[7], AR_[8]]
            bufs, tap_, n = win_b[l][f"B_pair{p}a"]
            wb = load_w(bufs, tap_, D, n)
            for (c0, dst) in ((0, q_t), (128, k_t)):
                pb = next_bank()
                proj(wb, c0, 128, pb[:, :], pb)
                k.act(dst[:, 0:TT], pb[:, :], AF.Copy, R=[pb], W=[dst])
            bufs, tap_, n = win_b[l][f"B_pair{p}b"]
            wb = load_w(bufs, tap_, D, n)
            for (c0, dst, fn) in ((0, v_t[0], AF.Copy), (128, v_t[1], AF.Copy), (256, g_t[0], AF.Silu), (384, g_t[1], AF.Silu)):
                pb = next_bank()
                proj(wb, c0, 128, pb[:, :], pb)
                k.act(dst[:, 0:TT], pb[:, :], fn, R=[pb], W=[dst])
            lg_t, bc_t, ep_t, en_t = AR_[9], AR_[10], AR_[11], AR_[12]
            pb = bank[2]
            k.mm(pb[:, :], sm[0:16, SM_GK + p * 128:SM_GK + (p + 1) * 128], gkl_t[0:16, 0:TT], True, True,
                 R=[sm, gkl_t], W=[pb])
            k.act(lg_t[:, 0:TT], pb[:, :], AF.Sigmoid, R=[pb, par], W=[lg_t], bias=pc(Lp + "gkb", p))
            k.act(lg_t[:, 0:TT], lg_t[:, 0:TT], AF.Ln, R=[lg_t], W=[lg_t])
            k.op("dve", lambda e, bc_t=bc_t, lg_t=lg_t: e.tensor_tensor_scan(
                bc_t[:, 0:TT], cst[:, C_RM:C_RM + TT], lg_t[:, 0:TT], 0.0, ALU.mult, ALU.add),
                [cst, lg_t], [bc_t])
            k.act(ep_t[:, 0:TT], bc_t[:, 0:TT], AF.Exp, R=[bc_t], W=[ep_t], scale=1.0 / 16)
            k.act(en_t[:, 0:TT], bc_t[:, 0:TT], AF.Exp, R=[bc_t], W=[en_t], scale=-1.0 / 16)
            def bfv(tl):
                return tl[:, 0:256].bitcast(BF16)

            bQd, bKd = AR_[13], AR_[16]
            qd_b, kd_b = bfv(bQd), bfv(bKd)
            kd_t, kdec_t = AR_[14], AR_[15]
            k.stt("dve", qd_b, q_t[:, 0:TT], 0.125, ep_t[:, 0:TT], ALU.mult, ALU.mult,
                  R=[q_t, ep_t], W=[bQd])
            k.tt("pool", kd_t[:, 0:TT], k_t[:, 0:TT], en_t[:, 0:TT], ALU.mult, R=[k_t, en_t], W=[kd_t])
            k.copy("act", kd_b, kd_t[:, 0:TT], R=[kd_t], W=[bKd])
            el_t = tiny[0]
            k.copy("dve", el_t[:, 0:NCH], ep_t[:, CH - 1:TT:CH], R=[ep_t], W=[el_t])
            elb = el_t[:, 0:NCH].unsqueeze(2).to_broadcast([128, NCH, CH])
            k.tt("dve", kdec_t[:, 0:TT].rearrange("p (n c) -> p n c", c=CH),
                 kd_t[:, 0:TT].rearrange("p (n c) -> p n c", c=CH), elb, ALU.mult, R=[kd_t, el_t], W=[kdec_t])
            st = stB[l][p]
            stb = stBb[l][p]
            po = [bank[6], bank[7]]
            iub = cst[0:64, C_IU:C_IU + 64].unsqueeze(1).to_broadcast([64, 2, 64])

            def pre(n):
                cs_ = slice(n * CH, (n + 1) * CH)
                pt = bank[4]
                k.tr(pt[0:64, 0:128], kdec_t[:, cs_], ident, R=[kdec_t, cst], W=[pt])
                k.tr(pt[0:64, 128:256], v_t[0][:, cs_], ident, R=[v_t[0], cst], W=[pt])
                k.tr(pt[0:64, 256:384], v_t[1][:, cs_], ident, R=[v_t[1], cst], W=[pt])
                tk = TB[5 + n % 2]
                k.act(tk[0:64, 0:384], pt[0:64, 0:384], AF.Copy, R=[pt], W=[tk])
                pa_ = bank[5]
                for hh in range(2):
                    bs = slice(64 * hh, 64 * hh + 64)
                    k.mm(pa_[0:64, hh * 64:hh * 64 + 64], kd_b[bs, cs_], qd_b[bs, cs_], True, True,
                         R=[bKd, bQd], W=[pa_])
                att = TB[7 + n % 2]
                k.tt("dve", att[0:64, 0:128].rearrange("p (h c) -> p h c", h=2),
                     pa_[0:64, 0:128].rearrange("p (h c) -> p h c", h=2), iub, ALU.mult, R=[pa_, cst], W=[att])
                pu = bank[3 - n % 2]
                for hh in range(2):
                    bs = slice(64 * hh, 64 * hh + 64)
                    vs = slice(128 + hh * 128, 256 + hh * 128)
                    k.mm(pu[bs, 0:128], tk[0:64, hh * 64:hh * 64 + 64], tk[0:64, vs], True, True, R=[tk], W=[pu])

            def dep(n):
                cs_ = slice(n * CH, (n + 1) * CH)
                tk = TB[5 + n % 2]
                att = TB[7 + n % 2]
                pu = bank[3 - n % 2]
                for hh in range(2):
                    bs = slice(64 * hh, 64 * hh + 64)
                    vs = slice(128 + hh * 128, 256 + hh * 128)
                    k.mm(po[hh][:, cs_], tk[0:64, vs], att[0:64, hh * 64:hh * 64 + 64], True, False,
                         R=[tk, att], W=[po[hh]])
                    k.mm(po[hh][:, cs_], stb[bs, :], qd_b[bs, cs_], False, True, R=[stb, bQd], W=[po[hh]])
                k.stt("dve", st[:, :], st[:, :], el_t[:, n:n + 1], pu[:, 0:128], ALU.mult, ALU.add,
                      R=[st, el_t, pu], W=[st])
                k.act(stb[:, :], st[:, :], AF.Copy, R=[st], W=[stb])

            pre(0)
            for n in range(NCH):
                if n + 1 < NCH:
                    pre(n + 1)
                dep(n)
            for hh in range(2):
                h = 2 * p + hh
                sq_t, rs_t = AR_[9], AR_[10]
                k.act(sq_t[:, 0:TT], po[hh][:, :], AF.Square, R=[po[hh]], W=[sq_t])
                pb = bank[2 + hh]
                k.mm(pb[:, :], ones, sq_t[:, 0:TT], True, True, R=[cst, sq_t], W=[pb])
                rstd_from(pb[:, :], pb, rs_t, 1e-5, 1.0 / 128)
                k.tt("dve", sq_t[:, 0:TT], po[hh][:, :], rs_t[:, 0:TT], ALU.mult, R=[po[hh], rs_t], W=[sq_t])
                if h == 0:
                    tap(f"B_on_l{l}", sq_t[:, 0:TT], [128, TT], [sq_t])
                k.stt("dve", yT[1][:, h, :], sq_t[:, 0:TT], pc(Lp + "bnw", 0), g_t[hh][:, 0:TT], ALU.mult, ALU.mult,
                      R=[sq_t, g_t[hh], par], W=[yT[1]])

    def mixer_C(l, t):
        Lp = f"L{l}_"
        bufs, tap_, n_ = win_b[l]["C_ab"]
        wb = load_w(bufs, tap_, D, n_)
        pab = bank[2]
        for n in range(NCH):
            for kc in range(KC):
                k.mm(pab[0:64, n * 8:n * 8 + 8], hTb[:, kc, n * CH:(n + 1) * CH], wb[:, kc, 0:8],
                     kc == 0, kc == KC - 1, R=[wb, hTb], W=[pab])
        g_tok, be_tok, gam, eg, egl, elast = tiny[0], tiny[1], tiny[2], tiny[3], tiny[4], tiny[5]
        ab3 = pab[0:64, 0:64].rearrange("p (n c) -> p n c", c=8)
        dtb = par[0:64, PM.off[Lp + "dtb"]:PM.off[Lp + "dtb"] + 4].unsqueeze(1).to_broadcast([64, NCH, 4])
        nea = lay[l][0:64, 0:4].unsqueeze(1).to_broadcast([64, NCH, 4])
        g3 = g_tok[0:64, 0:32].rearrange("p (n c) -> p n c", c=4)
        k.tt("dve", g3, ab3[:, :, 0:4], dtb, ALU.add, R=[pab, par], W=[g_tok])
        k.act(g_tok[0:64, 0:32], g_tok[0:64, 0:32], AF.Exp, R=[g_tok], W=[g_tok])
        k.act(g_tok[0:64, 0:32], g_tok[0:64, 0:32], AF.Ln, R=[g_tok], W=[g_tok], bias=1.0)
        k.tt("dve", g3, g3, nea, ALU.mult, R=[g_tok, lay[l]], W=[g_tok])
        k.act(be_tok[0:64, 0:32].rearrange("p (n c) -> p n c", c=4), ab3[:, :, 4:8], AF.Sigmoid, R=[pab], W=[be_tok])
        pg = bank[3]
        k.mm(pg[0:64, 0:32], cst[0:64, C_IU:C_IU + 64], g_tok[0:64, 0:32], True, True, R=[cst, g_tok], W=[pg])
        k.mm(pg[:, 32:64], ones[0:64, :], g_tok[0:64, 0:32], True, True, R=[cst, g_tok], W=[pg])
        k.act(gam[0:64, 0:32], pg[0:64, 0:32], AF.Copy, R=[pg], W=[gam])
        k.act(eg[0:64, 0:32], pg[0:64, 0:32], AF.Exp, R=[pg], W=[eg], scale=1.0)
        k.act(elast[:, 0:32], pg[:, 32:64], AF.Exp, R=[pg], W=[elast])
        k.tt("dve", egl[0:64, 0:32], pg[0:64, 32:64], gam[0:64, 0:32], ALU.subtract, R=[pg, gam], W=[egl])
        k.act(egl[0:64, 0:32], egl[0:64, 0:32], AF.Exp, R=[egl], W=[egl])
        nbeg = tiny[6]
        k.ts("pool", nbeg[0:64, 0:32], eg[0:64, 0:32], -1.0, ALU.mult, R=[eg], W=[nbeg])
        for h in range(4):
            bufs, tap_, n_ = win_b[l][f"C_head{h}"]
            wb = load_w(bufs, tap_, D, n_)
            Wk = AR_[2]
            q_t, k_t, v_t, z_t = AR_[3], AR_[4], AR_[5], AR_[6]
            for (c0, ci, dst) in ((0, h, q_t), (128, 4 + h, k_t), (256, 8 + h, v_t)):
                pb = next_bank()
                proj(wb, c0, 128, pb[:, :], pb)
                k.act(Wk[:, 3:515], pb[:, :], AF.Copy, R=[pb], W=[Wk])
                k.copy("dve", Wk[:, 0:3], carC[l][:, ci, :], R=[carC[l]], W=[Wk])
                k.copy("pool", carC[l][:, ci, :], Wk[:, 512:515], R=[Wk], W=[carC[l]])
                o = PM.off[Lp + "ccw"]
                k.ts("dve", dst[:, 0:TT], Wk[:, 0:512], par[:, o + ci:o + ci + 1], ALU.mult, R=[Wk, par], W=[dst])
                for j in range(1, 4):
                    k.stt("dve", dst[:, 0:TT], Wk[:, j:j + 512], par[:, o + 12 * j + ci:o + 12 * j + ci + 1],
                          dst[:, 0:TT], ALU.mult, ALU.add, R=[Wk, dst, par], W=[dst])
                k.act(dst[:, 0:TT], dst[:, 0:TT], AF.Silu, R=[dst], W=[dst])
            pb = next_bank()
            proj(wb, 384, 128, pb[:, :], pb)
            k.act(z_t[:, 0:TT], pb[:, :], AF.Silu, R=[pb], W=[z_t])
            sq_t, rs_t = AR_[7], AR_[8]
            for (src, scl) in ((q_t, 128.0 ** -0.5), (k_t, 1.0)):
                k.act(sq_t[:, 0:TT], src[:, 0:TT], AF.Square, R=[src], W=[sq_t])
                pb = bank[2]
                k.mm(pb[:, :], ones, sq_t[:, 0:TT], True, True, R=[cst, sq_t], W=[pb])
                rstd_from(pb[:, :], pb, rs_t, 1e-6, 1.0)
                k.stt("dve", src[:, 0:TT], src[:, 0:TT], scl, rs_t[:, 0:TT], ALU.mult, ALU.mult,
                      R=[src, rs_t], W=[src])
            if h == 0:
                tap(f"C_k_l{l}", k_t[:, 0:TT], [128, TT], [k_t])
                tap(f"C_v_l{l}", v_t[:, 0:TT], [128, TT], [v_t])
            st = stC[l][h]
            stb = stCb[l][h]
            po = bank[7]

            def bfv(tl):
                return tl[:, 0:256].bitcast(BF16)

            def v3(ap_):
                return ap_.rearrange("p (n c) -> p n c", c=CH)

            bKb, bQb, bQg = AR_[13], AR_[14], AR_[15]
            k_tb, q_tb, qg_b = bfv(bKb), bfv(bQb), bfv(bQg)
            k.copy("act", k_tb, k_t[:, 0:TT], R=[k_t], W=[bKb])
            k.copy("pool", q_tb, q_t[:, 0:TT], R=[q_t], W=[bQb])
            gcol = gam[0:64, h:32:4].unsqueeze(2).to_broadcast([64, NCH, CH])
            bcol = be_tok[0:64, h:32:4].unsqueeze(2).to_broadcast([64, NCH, CH])
            su8 = cst[0:64, C_SU:C_SU + 64].unsqueeze(1).to_broadcast([64, 8, 64])
            iu8 = cst[0:64, C_IU:C_IU + 64].unsqueeze(1).to_broadcast([64, 8, 64])
            id8 = ident[0:64, 0:64].unsqueeze(1).to_broadcast([64, 8, 64])
            dg, DT, Q32, qk32 = TK[1], TK[2], TK[3], TK[4]
            k.tt("pool", v3(dg[0:64, :]), id8, gcol, ALU.mult, R=[cst, gam], W=[dg])
            pgr = bank[5]
            k.mm(pgr[:, :], ones[0:64, :], dg[0:64, :], True, True, R=[cst, dg], W=[pgr])
            k.tt("dve", v3(DT[0:64, :]), v3(pgr[0:64, :]), gcol, ALU.subtract, R=[pgr, gam], W=[DT])
            k.ts("pool", DT[0:64, :], DT[0:64, :], 0.0, ALU.min, R=[DT], W=[DT])
            k.act(DT[0:64, :], DT[0:64, :], AF.Exp, R=[DT], W=[DT])
            eg_r = AR_[9]
            k.act(eg_r[:, 0:TT], pgr[:, :], AF.Exp, R=[pgr], W=[eg_r])
            k.tt("dve", qg_b, eg_r[:, 0:TT], q_t[:, 0:TT], ALU.mult, R=[eg_r, q_t], W=[bQg])
            pkk, pqk = bank[6], bank[4]
            for n in range(NCH):
                cs_ = slice(n * CH, (n + 1) * CH)
                k.mm(pkk[0:64, cs_], k_tb[:, cs_], k_tb[:, cs_], True, True, R=[bKb], W=[pkk])
            for n in range(NCH):
                cs_ = slice(n * CH, (n + 1) * CH)
                k.mm(pqk[0:64, cs_], k_tb[:, cs_], q_tb[:, cs_], True, True, R=[bKb, bQb], W=[pqk])
            k.tt("dve", Q32[0:64, :], pkk[0:64, :], DT[0:64, :], ALU.mult, R=[pkk, DT], W=[Q32])
            k.tt("dve", v3(Q32[0:64, :]), v3(Q32[0:64, :]), bcol, ALU.mult, R=[Q32, be_tok], W=[Q32])
            k.stt("dve", v3(Q32[0:64, :]), v3(Q32[0:64, :]), -1.0, su8, ALU.mult, ALU.mult, R=[Q32, cst], W=[Q32])
            Qb, Pb, Q2b, P2b, Accb, QKD = TB[0], TB[1], TB[2], TB[3], TB[4], TB[7]
            k.tt("dve", qk32[0:64, :], pqk[0:64, :], DT[0:64, :], ALU.mult, R=[pqk, DT], W=[qk32])
            k.tt("pool", v3(QKD[0:64, :]), v3(qk32[0:64, :]), iu8, ALU.mult, R=[qk32, cst], W=[QKD])
            k.copy("act", Qb[0:64, :], Q32[0:64, :], R=[Q32], W=[Qb])
            pp = bank[5]
            for m in range(8):
                ms = slice(m * 64, m * 64 + 64)
                k.tr(pp[0:64, ms], Q32[0:64, ms], ident[0:64, 0:64], R=[Q32, cst], W=[pp])
            k.act(Pb[0:64, :], pp[0:64, :], AF.Copy, R=[pp], W=[Pb])
            k.tt("pool", v3(Accb[0:64, :]), v3(Q32[0:64, :]), id8, ALU.add, R=[Q32, cst], W=[Accb])
            cq, cp, nq, np_ = Qb, Pb, Q2b, P2b
            for step in range(5):
                ps1 = bank[6]
                for m in range(8):
                    ms = slice(m * 64, m * 64 + 64)
                    k.mm(ps1[0:64, ms], cq[0:64, ms], cp[0:64, ms], True, True, R=[cq, cp], W=[ps1])
                k.act(np_[0:64, :], ps1[0:64, :], AF.Copy, R=[ps1], W=[np_])
                if step < 4:
                    ps2 = bank[4]
                    for m in range(8):
                        ms = slice(m * 64, m * 64 + 64)
                        k.mm(ps2[0:64, ms], cp[0:64, ms], cq[0:64, ms], True, True, R=[cq, cp], W=[ps2])
                    k.copy("dve", nq[0:64, :], ps2[0:64, :], R=[ps2], W=[nq])
                pa_ = bank[5]
                for m in range(8):
                    ms = slice(m * 64, m * 64 + 64)
                    k.mm(pa_[0:64, ms], np_[0:64, ms], Accb[0:64, ms], True, True, R=[np_, Accb], W=[pa_])
                k.tt("dve", Accb[0:64, :], Accb[0:64, :], pa_[0:64, :], ALU.add, R=[Accb, pa_], W=[Accb])
                cq, cp, nq, np_ = nq, np_, cq, cp

            def pre(n):
                cs_ = slice(n * CH, (n + 1) * CH)
                ci = n * 4 + h
                pt = bank[6]
                k.tr(pt[0:64, 0:128], k_t[:, cs_], ident, R=[k_t, cst], W=[pt])
                k.tr(pt[0:64, 128:256], v_t[:, cs_], ident, R=[v_t, cst], W=[pt])
                tk = TB[5 + n % 2]
                k.ts("dve", tk[0:64, 0:128], pt[0:64, 0:128], egl[0:64, ci:ci + 1], ALU.mult, R=[pt, egl], W=[tk])
                k.act(tk[0:64, 128:256], pt[0:64, 128:256], AF.Copy, R=[pt], W=[tk])

            def dep(n):
                cs_ = slice(n * CH, (n + 1) * CH)
                ci = n * 4 + h
                ms = slice(n * 64, n * 64 + 64)
                tk = TB[5 + n % 2]
                pks = bank[3]
                k.mm(pks[0:64, 0:128], k_tb[:, cs_], stb[:, :], True, True, R=[bKb, stb], W=[pks])
                Rp = TB[8]
                k.stt("dve", Rp[0:64, 0:128], pks[0:64, 0:128], nbeg[0:64, ci:ci + 1], tk[0:64, 128:256],
                      ALU.mult, ALU.add, R=[pks, nbeg, tk], W=[Rp])
                pv = bank[2]
                k.mm(pv[0:64, 0:128], Accb[0:64, ms], Rp[0:64, 0:128], True, True, R=[Accb, Rp], W=[pv])
                vn = TB[9]
                k.ts("dve", vn[0:64, 0:128], pv[0:64, 0:128], be_tok[0:64, ci:ci + 1], ALU.mult,
                     R=[pv, be_tok], W=[vn])
                k.mm(po[:, cs_], stb[:, :], qg_b[:, cs_], True, False, R=[stb, bQg], W=[po])
                k.mm(po[:, cs_], vn[0:64, 0:128], QKD[0:64, ms], False, True, R=[vn, QKD], W=[po])
                pst = bank[4]
                k.mm(pst[:, 0:128], tk[0:64, 0:128], vn[0:64, 0:128], True, True, R=[tk, vn], W=[pst])
                P.dbg.append(("C", l, t, h, n, k.cnt["dve"]))
                k.stt("dve", st[:, :], st[:, :], elast[:, ci:ci + 1], pst[:, 0:128], ALU.mult, ALU.add,
                      R=[st, elast, pst], W=[st])
                k.act(stb[:, :], st[:, :], AF.Copy, R=[st], W=[stb])

            pre(0)
            for n in range(NCH):
                if n + 1 < NCH:
                    pre(n + 1)
                dep(n)
            k.act(sq_t[:, 0:TT], po[:, :], AF.Square, R=[po], W=[sq_t])
            pb = bank[2]
            k.mm(pb[:, :], ones, sq_t[:, 0:TT], True, True, R=[cst, sq_t], W=[pb])
            rstd_from(pb[:, :], pb, rs_t, 1e-5, 1.0 / 128)
            k.tt("dve", sq_t[:, 0:TT], po[:, :], rs_t[:, 0:TT], ALU.mult, R=[po, rs_t], W=[sq_t])
            if h == 0:
                tap(f"C_on_l{l}", sq_t[:, 0:TT], [128, TT], [sq_t])
            k.stt("dve", yT[2][:, h, :], sq_t[:, 0:TT], pc(Lp + "cnw", 0), z_t[:, 0:TT], ALU.mult, ALU.mult,
                  R=[sq_t, z_t, par], W=[yT[2]])

    def merge_out(l, t):
        Lp = f"L{l}_"
        def mg(c):
            return AR_[2 + c // 2], AR_[2 + c // 2][:, 0:512].bitcast(BF16)[:, (c % 2) * 512:(c % 2) * 512 + 512]

        for cb in range(4):
            for i in range(3):
                b, tap_ = gate_b[l][i * 4 + cb]
                wg = load_w([b], tap_, D, 512)
                bb_, btap = branch_b[l][i]
                wbr = wbr_buf[wbr_sel[0] % 2]
                wbr_sel[0] += 1
                k.dma("sp", wbr[:, :, :], btap[:, cb * 512:(cb + 1) * 512].rearrange("(kc p) n -> p kc n", p=128),
                      reads=[bb_], writes=[wbr])
                for cc in range(4):
                    c = cb * 4 + cc
                    pg_ = next_bank()
                    proj(wg, cc * 128, 128, pg_[:, :], pg_)
                    pbr = bank[2 + (cc % 2)]
                    for kc in range(4):
                        k.mm(pbr[:, :], wbr[:, kc, cc * 128:(cc + 1) * 128], yT[i][:, kc, :], kc == 0, kc == 3,
                             R=[wbr, yT[i]], W=[pbr])
                    sg = AR_[14 + (cc % 2)]
                    k.act(sg[:, 0:TT], pg_[:, :], AF.Sigmoid, R=[pg_], W=[sg])
                    acc = AR_[16 + cc]
                    if i == 0:
                        k.tt("dve", acc[:, 0:TT], sg[:, 0:TT], pbr[:, :], ALU.mult, R=[sg, pbr], W=[acc])
                    else:
                        k.tt("dve", sg[:, 0:TT], sg[:, 0:TT], pbr[:, :], ALU.mult, R=[sg, pbr], W=[sg])
                        if i == 1:
                            k.tt("pool", acc[:, 0:TT], acc[:, 0:TT], sg[:, 0:TT], ALU.add, R=[acc, sg], W=[acc])
                        else:
                            mb, mv = mg(c)
                            k.tt("pool", mv, acc[:, 0:TT], sg[:, 0:TT], ALU.add, R=[acc, sg], W=[mb])
        if t == 0:
            mb, mv = mg(0)
            tmpf = AR_[20]
            k.copy("dve", tmpf[:, 0:TT], mv, R=[mb], W=[tmpf])
            tap(f"merged_l{l}", tmpf[:, 0:TT], [128, TT], [tmpf])
        for cb in range(4):
            b, tap_ = wout_b[l][cb]
            wo = load_w([b], tap_, D, 512)
            for cc in range(4):
                c = cb * 4 + cc
                pz = next_bank()
                for kc in range(KC):
                    mb, mv = mg(kc)
                    k.mm(pz[:, :], wo[:, kc, cc * 128:(cc + 1) * 128], mv, kc == 0, kc == KC - 1, R=[wo, mb], W=[pz])
                k.stt("dve", hT[c][:, :], hT[c][:, :], ALPHA, pz[:, :], ALU.mult, ALU.add, R=[hT[c], pz], W=[hT[c]])
        layer_norm(Lp + "ln1g", Lp + "ln1b")

    def ffn(l, t):
        Lp = f"L{l}_"
        ofw = PM.off[Lp + "fcw"]
        ofb = PM.off[Lp + "fcb"]

        def aT(c):
            return AR_[2 + c // 2], AR_[2 + c // 2][:, 0:512].bitcast(BF16)[:, (c % 2) * 512:(c % 2) * 512 + 512]

        Wk = [AR_[0], AR_[1]]
        cv = [AR_[24], AR_[25]]
        for j in range(22):
            bufs, tap_ = wup_b[l][j]
            wu = load_w(bufs, tap_, D, 512)
            for cc in range(2):
                c = j * 2 + cc
                res = []
                for half in range(2):
                    ci = half * 44 + c
                    pb = next_bank()
                    proj(wu, half * 256 + cc * 128, 128, pb[:, :], pb)
                    W_ = Wk[half]
                    k.act(W_[:, 2:514], pb[:, :], AF.Copy, R=[pb], W=[W_])
                    k.copy("dve", W_[:, 0:2], carF[l][:, ci, :], R=[carF[l]], W=[W_])
                    k.copy("pool", carF[l][:, ci, :], W_[:, 512:514], R=[W_], W=[carF[l]])
                    dst = cv[half]
                    k.ts("dve", dst[:, 0:TT], W_[:, 0:512], par[:, ofw + ci:ofw + ci + 1], ALU.mult,
                         R=[W_, par], W=[dst], s2=par[:, ofb + ci:ofb + ci + 1], op1=ALU.add)
                    for jj in range(1, 3):
                        k.stt("dve" if jj == 1 else "pool", dst[:, 0:TT], W_[:, jj:jj + 512],
                              par[:, ofw + 88 * jj + ci:ofw + 88 * jj + ci + 1], dst[:, 0:TT], ALU.mult, ALU.add,
                              R=[W_, dst, par], W=[dst])
                    res.append(dst)
                k.act(res[0][:, 0:TT], res[0][:, 0:TT], AF.Silu, R=[res[0]], W=[res[0]])
                ab, av = aT(c)
                k.tt("dve", av, res[0][:, 0:TT], res[1][:, 0:TT], ALU.mult, R=[res[0], res[1]], W=[ab])
        for c in range(16):
            b, tap_ = wdown_b[l][c]
            wd = wbuf[wsel[0] % 2]
            wsel[0] += 1
            wdv = wd[:, :, :].rearrange("p a b -> p (a b)")[:, 0:44 * 128].rearrange("p (a b) -> p a b", b=128)
            k.dma("sp", wdv, tap_.rearrange("(kc p) n -> p kc n", p=128), reads=[b], writes=[wd])
            pz = next_bank()
            for kc in range(44):
                ab, av = aT(kc)
                k.mm(pz[:, :], wdv[:, kc, :], av, kc == 0, kc == 43, R=[wd, ab], W=[pz])
            k.stt("dve", hT[c][:, :], hT[c][:, :], ALPHA, pz[:, :], ALU.mult, ALU.add, R=[hT[c], pz], W=[hT[c]])
        layer_norm(Lp + "ln2g", Lp + "ln2b")

    marks = []
    P.marks = marks

    def chk(name):
        if stop == name:
            raise _Stop()

    marks_act = []
    P.marks_act = marks_act
    P.dbg = []

    def mark(name):
        marks.append((name, k.cnt["pe"]))
        marks_act.append((name, k.cnt["dve"]))

    try:
      for t in range(NT):
          for s in range(4):
              for j in range(4):
                  k.dma("sp", AR_[s * 4 + j][:, 0:512], x_d[t * TT + s * 128:t * TT + (s + 1) * 128, j * 512:(j + 1) * 512],
                        writes=[AR_[s * 4 + j]])
          for kc in range(KC):
              pb = next_bank()
              for s in range(4):
                  src = AR_[s * 4 + kc // 4]
                  k.tr(pb[:, s * 128:(s + 1) * 128], src[:, (kc % 4) * 128:(kc % 4 + 1) * 128], ident,
                       R=[src, cst], W=[pb])
              k.copy("dve" if kc % 2 == 0 else "act", hT[kc][:, :], pb[:, :], R=[pb], W=[hT[kc]])
          layer_norm("lning", "lninb")
          if t == 0:
              tap("h0", hT[0][:, :], [128, TT], [hT[0]])
          chk("ln0")
          mark(f"t{t}_pre_end")
          for l in range(NL):
              mark(f"t{t}_l{l}_A")
              if SCHED and stop is None:
                  k.begin_sched()
              mixer_A(l, t)
              if SCHED and stop is None:
                  k.end_sched()
              if stop == "A":
                  tmpf = AR_[20]
                  k.copy("dve", tmpf[:, 0:TT], yT[0][:, 0, :], R=[yT[0]], W=[tmpf])
                  tap(f"y0_l{l}", tmpf[:, 0:TT], [128, TT], [tmpf])
              chk("A")
              mark(f"t{t}_l{l}_B")
              if SCHED and stop is None:
                  k.begin_sched()
              mixer_B(l, t)
              if SCHED and stop is None:
                  k.end_sched()
              if stop == "B":
                  tmpf = AR_[20]
                  k.copy("dve", tmpf[:, 0:TT], yT[1][:, 0, :], R=[yT[1]], W=[tmpf])
                  tap(f"y1_l{l}", tmpf[:, 0:TT], [128, TT], [tmpf])
              chk("B")
              mark(f"t{t}_l{l}_C")
              if SCHED and stop is None:
                  k.begin_sched()
              mixer_C(l, t)
              if SCHED and stop is None:
                  k.end_sched()
              if t == 0:
                  for i in range(3):
                      if i > {"A": 0, "B": 1}.get(stop, 2):
                          break
                      tmpf = AR_[20]
                      k.copy("dve", tmpf[:, 0:TT], yT[i][:, 0, :], R=[yT[i]], W=[tmpf])
                      tap(f"y{i}_l{l}", tmpf[:, 0:TT], [128, TT], [tmpf])
              mark(f"t{t}_l{l}_merge")
              merge_out(l, t)
              if t == 0:
                  tap(f"h1_l{l}", hT[0][:, :], [128, TT], [hT[0]])
              mark(f"t{t}_l{l}_ffn")
              ffn(l, t)
              mark(f"t{t}_l{l}_end")
              if t == 0:
                  tap(f"h2_l{l}", hT[0][:, :], [128, TT], [hT[0]])
          for s in range(4):
              for j in range(4):
                  pb = next_bank()
                  for q in range(4):
                      kc = j * 4 + q
                      k.tr(pb[:, q * 128:(q + 1) * 128], hT[kc][:, s * 128:(s + 1) * 128], ident, R=[hT[kc], cst], W=[pb])
                  ot = AR_[(s * 4 + j) % 8 + 2]
                  k.copy("act" if (s + j) % 2 else "dve", ot[:, 0:512], pb[:, :], R=[pb], W=[ot])
                  k.dma("sp", out_d[t * TT + s * 128:t * TT + (s + 1) * 128, j * 512:(j + 1) * 512], ot[:, 0:512],
                        reads=[ot], writes=[outb], is_output=True)
    except _Stop:
        pass
    k.emit()
    P.nc = nc
    P.k = k
    P.taps = taps
    return P


_CACHE = {}


def kernel(**inputs):
    inp = {kk_: np.asarray(v) for kk_, v in inputs.items()}
    x = inp["x"]
    B = x.shape[0]
    NT = x.shape[1] // TT
    if "prog" not in _CACHE:
        _CACHE["prog"] = build_program(NT=NT, NL=2)
    P = _CACHE["prog"]
    par = pack_params(inp)
    sm = pack_small(inp)
    cst = make_consts()
    shared = dict(
        w_in=np.ascontiguousarray(inp["w_in"], np.float32),
        w_gate=np.ascontiguousarray(inp["w_gate"], np.float32),
        w_branch=np.ascontiguousarray(inp["w_branch"], np.float32),
        w_out=np.ascontiguousarray(inp["w_out"], np.float32),
        w_up=np.ascontiguousarray(inp["w_up"], np.float32),
        w_down=np.ascontiguousarray(inp["w_down"], np.float32),
        par=par, sm=sm, cst=cst)
    in_maps = []
    for c in range(8):
        b = c % B
        m = dict(shared)
        m["x"] = np.ascontiguousarray(x[b], np.float32)
        in_maps.append(m)
    res = run_bass_kernel_spmd(P.nc, in_maps, core_ids=list(range(8)))
    out = np.stack([res.results[b]["out"] for b in range(B)], axis=0)
    return out.astype(np.float32)
```

```python
import numpy as np
import concourse.bass as bass
import concourse.mybir as mybir
from concourse.bass_utils import run_bass_kernel_spmd

F32 = mybir.dt.float32
BF16 = mybir.dt.bfloat16
AF = mybir.ActivationFunctionType
ALU = mybir.AluOpType

EPOCH = 20000
NDMA = 16
SAME_ENGINE_SYNC = True
SCHED = False

D = 2048
KC = 16
TT = 512
CH = 64
NCH = TT // CH
DFF = 5632
NIN = 5304
ALPHA = 4.0 ** 0.25
A_OFF, B_OFF, C_OFF = 0, 1696, 3248


class Buf:
    __slots__ = ("t", "lw", "rd", "name", "psum", "pe_rt")

    def __init__(self, t, name="", psum=False):
        self.t = t
        self.lw = None
        self.rd = {}
        self.name = name
        self.psum = psum
        self.pe_rt = None

    def __getitem__(self, k):
        return self.t[k]


class KB:
    def __init__(self, nc):
        self.nc = nc
        self.engs = ("pe", "act", "dve", "pool", "sp")
        self.prog = {e: [] for e in self.engs}
        self.cnt = {e: 0 for e in self.engs}
        self.sems = {e: [] for e in self.engs}
        self.waited = {}
        self.dma_sems = [nc.alloc_semaphore(f"dq{j}") for j in range(2 * NDMA)]
        self.dma_val = [0] * (2 * NDMA)
        self.dma_rr = {"sp": 0, "pool": 0, "act": 0}
        self.out_tokens = []
        self.nbuf = 0
        self.pending = None

    def sb(self, shape, dtype=F32, name=None):
        self.nbuf += 1
        name = (name or f"sb{self.nbuf}") + "_s"
        return Buf(self.nc.alloc_sbuf_tensor(name, list(shape), dtype), name)

    def ps(self, shape, dtype=F32, name=None):
        self.nbuf += 1
        name = name or f"ps{self.nbuf}"
        return Buf(self.nc.alloc_psum_tensor(name, list(shape), dtype), name, psum=True)

    def dram(self, name, shape, dtype=F32, kind="Internal"):
        t = self.nc.dram_tensor(name, list(shape), dtype, kind=kind)
        return Buf(t.ap(), name)

    def _sem(self, E, ep):
        while len(self.sems[E]) <= ep:
            self.sems[E].append(self.nc.alloc_semaphore(f"s_{E}_{len(self.sems[E])}"))
        return self.sems[E][ep]

    def _resolve(self, E, deps):
        need_c = {}
        need_d = {}
        for tok in deps:
            if tok is None:
                continue
            if tok[0] == "c":
                _, W, g = tok
                if W == E and (E == "pe" or not SAME_ENGINE_SYNC):
                    continue
                if self.waited.get((E, W), -1) >= g:
                    continue
                if need_c.get(W, -1) < g:
                    need_c[W] = g
            else:
                _, j, v = tok
                if self.waited.get((E, "d", j), 0) >= v:
                    continue
                if need_d.get(j, 0) < v:
                    need_d[j] = v
        waits = []
        for W, g in need_c.items():
            self.waited[(E, W)] = g
            ep, v = divmod(g, EPOCH)
            waits.append((self._sem(W, ep), v + 1))
        for j, v in need_d.items():
            self.waited[(E, "d", j)] = v
            waits.append((self.dma_sems[j], v))
        return waits

    def _deps(self, reads, writes, E=None):
        deps = []
        for b in reads:
            if b.lw is not None:
                deps.append(b.lw)
            if b.psum:
                for key, tok in b.rd.items():
                    if key != E:
                        deps.append(tok)
        for b in writes:
            if b.lw is not None:
                deps.append(b.lw)
            deps.extend(b.rd.values())
        return deps

    def begin_sched(self):
        assert self.pending is None
        self.pending = []

    def end_sched(self):
        pend = self.pending
        self.pending = None
        if not pend:
            return
        lastw = {}
        readers = {}
        n = len(pend)
        level = [0] * n
        succ = [[] for _ in range(n)]
        for i, (kind, args, reads, writes) in enumerate(pend):
            deps = set()
            for b in reads:
                if id(b) in lastw:
                    deps.add(lastw[id(b)])
            for b in writes:
                if id(b) in lastw:
                    deps.add(lastw[id(b)])
                for r in readers.get(id(b), ()):
                    deps.add(r)
            deps.discard(i)
            lv = 0
            for d in deps:
                succ[d].append(i)
                if level[d] + 1 > lv:
                    lv = level[d] + 1
            level[i] = lv
            for b in reads:
                readers.setdefault(id(b), []).append(i)
            for b in writes:
                lastw[id(b)] = i
                readers[id(b)] = []
        height = [0] * n
        for i in range(n - 1, -1, -1):
            h = 0
            for j in succ[i]:
                if height[j] + 1 > h:
                    h = height[j] + 1
            height[i] = h
        order = sorted(range(n), key=lambda i: (level[i], -height[i], i))
        for i in order:
            kind, args, reads, writes = pend[i]
            if kind == "op":
                self.op(*args)
            else:
                self.dma(*args)

    def op(self, E, fn, reads=(), writes=(), rt=None):
        if self.pending is not None:
            self.pending.append(("op", (E, fn, reads, writes, rt), list(reads), list(writes)))
            return None
        deps = self._deps(reads, writes, E)
        force = []
        if E == "pe" and rt is not None:
            for b in writes:
                if b.psum:
                    if b.pe_rt is not None and b.pe_rt != rt and b.lw is not None and b.lw[1] == "pe":
                        force.append(b.lw)
                    b.pe_rt = rt
        waits = self._resolve(E, deps)
        for tok in force:
            g = tok[2]
            if self.waited.get(("pe", "pe"), -1) < g:
                self.waited[("pe", "pe")] = g
                ep, v = divmod(g, EPOCH)
                waits.append((self._sem("pe", ep), v + 1))
        g = self.cnt[E]
        self.cnt[E] += 1
        ep, v = divmod(g, EPOCH)
        tok = ("c", E, g)
        self.prog[E].append((waits, fn, self._sem(E, ep), 1))
        for b in reads:
            b.rd[E] = tok
        for b in writes:
            b.lw = tok
            b.rd = {}
        return tok

    def dma(self, Q, out, in_, reads=(), writes=(), is_output=False):
        if self.pending is not None:
            self.pending.append(("dma", (Q, out, in_, reads, writes, is_output), list(reads), list(writes)))
            return None
        deps = self._deps(reads, writes)
        base = NDMA if Q == "pool" else 0
        j = base + self.dma_rr[Q]
        self.dma_rr[Q] = (self.dma_rr[Q] + 1) % NDMA
        if self.dma_val[j] > 0:
            deps.append(("d", j, self.dma_val[j]))
        waits = self._resolve(Q, deps)
        self.dma_val[j] += 16
        tok = ("d", j, self.dma_val[j])
        self.prog[Q].append((waits, lambda e: e.dma_start(out=out, in_=in_), self.dma_sems[j], 16))
        for b in reads:
            b.rd[("d", j)] = tok
        for b in writes:
            b.lw = tok
            b.rd = {}
        if is_output:
            self.out_tokens.append(tok)
        return tok

    def barrier_dma(self):
        toks = [("d", j, v) for j, v in enumerate(self.dma_val) if v > 0]
        for E in self.engs:
            for (h, v) in self._resolve(E, toks):
                self.prog[E].append(([(h, v)], None, None, 0))

    def finish(self):
        toks = list(self.out_tokens) + [("d", j, v) for j, v in enumerate(self.dma_val) if v > 0]
        for (h, v) in self._resolve("sp", toks):
            self.prog["sp"].append(([(h, v)], None, None, 0))

    def emit(self):
        nc = self.nc
        self.finish()
        with nc.Block() as block:
            decs = dict(sp=block.sync, act=block.scalar, dve=block.vector,
                        pool=block.gpsimd, pe=block.tensor)
            for E in self.engs:
                prog = self.prog[E]

                def body(eng, prog=prog):
                    for waits, fn, h, inc in prog:
                        for (wh, wv) in waits:
                            eng.wait_ge(wh, wv)
                        if fn is not None:
                            fn(eng).then_inc(h, inc)

                decs[E](body)

    def mm(self, out, lhsT, rhs, start, stop, R, W):
        rt = (lhsT.base_partition(), lhsT.partition_size())
        return self.op("pe", lambda e: e.matmul(out, lhsT, rhs, start=start, stop=stop), R, W, rt=rt)

    def tr(self, out, in_, ident, R, W):
        rt = (in_.base_partition(), in_.partition_size())
        return self.op("pe", lambda e: e.transpose(out, in_, ident), R, W, rt=rt)

    def act(self, out, in_, func, R, W, bias=None, scale=None):
        kw = {}
        if bias is not None:
            kw["bias"] = bias
        if scale is not None:
            kw["scale"] = scale
        return self.op("act", lambda e: e.activation(out, in_, func, **kw), R, W)

    def tt(self, E, out, in0, in1, op, R, W):
        return self.op(E, lambda e: e.tensor_tensor(out, in0, in1, op), R, W)

    def ts(self, E, out, in0, s1, op0, R, W, s2=None, op1=None):
        if op1 is None:
            return self.op(E, lambda e: e.tensor_scalar(out, in0, s1, None, op0), R, W)
        return self.op(E, lambda e: e.tensor_scalar(out, in0, s1, s2, op0, op1), R, W)

    def stt(self, E, out, in0, scalar, in1, op0, op1, R, W):
        E = "dve"
        return self.op(E, lambda e: e.scalar_tensor_tensor(out, in0, scalar, in1, op0, op1), R, W)

    def copy(self, E, out, in_, R, W):
        if E == "act":
            return self.op(E, lambda e: e.copy(out, in_), R, W)
        return self.op(E, lambda e: e.tensor_copy(out, in_), R, W)

    def memset(self, E, ap, val, W):
        return self.op(E, lambda e: e.memset(ap, val), (), W)

    def recip(self, out, in_, R, W):
        return self.op("dve", lambda e: e.reciprocal(out, in_), R, W)


class ParMap:
    def __init__(self):
        self.off = {}
        self.n = 0

    def add(self, name, ncols):
        self.off[name] = self.n
        self.n += ncols
        return self.off[name]


def _param_map():
    pm = ParMap()
    pm.add("lning", 16)
    pm.add("lninb", 16)
    for l in range(2):
        p = f"L{l}_"
        for nm, n in (("mu_r", 4), ("mu_k", 4), ("mu_v", 4), ("mu_w", 1), ("mu_a", 1), ("mu_g", 1),
                      ("w0", 4), ("a0", 4), ("kk", 4), ("ka", 4), ("rk", 4), ("lnxw", 4), ("lnxb", 4),
                      ("gkb", 2), ("bnw", 1), ("ccw", 48), ("cnw", 1),
                      ("ln1g", 16), ("ln1b", 16), ("ln2g", 16), ("ln2b", 16),
                      ("fcw", 264), ("fcb", 88), ("dtb", 4), ("alog", 4)):
            pm.add(p + nm, n)
    return pm


PM = _param_map()


def _cols(vec):
    vec = np.asarray(vec, np.float32).reshape(-1)
    n = (len(vec) + 127) // 128
    out = np.zeros((n * 128,), np.float32)
    out[:len(vec)] = vec
    return out.reshape(n, 128).T


def pack_params(inp):
    par = np.zeros((128, PM.n), np.float32)

    def put(name, arr):
        o = PM.off[name]
        par[:arr.shape[0], o:o + arr.shape[1]] = arr

    put("lning", _cols(inp["ln_in_g"]))
    put("lninb", _cols(inp["ln_in_b"]))
    for l in range(2):
        p = f"L{l}_"
        mu = inp["mu_a"][l]
        put(p + "mu_r", _cols(mu[0:512]))
        put(p + "mu_k", _cols(mu[512:1024]))
        put(p + "mu_v", _cols(mu[1024:1536]))
        put(p + "mu_w", _cols(mu[1536:1568]))
        put(p + "mu_a", _cols(mu[1568:1600]))
        put(p + "mu_g", _cols(mu[1600:1696]))
        put(p + "w0", _cols(inp["a_w0"][l]))
        put(p + "a0", _cols(inp["a_a0"][l]))
        put(p + "kk", _cols(inp["a_kk"][l]))
        put(p + "ka", _cols(inp["a_ka"][l]))
        put(p + "rk", _cols(inp["a_rk"][l].reshape(-1)))
        put(p + "lnxw", _cols(inp["a_lnx_w"][l]))
        put(p + "lnxb", _cols(inp["a_lnx_b"][l]))
        put(p + "gkb", _cols(inp["b_gk_b"][l]))
        put(p + "bnw", _cols(inp["b_norm_w"][l]))
        cw = inp["c_conv_w"][l]
        put(p + "ccw", np.concatenate([_cols(cw[j]) for j in range(4)], axis=1))
        put(p + "cnw", _cols(inp["c_norm_w"][l]))
        put(p + "ln1g", _cols(inp["ln1_g"][l]))
        put(p + "ln1b", _cols(inp["ln1_b"][l]))
        put(p + "ln2g", _cols(inp["ln2_g"][l]))
        put(p + "ln2b", _cols(inp["ln2_b"][l]))
        fw = inp["ffn_conv_w"][l]
        put(p + "fcw", np.concatenate([_cols(fw[j]) for j in range(3)], axis=1))
        put(p + "fcb", _cols(inp["ffn_conv_b"][l]))
        put(p + "dtb", np.broadcast_to(inp["c_dt_bias"][l][None, :], (128, 4)))
        put(p + "alog", np.broadcast_to(inp["c_a_log"][l][None, :], (128, 4)))
    return par


SM_W2, SM_A2, SM_G2, SM_GK = 0, 512, 1024, 1536
SM_N = 1792


def pack_small(inp):
    sm = np.zeros((2, 128, SM_N), np.float32)
    for l in range(2):
        sm[l, :32, SM_W2:SM_W2 + 512] = inp["a_w2"][l]
        sm[l, :32, SM_A2:SM_A2 + 512] = inp["a_a2"][l]
        sm[l, :96, SM_G2:SM_G2 + 512] = inp["a_g2"][l]
        sm[l, :16, SM_GK:SM_GK + 256] = inp["b_gk_w2"][l]
    return sm


C_ID, C_ONES, C_BD, C_SU, C_IU, C_G3, C_RM = 0, 128, 256, 384, 448, 512, 896
C_N = 1408


def make_consts():
    c = np.zeros((128, C_N), np.float32)
    c[:, C_ID:C_ID + 128] = np.eye(128)
    c[:, C_ONES:C_ONES + 128] = 1.0
    bd = np.zeros((128, 128), np.float32)
    bd[:64, :64] = 1.0
    bd[64:, 64:] = 1.0
    c[:, C_BD:C_BD + 128] = bd
    r = np.arange(64)[:, None]
    q = np.arange(64)[None, :]
    su = (r < q).astype(np.float32)
    iu = (r <= q).astype(np.float32)
    c[:64, C_SU:C_SU + 64] = su
    c[64:, C_SU:C_SU + 64] = su
    c[:64, C_IU:C_IU + 64] = iu
    c[64:, C_IU:C_IU + 64] = iu
    g3 = np.concatenate([iu, su, iu, iu, su, iu], axis=1)
    c[:64, C_G3:C_G3 + 384] = g3
    c[64:, C_G3:C_G3 + 384] = g3
    rm = np.ones((512,), np.float32)
    rm[::64] = 0.0
    c[:, C_RM:C_RM + 512] = rm[None, :]
    return c


def win_blocks():
    blocks = {}
    blocks["A_lora"] = [(A_OFF + 1536, 160)]
    for p in range(4):
        blocks[f"A_pair{p}"] = [(A_OFF + p * 128, 128), (A_OFF + 512 + p * 128, 128),
                                (A_OFF + 1024 + p * 128, 128)]
    blocks["B_lora"] = [(B_OFF + 1024, 16)]
    for p in range(2):
        blocks[f"B_pair{p}a"] = [(B_OFF + p * 128, 128), (B_OFF + 256 + p * 128, 128)]
        blocks[f"B_pair{p}b"] = [(B_OFF + 512 + p * 256, 256), (B_OFF + 1040 + p * 256, 256)]
    blocks["C_ab"] = [(C_OFF + 1536, 8)]
    for h in range(4):
        blocks[f"C_head{h}"] = [(C_OFF + h * 128, 128), (C_OFF + 512 + h * 128, 128),
                                (C_OFF + 1024 + h * 128, 128), (C_OFF + 1544 + h * 128, 128)]
    return blocks


class Prog:
    pass


class _Stop(Exception):
    pass


def build_program(NT=8, NL=2, debug_taps=None, stop=None):
    nc = bass.Bass("TRN2", target_bir_lowering=False)
    k = KB(nc)
    P = Prog()
    taps = {}

    x_d = nc.dram_tensor("x", [NT * TT, D], F32, kind="ExternalInput").ap()
    out_d = nc.dram_tensor("out", [NT * TT, D], F32, kind="ExternalOutput").ap()
    w_in_d = nc.dram_tensor("w_in", [2, D, NIN], F32, kind="ExternalInput").ap()
    w_gate_d = nc.dram_tensor("w_gate", [2, 3, D, D], F32, kind="ExternalInput").ap()
    w_branch_d = nc.dram_tensor("w_branch", [2, 3, 512, D], F32, kind="ExternalInput").ap()
    w_out_d = nc.dram_tensor("w_out", [2, D, D], F32, kind="ExternalInput").ap()
    w_up_d = nc.dram_tensor("w_up", [2, D, 2 * DFF], F32, kind="ExternalInput").ap()
    w_down_d = nc.dram_tensor("w_down", [2, DFF, D], F32, kind="ExternalInput").ap()
    par_d = nc.dram_tensor("par", [128, PM.n], F32, kind="ExternalInput").ap()
    sm_d = nc.dram_tensor("sm", [2, 128, SM_N], F32, kind="ExternalInput").ap()
    cst_d = nc.dram_tensor("cst", [128, C_N], F32, kind="ExternalInput").ap()
    outb = Buf(out_d, "out")

    def tap(name, ap, shape, reads):
        if debug_taps is None or name not in debug_taps or name in taps:
            return
        t = nc.dram_tensor("tap_" + name, list(shape), F32, kind="ExternalOutput").ap()
        taps[name] = t
        k.dma("sp", t, ap, reads=reads, writes=[Buf(t, name)], is_output=True)

    cst = k.sb([128, C_N], F32, "cst")
    par = k.sb([128, PM.n], F32, "par")
    sm = k.sb([128, SM_N], F32, "sm")
    hT = [k.sb([128, TT], F32, f"hT{i}") for i in range(KC)]
    hTb = k.sb([128, KC, TT], BF16, "hTb")
    yT = [k.sb([128, 4, TT], BF16, f"yT{i}") for i in range(3)]
    NAR = 26
    AR_ = [k.sb([128, 516], F32, f"ar{i}") for i in range(NAR)]
    wbuf = [k.sb([128, KC, 512], BF16, f"wbuf{i}") for i in range(2)]
    wsel = [0]
    wbr_buf = [k.sb([128, 4, 512], BF16, f"wbr{i}") for i in range(2)]
    wbr_sel = [0]
    bank = [k.ps([128, 512], F32, f"bank{i}") for i in range(8)]
    TK = [None] + [k.sb([64, 512], F32, f"tk{i}") for i in range(1, 5)]
    tiny = [k.sb([128, 128], F32, f"tiny{i}") for i in range(7)]
    TB = [k.sb([64, 512], BF16, f"tb{i}") for i in range(11)]
    stA = [[k.sb([128, 64], F32, f"stA{l}_{p}") for p in range(4)] for l in range(NL)]
    stB = [[k.sb([128, 128], F32, f"stB{l}_{p}") for p in range(2)] for l in range(NL)]
    stC = [[k.sb([128, 128], F32, f"stC{l}_{h}") for h in range(4)] for l in range(NL)]
    stAb = [[k.sb([128, 64], BF16, f"stAb{l}_{p}") for p in range(4)] for l in range(NL)]
    stBb = [[k.sb([128, 128], BF16, f"stBb{l}_{p}") for p in range(2)] for l in range(NL)]
    stCb = [[k.sb([128, 128], BF16, f"stCb{l}_{h}") for h in range(4)] for l in range(NL)]
    carA = [k.sb([128, 16], F32, f"carA{l}") for l in range(NL)]
    carC = [k.sb([128, 12, 3], F32, f"carC{l}") for l in range(NL)]
    carF = [k.sb([128, 88, 2], F32, f"carF{l}") for l in range(NL)]
    lay = [k.sb([128, 8], F32, f"lay{l}") for l in range(NL)]

    ident = cst[:, C_ID:C_ID + 128]
    ones = cst[:, C_ONES:C_ONES + 128]
    bd64 = cst[:, C_BD:C_BD + 128]

    def pc(name, j=0, rows=128):
        o = PM.off[name] + j
        return par[0:rows, o:o + 1]

    k.dma("sp", cst[:, :], cst_d, writes=[cst])
    k.dma("sp", par[:, :], par_d, writes=[par])

    WB = win_blocks()
    win_b = []
    gate_b, branch_b, wout_b, wup_b, wdown_b = [], [], [], [], []
    for l in range(NL):
        d = {}
        for bname, segs in WB.items():
            n = sum(s[1] for s in segs)
            t = nc.dram_tensor(f"wb_in{l}_{bname}", [D, n], BF16, kind="Internal").ap()
            bufs = []
            o = 0
            for (sc, sn) in segs:
                b = Buf(t[:, o:o + sn], f"{bname}{o}")
                k.dma("pool", t[:, o:o + sn], w_in_d[l, :, sc:sc + sn], writes=[b])
                bufs.append(b)
                o += sn
            d[bname] = (bufs, t, n)
        win_b.append(d)
        g = []
        for i in range(3):
            for cb in range(4):
                t = nc.dram_tensor(f"wb_g{l}_{i}_{cb}", [D, 512], BF16, kind="Internal").ap()
                b = Buf(t, f"g{l}{i}{cb}")
                k.dma("pool", t, w_gate_d[l, i, :, cb * 512:(cb + 1) * 512], writes=[b])
                g.append((b, t))
        gate_b.append(g)
        br = []
        for i in range(3):
            t = nc.dram_tensor(f"wb_br{l}_{i}", [512, D], BF16, kind="Internal").ap()
            b = Buf(t, f"br{l}{i}")
            k.dma("pool", t, w_branch_d[l, i], writes=[b])
            br.append((b, t))
        branch_b.append(br)
        wo = []
        for cb in range(4):
            t = nc.dram_tensor(f"wb_o{l}_{cb}", [D, 512], BF16, kind="Internal").ap()
            b = Buf(t, f"o{l}{cb}")
            k.dma("pool", t, w_out_d[l, :, cb * 512:(cb + 1) * 512], writes=[b])
            wo.append((b, t))
        wout_b.append(wo)
        wu = []
        for j in range(22):
            t = nc.dram_tensor(f"wb_u{l}_{j}", [D, 512], BF16, kind="Internal").ap()
            b0 = Buf(t[:, 0:256], f"u{l}{j}a")
            b1 = Buf(t[:, 256:512], f"u{l}{j}b")
            k.dma("pool", t[:, 0:256], w_up_d[l, :, j * 256:(j + 1) * 256], writes=[b0])
            k.dma("pool", t[:, 256:512], w_up_d[l, :, DFF + j * 256:DFF + (j + 1) * 256], writes=[b1])
            wu.append(([b0, b1], t))
        wup_b.append(wu)
        wd = []
        for cb in range(16):
            t = nc.dram_tensor(f"wb_d{l}_{cb}", [DFF, 128], BF16, kind="Internal").ap()
            b = Buf(t, f"d{l}{cb}")
            k.dma("pool", t, w_down_d[l, :, cb * 128:(cb + 1) * 128], writes=[b])
            wd.append((b, t))
        wdown_b.append(wd)

    k.barrier_dma()

    def load_w(bufs, t_ap, nrows, ncols):
        wb = wbuf[wsel[0] % 2]
        wsel[0] += 1
        kc = nrows // 128
        k.dma("sp", wb[:, 0:kc, 0:ncols], t_ap.rearrange("(kc p) n -> p kc n", p=128),
              reads=bufs, writes=[wb])
        return wb

    bsel = [0]

    def next_bank():
        b = bank[bsel[0] % 2]
        bsel[0] += 1
        return b

    def proj(wb, c0, M, ps_ap, psb):
        for kc in range(KC):
            k.mm(ps_ap, wb[:, kc, c0:c0 + M], hTb[:, kc, :], kc == 0, kc == KC - 1,
                 R=[wb, hTb], W=[psb])

    def layer_norm(gname, bname):
        mean = AR_[0]
        rstd = AR_[1]
        sq = [AR_[2], AR_[3]]
        pm_, pv_ = bank[2], bank[3]
        for kc in range(KC):
            k.mm(pm_[:, :], ones, hT[kc][:, :], kc == 0, kc == KC - 1, R=[hT[kc], cst], W=[pm_])
        k.act(mean[:, 0:TT], pm_[:, :], AF.Copy, R=[pm_], W=[mean], scale=1.0 / D)
        for kc in range(KC):
            k.tt("dve" if kc % 2 == 0 else "pool", hT[kc][:, :], hT[kc][:, :], mean[:, 0:TT], ALU.subtract,
                 R=[hT[kc], mean], W=[hT[kc]])
        for kc in range(KC):
            s = sq[kc % 2]
            k.act(s[:, 0:TT], hT[kc][:, :], AF.Square, R=[hT[kc]], W=[s])
            k.mm(pv_[:, :], ones, s[:, 0:TT], kc == 0, kc == KC - 1, R=[s, cst], W=[pv_])
        k.ts("dve", rstd[:, 0:TT], pv_[:, :], 1.0 / D, ALU.mult, R=[pv_], W=[rstd], s2=1e-5, op1=ALU.add)
        k.act(rstd[:, 0:TT], rstd[:, 0:TT], AF.Sqrt, R=[rstd], W=[rstd])
        k.recip(rstd[:, 0:TT], rstd[:, 0:TT], R=[rstd], W=[rstd])
        for kc in range(KC):
            k.tt("dve" if kc % 2 == 0 else "pool", hT[kc][:, :], hT[kc][:, :], rstd[:, 0:TT], ALU.mult,
                 R=[hT[kc], rstd], W=[hT[kc]])
            k.ts("dve", hT[kc][:, :], hT[kc][:, :], pc(gname, kc), ALU.mult, R=[hT[kc], par], W=[hT[kc]],
                 s2=pc(bname, kc), op1=ALU.add)
            k.copy("act", hTb[:, kc, :], hT[kc][:, :], R=[hT[kc]], W=[hTb])

    def rstd_from(ps_ap, psb, out_t, eps, scale, rows=128, n=TT):
        k.ts("dve", out_t[0:rows, 0:n], ps_ap, scale, ALU.mult, R=[psb], W=[out_t], s2=eps, op1=ALU.add)
        k.act(out_t[0:rows, 0:n], out_t[0:rows, 0:n], AF.Sqrt, R=[out_t], W=[out_t])
        k.recip(out_t[0:rows, 0:n], out_t[0:rows, 0:n], R=[out_t], W=[out_t])

    for l in range(NL):
        for b in stA[l] + stB[l] + stC[l] + stAb[l] + stBb[l] + stCb[l] + [carA[l], carC[l], carF[l]]:
            k.memset("pool", b.t[:], 0.0, W=[b])
        k.act(lay[l][:, 0:4], par[:, PM.off[f"L{l}_alog"]:PM.off[f"L{l}_alog"] + 4], AF.Exp, R=[par], W=[lay[l]])
        k.ts("dve", lay[l][:, 0:4], lay[l][:, 0:4], -1.0, ALU.mult, R=[lay[l]], W=[lay[l]])

    def mixer_A(l, t):
        Lp = f"L{l}_"
        wl_t, al_t, gl_t = AR_[2], AR_[3], AR_[4]
        W = AR_[5]

        def shift_mix(psb, ps_ap, rows, mu_name, mu_j, car_idx, out_t):
            k.act(W[0:rows, 1:513], ps_ap, AF.Copy, R=[psb], W=[W])
            k.copy("dve", W[0:rows, 0:1], carA[l][0:rows, car_idx:car_idx + 1], R=[carA[l]], W=[W])
            k.copy("pool", carA[l][0:rows, car_idx:car_idx + 1], W[0:rows, 512:513], R=[W], W=[carA[l]])
            mu = pc(Lp + mu_name, mu_j, rows)
            k.tt("dve", out_t[0:rows, 0:TT], W[0:rows, 0:512], W[0:rows, 1:513], ALU.subtract, R=[W], W=[out_t])
            k.stt("dve", out_t[0:rows, 0:TT], out_t[0:rows, 0:TT], mu, W[0:rows, 1:513], ALU.mult, ALU.add,
                  R=[out_t, W, par], W=[out_t])

        bufs, tap_, n = win_b[l]["A_lora"]
        wb = load_w(bufs, tap_, D, n)
        for (c0, M, nm, ci, dst) in ((0, 32, "mu_w", 12, wl_t), (32, 32, "mu_a", 13, al_t), (64, 96, "mu_g", 14, gl_t)):
            pb = next_bank()
            proj(wb, c0, M, pb[0:M, :], pb)
            shift_mix(pb, pb[0:M, :], M, nm, 0, ci, dst)
        k.act(wl_t[0:32, 0:TT], wl_t[0:32, 0:TT], AF.Tanh, R=[wl_t], W=[wl_t])
        k.act(gl_t[0:96, 0:TT], gl_t[0:96, 0:TT], AF.Sigmoid, R=[gl_t], W=[gl_t])
        chk("A1")
        if t == 0 or NL > 1:
            k.dma("sp", sm[:, :], sm_d[l], writes=[sm])

        for p in range(4):
            r_t, kx_t, v_t = AR_[6], AR_[7], AR_[8]
            bufs, tap_, n = win_b[l][f"A_pair{p}"]
            wb = load_w(bufs, tap_, D, n)
            for (c0, nm, ci, dst) in ((0, "mu_r", p, r_t), (128, "mu_k", 4 + p, kx_t), (256, "mu_v", 8 + p, v_t)):
                pb = next_bank()
                proj(wb, c0, 128, pb[:, :], pb)
                shift_mix(pb, pb[:, :], 128, nm, p, ci, dst)
            if p == 0:
                tap(f"A_r_l{l}", r_t[:, 0:TT], [128, TT], [r_t])
            chk("A1a")
            ld_t, ai_t, g_t = AR_[9], AR_[10], AR_[11]
            cs = slice(p * 128, (p + 1) * 128)
            pb = bank[2]
            k.mm(pb[:, :], sm[0:32, SM_W2 + p * 128:SM_W2 + (p + 1) * 128], wl_t[0:32, 0:TT], True, True,
                 R=[sm, wl_t], W=[pb])
            k.act(ld_t[:, 0:TT], pb[:, :], AF.Sigmoid, R=[pb, par], W=[ld_t], bias=pc(Lp + "w0", p))
            chk("A1b1")
            k.ts("pool", ld_t[:, 0:TT], ld_t[:, 0:TT], -float(np.exp(-0.5)), ALU.mult, R=[ld_t], W=[ld_t])
            chk("A1b2")
            pb = bank[3]
            k.mm(pb[:, :], sm[0:32, SM_A2 + p * 128:SM_A2 + (p + 1) * 128], al_t[0:32, 0:TT], True, True,
                 R=[sm, al_t], W=[pb])
            k.act(ai_t[:, 0:TT], pb[:, :], AF.Sigmoid, R=[pb, par], W=[ai_t], bias=pc(Lp + "a0", p))
            chk("A1b3")
            pb = bank[2]
            k.mm(pb[:, :], sm[0:96, SM_G2 + p * 128:SM_G2 + (p + 1) * 128], gl_t[0:96, 0:TT], True, True,
                 R=[sm, gl_t], W=[pb])
            k.act(g_t[:, 0:TT], pb[:, :], AF.Copy, R=[pb], W=[g_t])
            chk("A1b")
            kkn_t, tmp_t, rs_t = AR_[12], AR_[13], AR_[14]
            k.ts("pool", kkn_t[:, 0:TT], kx_t[:, 0:TT], pc(Lp + "kk", p), ALU.mult, R=[kx_t, par], W=[kkn_t])
            k.act(tmp_t[:, 0:TT], kkn_t[:, 0:TT], AF.Square, R=[kkn_t], W=[tmp_t])
            pb = bank[3]
            k.mm(pb[:, :], bd64, tmp_t[:, 0:TT], True, True, R=[cst, tmp_t], W=[pb])
            rstd_from(pb[:, :], pb, rs_t, 1e-6, 1.0)
            k.tt("dve", kkn_t[:, 0:TT], kkn_t[:, 0:TT], rs_t[:, 0:TT], ALU.mult, R=[kkn_t, rs_t], W=[kkn_t])
            chk("A1c")
            kp_t = AR_[15]
            k.ts("dve", tmp_t[:, 0:TT], ai_t[:, 0:TT], -1.0, ALU.add, R=[ai_t, par], W=[tmp_t],
                 s2=pc(Lp + "ka", p), op1=ALU.mult)
            k.stt("dve", kp_t[:, 0:TT], tmp_t[:, 0:TT], 1.0, kx_t[:, 0:TT], ALU.add, ALU.mult,
                  R=[tmp_t, kx_t], W=[kp_t])
            bon_t = AR_[16]
            k.stt("dve", tmp_t[:, 0:TT], r_t[:, 0:TT], pc(Lp + "rk", p), kp_t[:, 0:TT], ALU.mult, ALU.mult,
                  R=[r_t, kp_t, par], W=[tmp_t])
            pb = bank[2]
            k.mm(pb[:, :], bd64, tmp_t[:, 0:TT], True, True, R=[cst, tmp_t], W=[pb])
            k.tt("dve", bon_t[:, 0:TT], pb[:, :], v_t[:, 0:TT], ALU.mult, R=[pb, v_t], W=[bon_t])
            chk("A1d")
            bc_t, ep_t, en_t, ex_t = AR_[13], AR_[14], AR_[17], AR_[18]
            k.op("dve", lambda e, bc_t=bc_t, ld_t=ld_t: e.tensor_tensor_scan(
                bc_t[:, 0:TT], cst[:, C_RM:C_RM + TT], ld_t[:, 0:TT], 0.0, ALU.mult, ALU.add),
                [cst, ld_t], [bc_t])
            k.tt("pool", ex_t[:, 0:TT], bc_t[:, 0:TT], ld_t[:, 0:TT], ALU.subtract, R=[bc_t, ld_t], W=[ex_t])
            k.act(ex_t[:, 0:TT], ex_t[:, 0:TT], AF.Exp, R=[ex_t], W=[ex_t])
            k.act(ep_t[:, 0:TT], bc_t[:, 0:TT], AF.Exp, R=[bc_t], W=[ep_t])
            k.act(en_t[:, 0:TT], bc_t[:, 0:TT], AF.Exp, R=[bc_t], W=[en_t], scale=-1.0)
            chk("A1e")
            def bfv(tl):
                return tl[:, 0:256].bitcast(BF16)

            def v3(ap_):
                return ap_.rearrange("p (n c) -> p n c", c=CH)

            bA, bR, bB, bK = AR_[19], AR_[22], AR_[20], AR_[23]
            ARa, ARr, BKb, BKk = bfv(bA), bfv(bR), bfv(bB), bfv(bK)
            HBt, HBt2 = AR_[21], AR_[24]
            bt32, kt32 = AR_[9], AR_[25]
            k.stt("dve", ARa, kkn_t[:, 0:TT], -1.0, ex_t[:, 0:TT], ALU.mult, ALU.mult, R=[kkn_t, ex_t], W=[bA])
            k.tt("pool", ARr, r_t[:, 0:TT], ep_t[:, 0:TT], ALU.mult, R=[r_t, ep_t], W=[bR])
            k.tt("dve", bt32[:, 0:TT], kkn_t[:, 0:TT], ai_t[:, 0:TT], ALU.mult, R=[kkn_t, ai_t], W=[bt32])
            k.tt("dve", bt32[:, 0:TT], bt32[:, 0:TT], en_t[:, 0:TT], ALU.mult, R=[bt32, en_t], W=[bt32])
            k.copy("act", BKb, bt32[:, 0:TT], R=[bt32], W=[bB])
            k.tt("pool", kt32[:, 0:TT], kp_t[:, 0:TT], en_t[:, 0:TT], ALU.mult, R=[kp_t, en_t], W=[kt32])
            k.copy("act", BKk, kt32[:, 0:TT], R=[kt32], W=[bK])
            el_t = tiny[0]
            k.copy("dve", el_t[:, 0:NCH], ep_t[:, CH - 1:TT:CH], R=[ep_t], W=[el_t])
            elb = el_t[:, 0:NCH].unsqueeze(2).to_broadcast([128, NCH, CH])
            k.tt("dve", v3(HBt[:, 0:TT]), v3(bt32[:, 0:TT]), elb, ALU.mult, R=[bt32, el_t], W=[HBt])
            k.tt("pool", v3(HBt2[:, 0:TT]), v3(kt32[:, 0:TT]), elb, ALU.mult, R=[kt32, el_t], W=[HBt2])
            po = bank[7]
            st = stA[l][p]
            stb = stAb[l][p]
            chk("A2")
            Q32 = TK[2]
            Qb, Pb, Q2b, P2b, Accb = TB[0], TB[1], TB[2], TB[3], TB[4]
            su8 = cst[0:64, C_SU:C_SU + 64].unsqueeze(1).to_broadcast([64, 8, 64])
            id8 = ident[0:64, 0:64].unsqueeze(1).to_broadcast([64, 8, 64])

            def pre(n):
                cs_ = slice(n * CH, (n + 1) * CH)
                pt = bank[6]
                k.tr(pt[0:64, 0:128], v_t[:, cs_], ident, R=[v_t, cst], W=[pt])
                k.tr(pt[0:64, 128:256], HBt[:, cs_], ident, R=[HBt, cst], W=[pt])
                k.tr(pt[0:64, 256:384], HBt2[:, cs_], ident, R=[HBt2, cst], W=[pt])
                tk = TB[5 + n % 2]
                k.act(tk[0:64, 0:384], pt[0:64, 0:384], AF.Copy, R=[pt], W=[tk])
                pg = bank[4]
                for hh in range(2):
                    bs = slice(64 * hh, 64 * hh + 64)
                    o = hh * 192
                    k.mm(pg[0:64, o:o + 64], BKb[bs, cs_], ARr[bs, cs_], True, True, R=[bB, bR], W=[pg])
                    k.mm(pg[0:64, o + 64:o + 128], BKk[bs, cs_], ARa[bs, cs_], True, True, R=[bK, bA], W=[pg])
                    k.mm(pg[0:64, o + 128:o + 192], BKk[bs, cs_], ARr[bs, cs_], True, True, R=[bK, bR], W=[pg])
                gm = TB[7 + n % 2]
                k.tt("dve", gm[0:64, 0:384], pg[0:64, 0:384], cst[0:64, C_G3:C_G3 + 384], ALU.mult, R=[pg, cst], W=[gm])

            def dep(n, m0):
                cs_ = slice(n * CH, (n + 1) * CH)
                tk = TB[5 + n % 2]
                gm = TB[7 + n % 2]
                px = bank[5]
                for hh in range(2):
                    bs = slice(64 * hh, 64 * hh + 64)
                    hs = slice(hh * 64, hh * 64 + 64)
                    o = hh * 192
                    k.mm(px[0:64, hs], ARa[bs, cs_], stb[bs, :], True, False, R=[bA, stb], W=[px])
                    k.mm(px[0:64, hs], gm[0:64, o + 64:o + 128], tk[0:64, hs], False, True, R=[gm, tk], W=[px])
                X = TB[9]
                k.act(X[0:64, 0:128], px[0:64, 0:128], AF.Copy, R=[px], W=[X])
                pu = bank[3]
                for hh in range(2):
                    hs = slice(hh * 64, hh * 64 + 64)
                    ms = slice((m0 + hh) * 64, (m0 + hh) * 64 + 64)
                    k.mm(pu[0:64, hs], Accb[0:64, ms], X[0:64, hs], True, True, R=[Accb, X], W=[pu])
                U = TB[10]
                k.copy("dve", U[0:64, 0:128], pu[0:64, 0:128], R=[pu], W=[U])
                for hh in range(2):
                    bs = slice(64 * hh, 64 * hh + 64)
                    hs = slice(hh * 64, hh * 64 + 64)
                    o = hh * 192
                    k.mm(po[bs, cs_], stb[bs, :], ARr[bs, cs_], True, False, R=[stb, bR], W=[po])
                    k.mm(po[bs, cs_], U[0:64, hs], gm[0:64, o:o + 64], False, False, R=[U, gm], W=[po])
                    k.mm(po[bs, cs_], tk[0:64, hs], gm[0:64, o + 128:o + 192], False, True, R=[tk, gm], W=[po])
                pst = bank[2]
                for hh in range(2):
                    bs = slice(64 * hh, 64 * hh + 64)
                    hs = slice(hh * 64, hh * 64 + 64)
                    k.mm(pst[bs, 0:64], tk[0:64, 128 + hh * 64:128 + hh * 64 + 64], U[0:64, hs], True, False,
                         R=[tk, U], W=[pst])
                    k.mm(pst[bs, 0:64], tk[0:64, 256 + hh * 64:256 + hh * 64 + 64], tk[0:64, hs], False, True,
                         R=[tk], W=[pst])
                P.dbg.append(("A", l, t, p, n, k.cnt["dve"]))
                k.stt("dve", st[:, :], st[:, :], el_t[:, n:n + 1], pst[:, 0:64], ALU.mult, ALU.add,
                      R=[st, el_t, pst], W=[st])
                k.act(stb[:, :], st[:, :], AF.Copy, R=[st], W=[stb])

            for hf in range(2):
                pq = bank[4]
                for hh in range(2):
                    bs = slice(64 * hh, 64 * hh + 64)
                    for c4 in range(4):
                        n = hf * 4 + c4
                        cs_ = slice(n * CH, (n + 1) * CH)
                        o = (c4 * 2 + hh) * 64
                        k.mm(pq[0:64, o:o + 64], BKb[bs, cs_], ARa[bs, cs_], True, True, R=[bB, bA], W=[pq])
                k.tt("dve", v3(Q32[0:64, :]), v3(pq[0:64, :]), su8, ALU.mult, R=[pq, cst], W=[Q32])
                k.copy("act", Qb[0:64, :], Q32[0:64, :], R=[Q32], W=[Qb])
                pp = bank[5]
                for m in range(8):
                    ms = slice(m * 64, m * 64 + 64)
                    k.tr(pp[0:64, ms], Q32[0:64, ms], ident[0:64, 0:64], R=[Q32, cst], W=[pp])
                k.act(Pb[0:64, :], pp[0:64, :], AF.Copy, R=[pp], W=[Pb])
                k.tt("pool", v3(Accb[0:64, :]), v3(Q32[0:64, :]), id8, ALU.add, R=[Q32, cst], W=[Accb])
                cq, cp, nq, np_ = Qb, Pb, Q2b, P2b
                for step in range(5):
                    ps1 = bank[6]
                    for m in range(8):
                        ms = slice(m * 64, m * 64 + 64)
                        k.mm(ps1[0:64, ms], cq[0:64, ms], cp[0:64, ms], True, True, R=[cq, cp], W=[ps1])
                    k.act(np_[0:64, :], ps1[0:64, :], AF.Copy, R=[ps1], W=[np_])
                    if step < 4:
                        ps2 = bank[4]
                        for m in range(8):
                            ms = slice(m * 64, m * 64 + 64)
                            k.mm(ps2[0:64, ms], cp[0:64, ms], cq[0:64, ms], True, True, R=[cq, cp], W=[ps2])
                        k.copy("dve", nq[0:64, :], ps2[0:64, :], R=[ps2], W=[nq])
                    pa_ = bank[5]
                    for m in range(8):
                        ms = slice(m * 64, m * 64 + 64)
                        k.mm(pa_[0:64, ms], np_[0:64, ms], Accb[0:64, ms], True, True, R=[np_, Accb], W=[pa_])
                    k.tt("dve", Accb[0:64, :], Accb[0:64, :], pa_[0:64, :], ALU.add, R=[Accb, pa_], W=[Accb])
                    cq, cp, nq, np_ = nq, np_, cq, cp
                chk("A5")
                pre(hf * 4)
                for c4 in range(4):
                    n = hf * 4 + c4
                    if c4 + 1 < 4:
                        pre(n + 1)
                    dep(n, c4 * 2)
            o_t, cen_t = AR_[9], AR_[10]
            k.act(o_t[:, 0:TT], po[:, :], AF.Copy, R=[po], W=[o_t])
            if p == 0:
                tap(f"A_o_l{l}", o_t[:, 0:TT], [128, TT], [o_t])
            pb = bank[2]
            k.mm(pb[:, :], bd64, o_t[:, 0:TT], True, True, R=[cst, o_t], W=[pb])
            k.stt("dve", cen_t[:, 0:TT], pb[:, :], -1.0 / 64, o_t[:, 0:TT], ALU.mult, ALU.add, R=[pb, o_t], W=[cen_t])
            k.act(tmp_t[:, 0:TT], cen_t[:, 0:TT], AF.Square, R=[cen_t], W=[tmp_t])
            pb = bank[3]
            k.mm(pb[:, :], bd64, tmp_t[:, 0:TT], True, True, R=[cst, tmp_t], W=[pb])
            rstd_from(pb[:, :], pb, rs_t, 64e-5, 1.0 / 64)
            k.tt("dve", cen_t[:, 0:TT], cen_t[:, 0:TT], rs_t[:, 0:TT], ALU.mult, R=[cen_t, rs_t], W=[cen_t])
            k.ts("dve", cen_t[:, 0:TT], cen_t[:, 0:TT], pc(Lp + "lnxw", p), ALU.mult, R=[cen_t, par], W=[cen_t],
                 s2=pc(Lp + "lnxb", p), op1=ALU.add)
            k.tt("pool", cen_t[:, 0:TT], cen_t[:, 0:TT], bon_t[:, 0:TT], ALU.add, R=[cen_t, bon_t], W=[cen_t])
            k.tt("dve", yT[0][:, p, :], cen_t[:, 0:TT], g_t[:, 0:TT], ALU.mult, R=[cen_t, g_t], W=[yT[0]])

    def mixer_B(l, t):
        Lp = f"L{l}_"
        gkl_t = AR_[2]
        bufs, tap_, n = win_b[l]["B_lora"]
        wb = load_w(bufs, tap_, D, n)
        pb = next_bank()
        proj(wb, 0, 16, pb[0:16, :], pb)
        k.act(gkl_t[0:16, 0:TT], pb[0:16, :], AF.Copy, R=[pb], W=[gkl_t])
        for p in range(2):
            q_t, k_t = AR_[3], AR_[4]
            v_t = [AR_[5], AR_[6]]
            g_t = [AR_[7], AR_[8]]
            bufs, tap_, n = win_b[l][f"B_pair{p}a"]
            wb = load_w(bufs, tap_, D, n)
            for (c0, dst) in ((0, q_t), (128, k_t)):
                pb = next_bank()
                proj(wb, c0, 128, pb[:, :], pb)
                k.act(dst[:, 0:TT], pb[:, :], AF.Copy, R=[pb], W=[dst])
            bufs, tap_, n = win_b[l][f"B_pair{p}b"]
            wb = load_w(bufs, tap_, D, n)
            for (c0, dst, fn) in ((0, v_t[0], AF.Copy), (128, v_t[1], AF.Copy), (256, g_t[0], AF.Silu), (384, g_t[1], AF.Silu)):
                pb = next_bank()
                proj(wb, c0, 128, pb[:, :], pb)
                k.act(dst[:, 0:TT], pb[:, :], fn, R=[pb], W=[dst])
            lg_t, bc_t, ep_t, en_t = AR_[9], AR_[10], AR_[11], AR_[12]
            pb = bank[2]
            k.mm(pb[:, :], sm[0:16, SM_GK + p * 128:SM_GK + (p + 1) * 128], gkl_t[0:16, 0:TT], True, True,
                 R=[sm, gkl_t], W=[pb])
            k.act(lg_t[:, 0:TT], pb[:, :], AF.Sigmoid, R=[pb, par], W=[lg_t], bias=pc(Lp + "gkb", p))
            k.act(lg_t[:, 0:TT], lg_t[:, 0:TT], AF.Ln, R=[lg_t], W=[lg_t])
            k.op("dve", lambda e, bc_t=bc_t, lg_t=lg_t: e.tensor_tensor_scan(
                bc_t[:, 0:TT], cst[:, C_RM:C_RM + TT], lg_t[:, 0:TT], 0.0, ALU.mult, ALU.add),
                [cst, lg_t], [bc_t])
            k.act(ep_t[:, 0:TT], bc_t[:, 0:TT], AF.Exp, R=[bc_t], W=[ep_t], scale=1.0 / 16)
            k.act(en_t[:, 0:TT], bc_t[:, 0:TT], AF.Exp, R=[bc_t], W=[en_t], scale=-1.0 / 16)
            def bfv(tl):
                return tl[:, 0:256].bitcast(BF16)

            bQd, bKd = AR_[13], AR_[16]
            qd_b, kd_b = bfv(bQd), bfv(bKd)
            kd_t, kdec_t = AR_[14], AR_[15]
            k.stt("dve", qd_b, q_t[:, 0:TT], 0.125, ep_t[:, 0:TT], ALU.mult, ALU.mult,
                  R=[q_t, ep_t], W=[bQd])
            k.tt("pool", kd_t[:, 0:TT], k_t[:, 0:TT], en_t[:, 0:TT], ALU.mult, R=[k_t, en_t], W=[kd_t])
            k.copy("act", kd_b, kd_t[:, 0:TT], R=[kd_t], W=[bKd])
            el_t = tiny[0]
            k.copy("dve", el_t[:, 0:NCH], ep_t[:, CH - 1:TT:CH], R=[ep_t], W=[el_t])
            elb = el_t[:, 0:NCH].unsqueeze(2).to_broadcast([128, NCH, CH])
            k.tt("dve", kdec_t[:, 0:TT].rearrange("p (n c) -> p n c", c=CH),
                 kd_t[:, 0:TT].rearrange("p (n c) -> p n c", c=CH), elb, ALU.mult, R=[kd_t, el_t], W=[kdec_t])
            st = stB[l][p]
            stb = stBb[l][p]
            po = [bank[6], bank[7]]
            iub = cst[0:64, C_IU:C_IU + 64].unsqueeze(1).to_broadcast([64, 2, 64])

            def pre(n):
                cs_ = slice(n * CH, (n + 1) * CH)
                pt = bank[4]
                k.tr(pt[0:64, 0:128], kdec_t[:, cs_], ident, R=[kdec_t, cst], W=[pt])
                k.tr(pt[0:64, 128:256], v_t[0][:, cs_], ident, R=[v_t[0], cst], W=[pt])
                k.tr(pt[0:64, 256:384], v_t[1][:, cs_], ident, R=[v_t[1], cst], W=[pt])
                tk = TB[5 + n % 2]
                k.act(tk[0:64, 0:384], pt[0:64, 0:384], AF.Copy, R=[pt], W=[tk])
                pa_ = bank[5]
                for hh in range(2):
                    bs = slice(64 * hh, 64 * hh + 64)
                    k.mm(pa_[0:64, hh * 64:hh * 64 + 64], kd_b[bs, cs_], qd_b[bs, cs_], True, True,
                         R=[bKd, bQd], W=[pa_])
                att = TB[7 + n % 2]
                k.tt("dve", att[0:64, 0:128].rearrange("p (h c) -> p h c", h=2),
                     pa_[0:64, 0:128].rearrange("p (h c) -> p h c", h=2), iub, ALU.mult, R=[pa_, cst], W=[att])
                pu = bank[3 - n % 2]
                for hh in range(2):
                    bs = slice(64 * hh, 64 * hh + 64)
                    vs = slice(128 + hh * 128, 256 + hh * 128)
                    k.mm(pu[bs, 0:128], tk[0:64, hh * 64:hh * 64 + 64], tk[0:64, vs], True, True, R=[tk], W=[pu])

            def dep(n):
                cs_ = slice(n * CH, (n + 1) * CH)
                tk = TB[5 + n % 2]
                att = TB[7 + n % 2]
                pu = bank[3 - n % 2]
                for hh in range(2):
                    bs = slice(64 * hh, 64 * hh + 64)
                    vs = slice(128 + hh * 128, 256 + hh * 128)
                    k.mm(po[hh][:, cs_], tk[0:64, vs], att[0:64, hh * 64:hh * 64 + 64], True, False,
                         R=[tk, att], W=[po[hh]])
                    k.mm(po[hh][:, cs_], stb[bs, :], qd_b[bs, cs_], False, True, R=[stb, bQd], W=[po[hh]])
                k.stt("dve", st[:, :], st[:, :], el_t[:, n:n + 1], pu[:, 0:128], ALU.mult, ALU.add,
                      R=[st, el_t, pu], W=[st])
                k.act(stb[:, :], st[:, :], AF.Copy, R=[st], W=[stb])

            pre(0)
            for n in range(NCH):
                if n + 1 < NCH:
                    pre(n + 1)
                dep(n)
            for hh in range(2):
                h = 2 * p + hh
                sq_t, rs_t = AR_[9], AR_[10]
                k.act(sq_t[:, 0:TT], po[hh][:, :], AF.Square, R=[po[hh]], W=[sq_t])
                pb = bank[2 + hh]
                k.mm(pb[:, :], ones, sq_t[:, 0:TT], True, True, R=[cst, sq_t], W=[pb])
                rstd_from(pb[:, :], pb, rs_t, 1e-5, 1.0 / 128)
                k.tt("dve", sq_t[:, 0:TT], po[hh][:, :], rs_t[:, 0:TT], ALU.mult, R=[po[hh], rs_t], W=[sq_t])
                if h == 0:
                    tap(f"B_on_l{l}", sq_t[:, 0:TT], [128, TT], [sq_t])
                k.stt("dve", yT[1][:, h, :], sq_t[:, 0:TT], pc(Lp + "bnw", 0), g_t[hh][:, 0:TT], ALU.mult, ALU.mult,
                      R=[sq_t, g_t[hh], par], W=[yT[1]])

    def mixer_C(l, t):
        Lp = f"L{l}_"
        bufs, tap_, n_ = win_b[l]["C_ab"]
        wb = load_w(bufs, tap_, D, n_)
        pab = bank[2]
        for n in range(NCH):
            for kc in range(KC):
                k.mm(pab[0:64, n * 8:n * 8 + 8], hTb[:, kc, n * CH:(n + 1) * CH], wb[:, kc, 0:8],
                     kc == 0, kc == KC - 1, R=[wb, hTb], W=[pab])
        g_tok, be_tok, gam, eg, egl, elast = tiny[0], tiny[1], tiny[2], tiny[3], tiny[4], tiny[5]
        ab3 = pab[0:64, 0:64].rearrange("p (n c) -> p n c", c=8)
        dtb = par[0:64, PM.off[Lp + "dtb"]:PM.off[Lp + "dtb"] + 4].unsqueeze(1).to_broadcast([64, NCH, 4])
        nea = lay[l][0:64, 0:4].unsqueeze(1).to_broadcast([64, NCH, 4])
        g3 = g_tok[0:64, 0:32].rearrange("p (n c) -> p n c", c=4)
        k.tt("dve", g3, ab3[:, :, 0:4], dtb, ALU.add, R=[pab, par], W=[g_tok])
        k.act(g_tok[0:64, 0:32], g_tok[0:64, 0:32], AF.Exp, R=[g_tok], W=[g_tok])
        k.act(g_tok[0:64, 0:32], g_tok[0:64, 0:32], AF.Ln, R=[g_tok], W=[g_tok], bias=1.0)
        k.tt("dve", g3, g3, nea, ALU.mult, R=[g_tok, lay[l]], W=[g_tok])
        k.act(be_tok[0:64, 0:32].rearrange("p (n c) -> p n c", c=4), ab3[:, :, 4:8], AF.Sigmoid, R=[pab], W=[be_tok])
        pg = bank[3]
        k.mm(pg[0:64, 0:32], cst[0:64, C_IU:C_IU + 64], g_tok[0:64, 0:32], True, True, R=[cst, g_tok], W=[pg])
        k.mm(pg[:, 32:64], ones[0:64, :], g_tok[0:64, 0:32], True, True, R=[cst, g_tok], W=[pg])
        k.act(gam[0:64, 0:32], pg[0:64, 0:32], AF.Copy, R=[pg], W=[gam])
        k.act(eg[0:64, 0:32], pg[0:64, 0:32], AF.Exp, R=[pg], W=[eg], scale=1.0)
        k.act(elast[:, 0:32], pg[:, 32:64], AF.Exp, R=[pg], W=[elast])
        k.tt("dve", egl[0:64, 0:32], pg[0:64, 32:64], gam[0:64, 0:32], ALU.subtract, R=[pg, gam], W=[egl])
        k.act(egl[0:64, 0:32], egl[0:64, 0:32], AF.Exp, R=[egl], W=[egl])
        nbeg = tiny[6]
        k.ts("pool", nbeg[0:64, 0:32], eg[0:64, 0:32], -1.0, ALU.mult, R=[eg], W=[nbeg])
        for h in range(4):
            bufs, tap_, n_ = win_b[l][f"C_head{h}"]
            wb = load_w(bufs, tap_, D, n_)
            Wk = AR_[2]
            q_t, k_t, v_t, z_t = AR_[3], AR_[4], AR_[5], AR_[6]
            for (c0, ci, dst) in ((0, h, q_t), (128, 4 + h, k_t), (256, 8 + h, v_t)):
                pb = next_bank()
                proj(wb, c0, 128, pb[:, :], pb)
                k.act(Wk[:, 3:515], pb[:, :], AF.Copy, R=[pb], W=[Wk])
                k.copy("dve", Wk[:, 0:3], carC[l][:, ci, :], R=[carC[l]], W=[Wk])
                k.copy("pool", carC[l][:, ci, :], Wk[:, 512:515], R=[Wk], W=[carC[l]])
                o = PM.off[Lp + "ccw"]
                k.ts("dve", dst[:, 0:TT], Wk[:, 0:512], par[:, o + ci:o + ci + 1], ALU.mult, R=[Wk, par], W=[dst])
                for j in range(1, 4):
                    k.stt("dve", dst[:, 0:TT], Wk[:, j:j + 512], par[:, o + 12 * j + ci:o + 12 * j + ci + 1],
                          dst[:, 0:TT], ALU.mult, ALU.add, R=[Wk, dst, par], W=[dst])
                k.act(dst[:, 0:TT], dst[:, 0:TT], AF.Silu, R=[dst], W=[dst])
            pb = next_bank()
            proj(wb, 384, 128, pb[:, :], pb)
            k.act(z_t[:, 0:TT], pb[:, :], AF.Silu, R=[pb], W=[z_t])
            sq_t, rs_t = AR_[7], AR_[8]
            for (src, scl) in ((q_t, 128.0 ** -0.5), (k_t, 1.0)):
                k.act(sq_t[:, 0:TT], src[:, 0:TT], AF.Square, R=[src], W=[sq_t])
                pb = bank[2]
                k.mm(pb[:, :], ones, sq_t[:, 0:TT], True, True, R=[cst, sq_t], W=[pb])
                rstd_from(pb[:, :], pb, rs_t, 1e-6, 1.0)
                k.stt("dve", src[:, 0:TT], src[:, 0:TT], scl, rs_t[:, 0:TT], ALU.mult, ALU.mult,
                      R=[src, rs_t], W=[src])
            if h == 0:
                tap(f"C_k_l{l}", k_t[:, 0:TT], [128, TT], [k_t])
                tap(f"C_v_l{l}", v_t[:, 0:TT], [128, TT], [v_t])
            st = stC[l][h]
            stb = stCb[l][h]
            po = bank[7]

            def bfv(tl):
                return tl[:, 0:256].bitcast(BF16)

            def v3(ap_):
                return ap_.rearrange("p (n c) -> p n c", c=CH)

            bKb, bQb, bQg = AR_[13], AR_[14], AR_[15]
            k_tb, q_tb, qg_b = bfv(bKb), bfv(bQb), bfv(bQg)
            k.copy("act", k_tb, k_t[:, 0:TT], R=[k_t], W=[bKb])
            k.copy("pool", q_tb, q_t[:, 0:TT], R=[q_t], W=[bQb])
            gcol = gam[0:64, h:32:4].unsqueeze(2).to_broadcast([64, NCH, CH])
            bcol = be_tok[0:64, h:32:4].unsqueeze(2).to_broadcast([64, NCH, CH])
            su8 = cst[0:64, C_SU:C_SU + 64].unsqueeze(1).to_broadcast([64, 8, 64])
            iu8 = cst[0:64, C_IU:C_IU + 64].unsqueeze(1).to_broadcast([64, 8, 64])
            id8 = ident[0:64, 0:64].unsqueeze(1).to_broadcast([64, 8, 64])
            dg, DT, Q32, qk32 = TK[1], TK[2], TK[3], TK[4]
            k.tt("pool", v3(dg[0:64, :]), id8, gcol, ALU.mult, R=[cst, gam], W=[dg])
            pgr = bank[5]
            k.mm(pgr[:, :], ones[0:64, :], dg[0:64, :], True, True, R=[cst, dg], W=[pgr])
            k.tt("dve", v3(DT[0:64, :]), v3(pgr[0:64, :]), gcol, ALU.subtract, R=[pgr, gam], W=[DT])
            k.ts("pool", DT[0:64, :], DT[0:64, :], 0.0, ALU.min, R=[DT], W=[DT])
            k.act(DT[0:64, :], DT[0:64, :], AF.Exp, R=[DT], W=[DT])
            eg_r = AR_[9]
            k.act(eg_r[:, 0:TT], pgr[:, :], AF.Exp, R=[pgr], W=[eg_r])
            k.tt("dve", qg_b, eg_r[:, 0:TT], q_t[:, 0:TT], ALU.mult, R=[eg_r, q_t], W=[bQg])
            pkk, pqk = bank[6], bank[4]
            for n in range(NCH):
                cs_ = slice(n * CH, (n + 1) * CH)
                k.mm(pkk[0:64, cs_], k_tb[:, cs_], k_tb[:, cs_], True, True, R=[bKb], W=[pkk])
            for n in range(NCH):
                cs_ = slice(n * CH, (n + 1) * CH)
                k.mm(pqk[0:64, cs_], k_tb[:, cs_], q_tb[:, cs_], True, True, R=[bKb, bQb], W=[pqk])
            k.tt("dve", Q32[0:64, :], pkk[0:64, :], DT[0:64, :], ALU.mult, R=[pkk, DT], W=[Q32])
            k.tt("dve", v3(Q32[0:64, :]), v3(Q32[0:64, :]), bcol, ALU.mult, R=[Q32, be_tok], W=[Q32])
            k.stt("dve", v3(Q32[0:64, :]), v3(Q32[0:64, :]), -1.0, su8, ALU.mult, ALU.mult, R=[Q32, cst], W=[Q32])
            Qb, Pb, Q2b, P2b, Accb, QKD = TB[0], TB[1], TB[2], TB[3], TB[4], TB[7]
            k.tt("dve", qk32[0:64, :], pqk[0:64, :], DT[0:64, :], ALU.mult, R=[pqk, DT], W=[qk32])
            k.tt("pool", v3(QKD[0:64, :]), v3(qk32[0:64, :]), iu8, ALU.mult, R=[qk32, cst], W=[QKD])
            k.copy("act", Qb[0:64, :], Q32[0:64, :], R=[Q32], W=[Qb])
            pp = bank[5]
            for m in range(8):
                ms = slice(m * 64, m * 64 + 64)
                k.tr(pp[0:64, ms], Q32[0:64, ms], ident[0:64, 0:64], R=[Q32, cst], W=[pp])
            k.act(Pb[0:64, :], pp[0:64, :], AF.Copy, R=[pp], W=[Pb])
            k.tt("pool", v3(Accb[0:64, :]), v3(Q32[0:64, :]), id8, ALU.add, R=[Q32, cst], W=[Accb])
            cq, cp, nq, np_ = Qb, Pb, Q2b, P2b
            for step in range(5):
                ps1 = bank[6]
                for m in range(8):
                    ms = slice(m * 64, m * 64 + 64)
                    k.mm(ps1[0:64, ms], cq[0:64, ms], cp[0:64, ms], True, True, R=[cq, cp], W=[ps1])
                k.act(np_[0:64, :], ps1[0:64, :], AF.Copy, R=[ps1], W=[np_])
                if step < 4:
                    ps2 = bank[4]
                    for m in range(8):
                        ms = slice(m * 64, m * 64 + 64)
                        k.mm(ps2[0:64, ms], cp[0:64, ms], cq[0:64, ms], True, True, R=[cq, cp], W=[ps2])
                    k.copy("dve", nq[0:64, :], ps2[0:64, :], R=[ps2], W=[nq])
                pa_ = bank[5]
                for m in range(8):
                    ms = slice(m * 64, m * 64 + 64)
                    k.mm(pa_[0:64, ms], np_[0:64, ms], Accb[0:64, ms], True, True, R=[np_, Accb], W=[pa_])
                k.tt("dve", Accb[0:64, :], Accb[0:64, :], pa_[0:64, :], ALU.add, R=[Accb, pa_], W=[Accb])
                cq, cp, nq, np_ = nq, np_, cq, cp

            def pre(n):
                cs_ = slice(n * CH, (n + 1) * CH)
                ci = n * 4 + h
                pt = bank[6]
                k.tr(pt[0:64, 0:128], k_t[:, cs_], ident, R=[k_t, cst], W=[pt])
                k.tr(pt[0:64, 128:256], v_t[:, cs_], ident, R=[v_t, cst], W=[pt])
                tk = TB[5 + n % 2]
                k.ts("dve", tk[0:64, 0:128], pt[0:64, 0:128], egl[0:64, ci:ci + 1], ALU.mult, R=[pt, egl], W=[tk])
                k.act(tk[0:64, 128:256], pt[0:64, 128:256], AF.Copy, R=[pt], W=[tk])

            def dep(n):
                cs_ = slice(n * CH, (n + 1) * CH)
                ci = n * 4 + h
                ms = slice(n * 64, n * 64 + 64)
                tk = TB[5 + n % 2]
                pks = bank[3]
                k.mm(pks[0:64, 0:128], k_tb[:, cs_], stb[:, :], True, True, R=[bKb, stb], W=[pks])
                Rp = TB[8]
                k.stt("dve", Rp[0:64, 0:128], pks[0:64, 0:128], nbeg[0:64, ci:ci + 1], tk[0:64, 128:256],
                      ALU.mult, ALU.add, R=[pks, nbeg, tk], W=[Rp])
                pv = bank[2]
                k.mm(pv[0:64, 0:128], Accb[0:64, ms], Rp[0:64, 0:128], True, True, R=[Accb, Rp], W=[pv])
                vn = TB[9]
                k.ts("dve", vn[0:64, 0:128], pv[0:64, 0:128], be_tok[0:64, ci:ci + 1], ALU.mult,
                     R=[pv, be_tok], W=[vn])
                k.mm(po[:, cs_], stb[:, :], qg_b[:, cs_], True, False, R=[stb, bQg], W=[po])
                k.mm(po[:, cs_], vn[0:64, 0:128], QKD[0:64, ms], False, True, R=[vn, QKD], W=[po])
                pst = bank[4]
                k.mm(pst[:, 0:128], tk[0:64, 0:128], vn[0:64, 0:128], True, True, R=[tk, vn], W=[pst])
                P.dbg.append(("C", l, t, h, n, k.cnt["dve"]))
                k.stt("dve", st[:, :], st[:, :], elast[:, ci:ci + 1], pst[:, 0:128], ALU.mult, ALU.add,
                      R=[st, elast, pst], W=[st])
                k.act(stb[:, :], st[:, :], AF.Copy, R=[st], W=[stb])

            pre(0)
            for n in range(NCH):
                if n + 1 < NCH:
                    pre(n + 1)
                dep(n)
            k.act(sq_t[:, 0:TT], po[:, :], AF.Square, R=[po], W=[sq_t])
            pb = bank[2]
            k.mm(pb[:, :], ones, sq_t[:, 0:TT], True, True, R=[cst, sq_t], W=[pb])
            rstd_from(pb[:, :], pb, rs_t, 1e-5, 1.0 / 128)
            k.tt("dve", sq_t[:, 0:TT], po[:, :], rs_t[:, 0:TT], ALU.mult, R=[po, rs_t], W=[sq_t])
            if h == 0:
                tap(f"C_on_l{l}", sq_t[:, 0:TT], [128, TT], [sq_t])
            k.stt("dve", yT[2][:, h, :], sq_t[:, 0:TT], pc(Lp + "cnw", 0), z_t[:, 0:TT], ALU.mult, ALU.mult,
                  R=[sq_t, z_t, par], W=[yT[2]])

    def merge_out(l, t):
        Lp = f"L{l}_"
        def mg(c):
            return AR_[2 + c // 2], AR_[2 + c // 2][:, 0:512].bitcast(BF16)[:, (c % 2) * 512:(c % 2) * 512 + 512]

        for cb in range(4):
            for i in range(3):
                b, tap_ = gate_b[l][i * 4 + cb]
                wg = load_w([b], tap_, D, 512)
                bb_, btap = branch_b[l][i]
                wbr = wbr_buf[wbr_sel[0] % 2]
                wbr_sel[0] += 1
                k.dma("sp", wbr[:, :, :], btap[:, cb * 512:(cb + 1) * 512].rearrange("(kc p) n -> p kc n", p=128),
                      reads=[bb_], writes=[wbr])
                for cc in range(4):
                    c = cb * 4 + cc
                    pg_ = next_bank()
                    proj(wg, cc * 128, 128, pg_[:, :], pg_)
                    pbr = bank[2 + (cc % 2)]
                    for kc in range(4):
                        k.mm(pbr[:, :], wbr[:, kc, cc * 128:(cc + 1) * 128], yT[i][:, kc, :], kc == 0, kc == 3,
                             R=[wbr, yT[i]], W=[pbr])
                    sg = AR_[14 + (cc % 2)]
                    k.act(sg[:, 0:TT], pg_[:, :], AF.Sigmoid, R=[pg_], W=[sg])
                    acc = AR_[16 + cc]
                    if i == 0:
                        k.tt("dve", acc[:, 0:TT], sg[:, 0:TT], pbr[:, :], ALU.mult, R=[sg, pbr], W=[acc])
                    else:
                        k.tt("dve", sg[:, 0:TT], sg[:, 0:TT], pbr[:, :], ALU.mult, R=[sg, pbr], W=[sg])
                        if i == 1:
                            k.tt("pool", acc[:, 0:TT], acc[:, 0:TT], sg[:, 0:TT], ALU.add, R=[acc, sg], W=[acc])
                        else:
                            mb, mv = mg(c)
                            k.tt("pool", mv, acc[:, 0:TT], sg[:, 0:TT], ALU.add, R=[acc, sg], W=[mb])
        if t == 0:
            mb, mv = mg(0)
            tmpf = AR_[20]
            k.copy("dve", tmpf[:, 0:TT], mv, R=[mb], W=[tmpf])
            tap(f"merged_l{l}", tmpf[:, 0:TT], [128, TT], [tmpf])
        for cb in range(4):
            b, tap_ = wout_b[l][cb]
            wo = load_w([b], tap_, D, 512)
            for cc in range(4):
                c = cb * 4 + cc
                pz = next_bank()
                for kc in range(KC):
                    mb, mv = mg(kc)
                    k.mm(pz[:, :], wo[:, kc, cc * 128:(cc + 1) * 128], mv, kc == 0, kc == KC - 1, R=[wo, mb], W=[pz])
                k.stt("dve", hT[c][:, :], hT[c][:, :], ALPHA, pz[:, :], ALU.mult, ALU.add, R=[hT[c], pz], W=[hT[c]])
        layer_norm(Lp + "ln1g", Lp + "ln1b")

    def ffn(l, t):
        Lp = f"L{l}_"
        ofw = PM.off[Lp + "fcw"]
        ofb = PM.off[Lp + "fcb"]

        def aT(c):
            return AR_[2 + c // 2], AR_[2 + c // 2][:, 0:512].bitcast(BF16)[:, (c % 2) * 512:(c % 2) * 512 + 512]

        Wk = [AR_[0], AR_[1]]
        cv = [AR_[24], AR_[25]]
        for j in range(22):
            bufs, tap_ = wup_b[l][j]
            wu = load_w(bufs, tap_, D, 512)
            for cc in range(2):
                c = j * 2 + cc
                res = []
                for half in range(2):
                    ci = half * 44 + c
                    pb = next_bank()
                    proj(wu, half * 256 + cc * 128, 128, pb[:, :], pb)
                    W_ = Wk[half]
                    k.act(W_[:, 2:514], pb[:, :], AF.Copy, R=[pb], W=[W_])
                    k.copy("dve", W_[:, 0:2], carF[l][:, ci, :], R=[carF[l]], W=[W_])
                    k.copy("pool", carF[l][:, ci, :], W_[:, 512:514], R=[W_], W=[carF[l]])
                    dst = cv[half]
                    k.ts("dve", dst[:, 0:TT], W_[:, 0:512], par[:, ofw + ci:ofw + ci + 1], ALU.mult,
                         R=[W_, par], W=[dst], s2=par[:, ofb + ci:ofb + ci + 1], op1=ALU.add)
                    for jj in range(1, 3):
                        k.stt("dve" if jj == 1 else "pool", dst[:, 0:TT], W_[:, jj:jj + 512],
                              par[:, ofw + 88 * jj + ci:ofw + 88 * jj + ci + 1], dst[:, 0:TT], ALU.mult, ALU.add,
                              R=[W_, dst, par], W=[dst])
                    res.append(dst)
                k.act(res[0][:, 0:TT], res[0][:, 0:TT], AF.Silu, R=[res[0]], W=[res[0]])
                ab, av = aT(c)
                k.tt("dve", av, res[0][:, 0:TT], res[1][:, 0:TT], ALU.mult, R=[res[0], res[1]], W=[ab])
        for c in range(16):
            b, tap_ = wdown_b[l][c]
            wd = wbuf[wsel[0] % 2]
            wsel[0] += 1
            wdv = wd[:, :, :].rearrange("p a b -> p (a b)")[:, 0:44 * 128].rearrange("p (a b) -> p a b", b=128)
            k.dma("sp", wdv, tap_.rearrange("(kc p) n -> p kc n", p=128), reads=[b], writes=[wd])
            pz = next_bank()
            for kc in range(44):
                ab, av = aT(kc)
                k.mm(pz[:, :], wdv[:, kc, :], av, kc == 0, kc == 43, R=[wd, ab], W=[pz])
            k.stt("dve", hT[c][:, :], hT[c][:, :], ALPHA, pz[:, :], ALU.mult, ALU.add, R=[hT[c], pz], W=[hT[c]])
        layer_norm(Lp + "ln2g", Lp + "ln2b")

    marks = []
    P.marks = marks

    def chk(name):
        if stop == name:
            raise _Stop()

    marks_act = []
    P.marks_act = marks_act
    P.dbg = []

    def mark(name):
        marks.append((name, k.cnt["pe"]))
        marks_act.append((name, k.cnt["dve"]))

    try:
      for t in range(NT):
          for s in range(4):
              for j in range(4):
                  k.dma("sp", AR_[s * 4 + j][:, 0:512], x_d[t * TT + s * 128:t * TT + (s + 1) * 128, j * 512:(j + 1) * 512],
                        writes=[AR_[s * 4 + j]])
          for kc in range(KC):
              pb = next_bank()
              for s in range(4):
                  src = AR_[s * 4 + kc // 4]
                  k.tr(pb[:, s * 128:(s + 1) * 128], src[:, (kc % 4) * 128:(kc % 4 + 1) * 128], ident,
                       R=[src, cst], W=[pb])
              k.copy("dve" if kc % 2 == 0 else "act", hT[kc][:, :], pb[:, :], R=[pb], W=[hT[kc]])
          layer_norm("lning", "lninb")
          if t == 0:
              tap("h0", hT[0][:, :], [128, TT], [hT[0]])
          chk("ln0")
          mark(f"t{t}_pre_end")
          for l in range(NL):
              mark(f"t{t}_l{l}_A")
              if SCHED and stop is None:
                  k.begin_sched()
              mixer_A(l, t)
              if SCHED and stop is None:
                  k.end_sched()
              if stop == "A":
                  tmpf = AR_[20]
                  k.copy("dve", tmpf[:, 0:TT], yT[0][:, 0, :], R=[yT[0]], W=[tmpf])
                  tap(f"y0_l{l}", tmpf[:, 0:TT], [128, TT], [tmpf])
              chk("A")
              mark(f"t{t}_l{l}_B")
              if SCHED and stop is None:
                  k.begin_sched()
              mixer_B(l, t)
              if SCHED and stop is None:
                  k.end_sched()
              if stop == "B":
                  tmpf = AR_[20]
                  k.copy("dve", tmpf[:, 0:TT], yT[1][:, 0, :], R=[yT[1]], W=[tmpf])
                  tap(f"y1_l{l}", tmpf[:, 0:TT], [128, TT], [tmpf])
              chk("B")
              mark(f"t{t}_l{l}_C")
              if SCHED and stop is None:
                  k.begin_sched()
              mixer_C(l, t)
              if SCHED and stop is None:
                  k.end_sched()
              if t == 0:
                  for i in range(3):
                      if i > {"A": 0, "B": 1}.get(stop, 2):
                          break
                      tmpf = AR_[20]
                      k.copy("dve", tmpf[:, 0:TT], yT[i][:, 0, :], R=[yT[i]], W=[tmpf])
                      tap(f"y{i}_l{l}", tmpf[:, 0:TT], [128, TT], [tmpf])
              mark(f"t{t}_l{l}_merge")
              merge_out(l, t)
              if t == 0:
                  tap(f"h1_l{l}", hT[0][:, :], [128, TT], [hT[0]])
              mark(f"t{t}_l{l}_ffn")
              ffn(l, t)
              mark(f"t{t}_l{l}_end")
              if t == 0:
                  tap(f"h2_l{l}", hT[0][:, :], [128, TT], [hT[0]])
          for s in range(4):
              for j in range(4):
                  pb = next_bank()
                  for q in range(4):
                      kc = j * 4 + q
                      k.tr(pb[:, q * 128:(q + 1) * 128], hT[kc][:, s * 128:(s + 1) * 128], ident, R=[hT[kc], cst], W=[pb])
                  ot = AR_[(s * 4 + j) % 8 + 2]
                  k.copy("act" if (s + j) % 2 else "dve", ot[:, 0:512], pb[:, :], R=[pb], W=[ot])
                  k.dma("sp", out_d[t * TT + s * 128:t * TT + (s + 1) * 128, j * 512:(j + 1) * 512], ot[:, 0:512],
                        reads=[ot], writes=[outb], is_output=True)
    except _Stop:
        pass
    k.emit()
    P.nc = nc
    P.k = k
    P.taps = taps
    return P


_CACHE = {}


def kernel(**inputs):
    inp = {kk_: np.asarray(v) for kk_, v in inputs.items()}
    x = inp["x"]
    B = x.shape[0]
    NT = x.shape[1] // TT
    if "prog" not in _CACHE:
        _CACHE["prog"] = build_program(NT=NT, NL=2)
    P = _CACHE["prog"]
    par = pack_params(inp)
    sm = pack_small(inp)
    cst = make_consts()
    shared = dict(
        w_in=np.ascontiguousarray(inp["w_in"], np.float32),
        w_gate=np.ascontiguousarray(inp["w_gate"], np.float32),
        w_branch=np.ascontiguousarray(inp["w_branch"], np.float32),
        w_out=np.ascontiguousarray(inp["w_out"], np.float32),
        w_up=np.ascontiguousarray(inp["w_up"], np.float32),
        w_down=np.ascontiguousarray(inp["w_down"], np.float32),
        par=par, sm=sm, cst=cst)
    in_maps = []
    for c in range(8):
        b = c % B
        m = dict(shared)
        m["x"] = np.ascontiguousarray(x[b], np.float32)
        in_maps.append(m)
    res = run_bass_kernel_spmd(P.nc, in_maps, core_ids=list(range(8)))
    out = np.stack([res.results[b]["out"] for b in range(B)], axis=0)
    return out.astype(np.float32)
```

```python
import numpy as np
import concourse.bass as bass
import concourse.mybir as mybir
from concourse.bass_utils import run_bass_kernel_spmd

F32 = mybir.dt.float32
BF16 = mybir.dt.bfloat16
AF = mybir.ActivationFunctionType
ALU = mybir.AluOpType

EPOCH = 20000
NDMA = 16
SAME_ENGINE_SYNC = True
SCHED = False

D = 2048
KC = 16
TT = 512
CH = 64
NCH = TT // CH
DFF = 5632
NIN = 5304
ALPHA = 4.0 ** 0.25
A_OFF, B_OFF, C_OFF = 0, 1696, 3248


class Buf:
    __slots__ = ("t", "lw", "rd", "name", "psum", "pe_rt")

    def __init__(self, t, name="", psum=False):
        self.t = t
        self.lw = None
        self.rd = {}
        self.name = name
        self.psum = psum
        self.pe_rt = None

    def __getitem__(self, k):
        return self.t[k]


class KB:
    def __init__(self, nc):
        self.nc = nc
        self.engs = ("pe", "act", "dve", "pool", "sp")
        self.prog = {e: [] for e in self.engs}
        self.cnt = {e: 0 for e in self.engs}
        self.sems = {e: [] for e in self.engs}
        self.waited = {}
        self.dma_sems = [nc.alloc_semaphore(f"dq{j}") for j in range(2 * NDMA)]
        self.dma_val = [0] * (2 * NDMA)
        self.dma_rr = {"sp": 0, "pool": 0, "act": 0}
        self.out_tokens = []
        self.nbuf = 0
        self.pending = None

    def sb(self, shape, dtype=F32, name=None):
        self.nbuf += 1
        name = (name or f"sb{self.nbuf}") + "_s"
        return Buf(self.nc.alloc_sbuf_tensor(name, list(shape), dtype), name)

    def ps(self, shape, dtype=F32, name=None):
        self.nbuf += 1
        name = name or f"ps{self.nbuf}"
        return Buf(self.nc.alloc_psum_tensor(name, list(shape), dtype), name, psum=True)

    def dram(self, name, shape, dtype=F32, kind="Internal"):
        t = self.nc.dram_tensor(name, list(shape), dtype, kind=kind)
        return Buf(t.ap(), name)

    def _sem(self, E, ep):
        while len(self.sems[E]) <= ep:
            self.sems[E].append(self.nc.alloc_semaphore(f"s_{E}_{len(self.sems[E])}"))
        return self.sems[E][ep]

    def _resolve(self, E, deps):
        need_c = {}
        need_d = {}
        for tok in deps:
            if tok is None:
                continue
            if tok[0] == "c":
                _, W, g = tok
                if W == E and (E == "pe" or not SAME_ENGINE_SYNC):
                    continue
                if self.waited.get((E, W), -1) >= g:
                    continue
                if need_c.get(W, -1) < g:
                    need_c[W] = g
            else:
                _, j, v = tok
                if self.waited.get((E, "d", j), 0) >= v:
                    continue
                if need_d.get(j, 0) < v:
                    need_d[j] = v
        waits = []
        for W, g in need_c.items():
            self.waited[(E, W)] = g
            ep, v = divmod(g, EPOCH)
            waits.append((self._sem(W, ep), v + 1))
        for j, v in need_d.items():
            self.waited[(E, "d", j)] = v
            waits.append((self.dma_sems[j], v))
        return waits

    def _deps(self, reads, writes, E=None):
        deps = []
        for b in reads:
            if b.lw is not None:
                deps.append(b.lw)
            if b.psum:
                for key, tok in b.rd.items():
                    if key != E:
                        deps.append(tok)
        for b in writes:
            if b.lw is not None:
                deps.append(b.lw)
            deps.extend(b.rd.values())
        return deps

    def begin_sched(self):
        assert self.pending is None
        self.pending = []

    def end_sched(self):
        pend = self.pending
        self.pending = None
        if not pend:
            return
        lastw = {}
        readers = {}
        n = len(pend)
        level = [0] * n
        succ = [[] for _ in range(n)]
        for i, (kind, args, reads, writes) in enumerate(pend):
            deps = set()
            for b in reads:
                if id(b) in lastw:
                    deps.add(lastw[id(b)])
            for b in writes:
                if id(b) in lastw:
                    deps.add(lastw[id(b)])
                for r in readers.get(id(b), ()):
                    deps.add(r)
            deps.discard(i)
            lv = 0
            for d in deps:
                succ[d].append(i)
                if level[d] + 1 > lv:
                    lv = level[d] + 1
            level[i] = lv
            for b in reads:
                readers.setdefault(id(b), []).append(i)
            for b in writes:
                lastw[id(b)] = i
                readers[id(b)] = []
        height = [0] * n
        for i in range(n - 1, -1, -1):
            h = 0
            for j in succ[i]:
                if height[j] + 1 > h:
                    h = height[j] + 1
            height[i] = h
        order = sorted(range(n), key=lambda i: (level[i], -height[i], i))
        for i in order:
            kind, args, reads, writes = pend[i]
            if kind == "op":
                self.op(*args)
            else:
                self.dma(*args)

    def op(self, E, fn, reads=(), writes=(), rt=None):
        if self.pending is not None:
            self.pending.append(("op", (E, fn, reads, writes, rt), list(reads), list(writes)))
            return None
        deps = self._deps(reads, writes, E)
        force = []
        if E == "pe" and rt is not None:
            for b in writes:
                if b.psum:
                    if b.pe_rt is not None and b.pe_rt != rt and b.lw is not None and b.lw[1] == "pe":
                        force.append(b.lw)
                    b.pe_rt = rt
        waits = self._resolve(E, deps)
        for tok in force:
            g = tok[2]
            if self.waited.get(("pe", "pe"), -1) < g:
                self.waited[("pe", "pe")] = g
                ep, v = divmod(g, EPOCH)
                waits.append((self._sem("pe", ep), v + 1))
        g = self.cnt[E]
        self.cnt[E] += 1
        ep, v = divmod(g, EPOCH)
        tok = ("c", E, g)
        self.prog[E].append((waits, fn, self._sem(E, ep), 1))
        for b in reads:
            b.rd[E] = tok
        for b in writes:
            b.lw = tok
            b.rd = {}
        return tok

    def dma(self, Q, out, in_, reads=(), writes=(), is_output=False):
        if self.pending is not None:
            self.pending.append(("dma", (Q, out, in_, reads, writes, is_output), list(reads), list(writes)))
            return None
        deps = self._deps(reads, writes)
        base = NDMA if Q == "pool" else 0
        j = base + self.dma_rr[Q]
        self.dma_rr[Q] = (self.dma_rr[Q] + 1) % NDMA
        if self.dma_val[j] > 0:
            deps.append(("d", j, self.dma_val[j]))
        waits = self._resolve(Q, deps)
        self.dma_val[j] += 16
        tok = ("d", j, self.dma_val[j])
        self.prog[Q].append((waits, lambda e: e.dma_start(out=out, in_=in_), self.dma_sems[j], 16))
        for b in reads:
            b.rd[("d", j)] = tok
        for b in writes:
            b.lw = tok
            b.rd = {}
        if is_output:
            self.out_tokens.append(tok)
        return tok

    def barrier_dma(self):
        toks = [("d", j, v) for j, v in enumerate(self.dma_val) if v > 0]
        for E in self.engs:
            for (h, v) in self._resolve(E, toks):
                self.prog[E].append(([(h, v)], None, None, 0))

    def finish(self):
        toks = list(self.out_tokens) + [("d", j, v) for j, v in enumerate(self.dma_val) if v > 0]
        for (h, v) in self._resolve("sp", toks):
            self.prog["sp"].append(([(h, v)], None, None, 0))

    def emit(self):
        nc = self.nc
        self.finish()
        with nc.Block() as block:
            decs = dict(sp=block.sync, act=block.scalar, dve=block.vector,
                        pool=block.gpsimd, pe=block.tensor)
            for E in self.engs:
                prog = self.prog[E]

                def body(eng, prog=prog):
                    for waits, fn, h, inc in prog:
                        for (wh, wv) in waits:
                            eng.wait_ge(wh, wv)
                        if fn is not None:
                            fn(eng).then_inc(h, inc)

                decs[E](body)

    def mm(self, out, lhsT, rhs, start, stop, R, W):
        rt = (lhsT.base_partition(), lhsT.partition_size())
        return self.op("pe", lambda e: e.matmul(out, lhsT, rhs, start=start, stop=stop), R, W, rt=rt)

    def tr(self, out, in_, ident, R, W):
        rt = (in_.base_partition(), in_.partition_size())
        return self.op("pe", lambda e: e.transpose(out, in_, ident), R, W, rt=rt)

    def act(self, out, in_, func, R, W, bias=None, scale=None):
        kw = {}
        if bias is not None:
            kw["bias"] = bias
        if scale is not None:
            kw["scale"] = scale
        return self.op("act", lambda e: e.activation(out, in_, func, **kw), R, W)

    def tt(self, E, out, in0, in1, op, R, W):
        return self.op(E, lambda e: e.tensor_tensor(out, in0, in1, op), R, W)

    def ts(self, E, out, in0, s1, op0, R, W, s2=None, op1=None):
        if op1 is None:
            return self.op(E, lambda e: e.tensor_scalar(out, in0, s1, None, op0), R, W)
        return self.op(E, lambda e: e.tensor_scalar(out, in0, s1, s2, op0, op1), R, W)

    def stt(self, E, out, in0, scalar, in1, op0, op1, R, W):
        E = "dve"
        return self.op(E, lambda e: e.scalar_tensor_tensor(out, in0, scalar, in1, op0, op1), R, W)

    def copy(self, E, out, in_, R, W):
        if E == "act":
            return self.op(E, lambda e: e.copy(out, in_), R, W)
        return self.op(E, lambda e: e.tensor_copy(out, in_), R, W)

    def memset(self, E, ap, val, W):
        return self.op(E, lambda e: e.memset(ap, val), (), W)

    def recip(self, out, in_, R, W):
        return self.op("dve", lambda e: e.reciprocal(out, in_), R, W)


class ParMap:
    def __init__(self):
        self.off = {}
        self.n = 0

    def add(self, name, ncols):
        self.off[name] = self.n
        self.n += ncols
        return self.off[name]


def _param_map():
    pm = ParMap()
    pm.add("lning", 16)
    pm.add("lninb", 16)
    for l in range(2):
        p = f"L{l}_"
        for nm, n in (("mu_r", 4), ("mu_k", 4), ("mu_v", 4), ("mu_w", 1), ("mu_a", 1), ("mu_g", 1),
                      ("w0", 4), ("a0", 4), ("kk", 4), ("ka", 4), ("rk", 4), ("lnxw", 4), ("lnxb", 4),
                      ("gkb", 2), ("bnw", 1), ("ccw", 48), ("cnw", 1),
                      ("ln1g", 16), ("ln1b", 16), ("ln2g", 16), ("ln2b", 16),
                      ("fcw", 264), ("fcb", 88), ("dtb", 4), ("alog", 4)):
            pm.add(p + nm, n)
    return pm


PM = _param_map()


def _cols(vec):
    vec = np.asarray(vec, np.float32).reshape(-1)
    n = (len(vec) + 127) // 128
    out = np.zeros((n * 128,), np.float32)
    out[:len(vec)] = vec
    return out.reshape(n, 128).T


def pack_params(inp):
    par = np.zeros((128, PM.n), np.float32)

    def put(name, arr):
        o = PM.off[name]
        par[:arr.shape[0], o:o + arr.shape[1]] = arr

    put("lning", _cols(inp["ln_in_g"]))
    put("lninb", _cols(inp["ln_in_b"]))
    for l in range(2):
        p = f"L{l}_"
        mu = inp["mu_a"][l]
        put(p + "mu_r", _cols(mu[0:512]))
        put(p + "mu_k", _cols(mu[512:1024]))
        put(p + "mu_v", _cols(mu[1024:1536]))
        put(p + "mu_w", _cols(mu[1536:1568]))
        put(p + "mu_a", _cols(mu[1568:1600]))
        put(p + "mu_g", _cols(mu[1600:1696]))
        put(p + "w0", _cols(inp["a_w0"][l]))
        put(p + "a0", _cols(inp["a_a0"][l]))
        put(p + "kk", _cols(inp["a_kk"][l]))
        put(p + "ka", _cols(inp["a_ka"][l]))
        put(p + "rk", _cols(inp["a_rk"][l].reshape(-1)))
        put(p + "lnxw", _cols(inp["a_lnx_w"][l]))
        put(p + "lnxb", _cols(inp["a_lnx_b"][l]))
        put(p + "gkb", _cols(inp["b_gk_b"][l]))
        put(p + "bnw", _cols(inp["b_norm_w"][l]))
        cw = inp["c_conv_w"][l]
        put(p + "ccw", np.concatenate([_cols(cw[j]) for j in range(4)], axis=1))
        put(p + "cnw", _cols(inp["c_norm_w"][l]))
        put(p + "ln1g", _cols(inp["ln1_g"][l]))
        put(p + "ln1b", _cols(inp["ln1_b"][l]))
        put(p + "ln2g", _cols(inp["ln2_g"][l]))
        put(p + "ln2b", _cols(inp["ln2_b"][l]))
        fw = inp["ffn_conv_w"][l]
        put(p + "fcw", np.concatenate([_cols(fw[j]) for j in range(3)], axis=1))
        put(p + "fcb", _cols(inp["ffn_conv_b"][l]))
        put(p + "dtb", np.broadcast_to(inp["c_dt_bias"][l][None, :], (128, 4)))
        put(p + "alog", np.broadcast_to(inp["c_a_log"][l][None, :], (128, 4)))
    return par


SM_W2, SM_A2, SM_G2, SM_GK = 0, 512, 1024, 1536
SM_N = 1792


def pack_small(inp):
    sm = np.zeros((2, 128, SM_N), np.float32)
    for l in range(2):
        sm[l, :32, SM_W2:SM_W2 + 512] = inp["a_w2"][l]
        sm[l, :32, SM_A2:SM_A2 + 512] = inp["a_a2"][l]
        sm[l, :96, SM_G2:SM_G2 + 512] = inp["a_g2"][l]
        sm[l, :16, SM_GK:SM_GK + 256] = inp["b_gk_w2"][l]
    return sm


C_ID, C_ONES, C_BD, C_SU, C_IU, C_G3, C_RM = 0, 128, 256, 384, 448, 512, 896
C_N = 1408


def make_consts():
    c = np.zeros((128, C_N), np.float32)
    c[:, C_ID:C_ID + 128] = np.eye(128)
    c[:, C_ONES:C_ONES + 128] = 1.0
    bd = np.zeros((128, 128), np.float32)
    bd[:64, :64] = 1.0
    bd[64:, 64:] = 1.0
    c[:, C_BD:C_BD + 128] = bd
    r = np.arange(64)[:, None]
    q = np.arange(64)[None, :]
    su = (r < q).astype(np.float32)
    iu = (r <= q).astype(np.float32)
    c[:64, C_SU:C_SU + 64] = su
    c[64:, C_SU:C_SU + 64] = su
    c[:64, C_IU:C_IU + 64] = iu
    c[64:, C_IU:C_IU + 64] = iu
    g3 = np.concatenate([iu, su, iu, iu, su, iu], axis=1)
    c[:64, C_G3:C_G3 + 384] = g3
    c[64:, C_G3:C_G3 + 384] = g3
    rm = np.ones((512,), np.float32)
    rm[::64] = 0.0
    c[:, C_RM:C_RM + 512] = rm[None, :]
    return c


def win_blocks():
    blocks = {}
    blocks["A_lora"] = [(A_OFF + 1536, 160)]
    for p in range(4):
        blocks[f"A_pair{p}"] = [(A_OFF + p * 128, 128), (A_OFF + 512 + p * 128, 128),
                                (A_OFF + 1024 + p * 128, 128)]
    blocks["B_lora"] = [(B_OFF + 1024, 16)]
    for p in range(2):
        blocks[f"B_pair{p}a"] = [(B_OFF + p * 128, 128), (B_OFF + 256 + p * 128, 128)]
        blocks[f"B_pair{p}b"] = [(B_OFF + 512 + p * 256, 256), (B_OFF + 1040 + p * 256, 256)]
    blocks["C_ab"] = [(C_OFF + 1536, 8)]
    for h in range(4):
        blocks[f"C_head{h}"] = [(C_OFF + h * 128, 128), (C_OFF + 512 + h * 128, 128),
                                (C_OFF + 1024 + h * 128, 128), (C_OFF + 1544 + h * 128, 128)]
    return blocks


class Prog:
    pass


class _Stop(Exception):
    pass


def build_program(NT=8, NL=2, debug_taps=None, stop=None):
    nc = bass.Bass("TRN2", target_bir_lowering=False)
    k = KB(nc)
    P = Prog()
    taps = {}

    x_d = nc.dram_tensor("x", [NT * TT, D], F32, kind="ExternalInput").ap()
    out_d = nc.dram_tensor("out", [NT * TT, D], F32, kind="ExternalOutput").ap()
    w_in_d = nc.dram_tensor("w_in", [2, D, NIN], F32, kind="ExternalInput").ap()
    w_gate_d = nc.dram_tensor("w_gate", [2, 3, D, D], F32, kind="ExternalInput").ap()
    w_branch_d = nc.dram_tensor("w_branch", [2, 3, 512, D], F32, kind="ExternalInput").ap()
    w_out_d = nc.dram_tensor("w_out", [2, D, D], F32, kind="ExternalInput").ap()
    w_up_d = nc.dram_tensor("w_up", [2, D, 2 * DFF], F32, kind="ExternalInput").ap()
    w_down_d = nc.dram_tensor("w_down", [2, DFF, D], F32, kind="ExternalInput").ap()
    par_d = nc.dram_tensor("par", [128, PM.n], F32, kind="ExternalInput").ap()
    sm_d = nc.dram_tensor("sm", [2, 128, SM_N], F32, kind="ExternalInput").ap()
    cst_d = nc.dram_tensor("cst", [128, C_N], F32, kind="ExternalInput").ap()
    outb = Buf(out_d, "out")

    def tap(name, ap, shape, reads):
        if debug_taps is None or name not in debug_taps or name in taps:
            return
        t = nc.dram_tensor("tap_" + name, list(shape), F32, kind="ExternalOutput").ap()
        taps[name] = t
        k.dma("sp", t, ap, reads=reads, writes=[Buf(t, name)], is_output=True)

    cst = k.sb([128, C_N], F32, "cst")
    par = k.sb([128, PM.n], F32, "par")
    sm = k.sb([128, SM_N], F32, "sm")
    hT = [k.sb([128, TT], F32, f"hT{i}") for i in range(KC)]
    hTb = k.sb([128, KC, TT], BF16, "hTb")
    yT = [k.sb([128, 4, TT], BF16, f"yT{i}") for i in range(3)]
    NAR = 26
    AR_ = [k.sb([128, 516], F32, f"ar{i}") for i in range(NAR)]
    wbuf = [k.sb([128, KC, 512], BF16, f"wbuf{i}") for i in range(2)]
    wsel = [0]
    wbr_buf = [k.sb([128, 4, 512], BF16, f"wbr{i}") for i in range(2)]
    wbr_sel = [0]
    bank = [k.ps([128, 512], F32, f"bank{i}") for i in range(8)]
    TK = [None] + [k.sb([64, 512], F32, f"tk{i}") for i in range(1, 5)]
    tiny = [k.sb([128, 128], F32, f"tiny{i}") for i in range(7)]
    TB = [k.sb([64, 512], BF16, f"tb{i}") for i in range(11)]
    stA = [[k.sb([128, 64], F32, f"stA{l}_{p}") for p in range(4)] for l in range(NL)]
    stB = [[k.sb([128, 128], F32, f"stB{l}_{p}") for p in range(2)] for l in range(NL)]
    stC = [[k.sb([128, 128], F32, f"stC{l}_{h}") for h in range(4)] for l in range(NL)]
    stAb = [[k.sb([128, 64], BF16, f"stAb{l}_{p}") for p in range(4)] for l in range(NL)]
    stBb = [[k.sb([128, 128], BF16, f"stBb{l}_{p}") for p in range(2)] for l in range(NL)]
    stCb = [[k.sb([128, 128], BF16, f"stCb{l}_{h}") for h in range(4)] for l in range(NL)]
    carA = [k.sb([128, 16], F32, f"carA{l}") for l in range(NL)]
    carC = [k.sb([128, 12, 3], F32, f"carC{l}") for l in range(NL)]
    carF = [k.sb([128, 88, 2], F32, f"carF{l}") for l in range(NL)]
    lay = [k.sb([128, 8], F32, f"lay{l}") for l in range(NL)]

    ident = cst[:, C_ID:C_ID + 128]
    ones = cst[:, C_ONES:C_ONES + 128]
    bd64 = cst[:, C_BD:C_BD + 128]

    def pc(name, j=0, rows=128):
        o = PM.off[name] + j
        return par[0:rows, o:o + 1]

    k.dma("sp", cst[:, :], cst_d, writes=[cst])
    k.dma("sp", par[:, :], par_d, writes=[par])

    WB = win_blocks()
    win_b = []
    gate_b, branch_b, wout_b, wup_b, wdown_b = [], [], [], [], []
    for l in range(NL):
        d = {}
        for bname, segs in WB.items():
            n = sum(s[1] for s in segs)
            t = nc.dram_tensor(f"wb_in{l}_{bname}", [D, n], BF16, kind="Internal").ap()
            bufs = []
            o = 0
            for (sc, sn) in segs:
                b = Buf(t[:, o:o + sn], f"{bname}{o}")
                k.dma("pool", t[:, o:o + sn], w_in_d[l, :, sc:sc + sn], writes=[b])
                bufs.append(b)
                o += sn
            d[bname] = (bufs, t, n)
        win_b.append(d)
        g = []
        for i in range(3):
            for cb in range(4):
                t = nc.dram_tensor(f"wb_g{l}_{i}_{cb}", [D, 512], BF16, kind="Internal").ap()
                b = Buf(t, f"g{l}{i}{cb}")
                k.dma("pool", t, w_gate_d[l, i, :, cb * 512:(cb + 1) * 512], writes=[b])
                g.append((b, t))
        gate_b.append(g)
        br = []
        for i in range(3):
            t = nc.dram_tensor(f"wb_br{l}_{i}", [512, D], BF16, kind="Internal").ap()
            b = Buf(t, f"br{l}{i}")
            k.dma("pool", t, w_branch_d[l, i], writes=[b])
            br.append((b, t))
        branch_b.append(br)
        wo = []
        for cb in range(4):
            t = nc.dram_tensor(f"wb_o{l}_{cb}", [D, 512], BF16, kind="Internal").ap()
            b = Buf(t, f"o{l}{cb}")
            k.dma("pool", t, w_out_d[l, :, cb * 512:(cb + 1) * 512], writes=[b])
            wo.append((b, t))
        wout_b.append(wo)
        wu = []
        for j in range(22):
            t = nc.dram_tensor(f"wb_u{l}_{j}", [D, 512], BF16, kind="Internal").ap()
            b0 = Buf(t[:, 0:256], f"u{l}{j}a")
            b1 = Buf(t[:, 256:512], f"u{l}{j}b")
            k.dma("pool", t[:, 0:256], w_up_d[l, :, j * 256:(j + 1) * 256], writes=[b0])
            k.dma("pool", t[:, 256:512], w_up_d[l, :, DFF + j * 256:DFF + (j + 1) * 256], writes=[b1])
            wu.append(([b0, b1], t))
        wup_b.append(wu)
        wd = []
        for cb in range(16):
            t = nc.dram_tensor(f"wb_d{l}_{cb}", [DFF, 128], BF16, kind="Internal").ap()
            b = Buf(t, f"d{l}{cb}")
            k.dma("pool", t, w_down_d[l, :, cb * 128:(cb + 1) * 128], writes=[b])
            wd.append((b, t))
        wdown_b.append(wd)

    k.barrier_dma()

    def load_w(bufs, t_ap, nrows, ncols):
        wb = wbuf[wsel[0] % 2]
        wsel[0] += 1
        kc = nrows // 128
        k.dma("sp", wb[:, 0:kc, 0:ncols], t_ap.rearrange("(kc p) n -> p kc n", p=128),
              reads=bufs, writes=[wb])
        return wb

    bsel = [0]

    def next_bank():
        b = bank[bsel[0] % 2]
        bsel[0] += 1
        return b

    def proj(wb, c0, M, ps_ap, psb):
        for kc in range(KC):
            k.mm(ps_ap, wb[:, kc, c0:c0 + M], hTb[:, kc, :], kc == 0, kc == KC - 1,
                 R=[wb, hTb], W=[psb])

    def layer_norm(gname, bname):
        mean = AR_[0]
        rstd = AR_[1]
        sq = [AR_[2], AR_[3]]
        pm_, pv_ = bank[2], bank[3]
        for kc in range(KC):
            k.mm(pm_[:, :], ones, hT[kc][:, :], kc == 0, kc == KC - 1, R=[hT[kc], cst], W=[pm_])
        k.act(mean[:, 0:TT], pm_[:, :], AF.Copy, R=[pm_], W=[mean], scale=1.0 / D)
        for kc in range(KC):
            k.tt("dve" if kc % 2 == 0 else "pool", hT[kc][:, :], hT[kc][:, :], mean[:, 0:TT], ALU.subtract,
                 R=[hT[kc], mean], W=[hT[kc]])
        for kc in range(KC):
            s = sq[kc % 2]
            k.act(s[:, 0:TT], hT[kc][:, :], AF.Square, R=[hT[kc]], W=[s])
            k.mm(pv_[:, :], ones, s[:, 0:TT], kc == 0, kc == KC - 1, R=[s, cst], W=[pv_])
        k.ts("dve", rstd[:, 0:TT], pv_[:, :], 1.0 / D, ALU.mult, R=[pv_], W=[rstd], s2=1e-5, op1=ALU.add)
        k.act(rstd[:, 0:TT], rstd[:, 0:TT], AF.Sqrt, R=[rstd], W=[rstd])
        k.recip(rstd[:, 0:TT], rstd[:, 0:TT], R=[rstd], W=[rstd])
        for kc in range(KC):
            k.tt("dve" if kc % 2 == 0 else "pool", hT[kc][:, :], hT[kc][:, :], rstd[:, 0:TT], ALU.mult,
                 R=[hT[kc], rstd], W=[hT[kc]])
            k.ts("dve", hT[kc][:, :], hT[kc][:, :], pc(gname, kc), ALU.mult, R=[hT[kc], par], W=[hT[kc]],
                 s2=pc(bname, kc), op1=ALU.add)
            k.copy("act", hTb[:, kc, :], hT[kc][:, :], R=[hT[kc]], W=[hTb])

    def rstd_from(ps_ap, psb, out_t, eps, scale, rows=128, n=TT):
        k.ts("dve", out_t[0:rows, 0:n], ps_ap, scale, ALU.mult, R=[psb], W=[out_t], s2=eps, op1=ALU.add)
        k.act(out_t[0:rows, 0:n], out_t[0:rows, 0:n], AF.Sqrt, R=[out_t], W=[out_t])
        k.recip(out_t[0:rows, 0:n], out_t[0:rows, 0:n], R=[out_t], W=[out_t])

    for l in range(NL):
        for b in stA[l] + stB[l] + stC[l] + stAb[l] + stBb[l] + stCb[l] + [carA[l], carC[l], carF[l]]:
            k.memset("pool", b.t[:], 0.0, W=[b])
        k.act(lay[l][:, 0:4], par[:, PM.off[f"L{l}_alog"]:PM.off[f"L{l}_alog"] + 4], AF.Exp, R=[par], W=[lay[l]])
        k.ts("dve", lay[l][:, 0:4], lay[l][:, 0:4], -1.0, ALU.mult, R=[lay[l]], W=[lay[l]])

    def mixer_A(l, t):
        Lp = f"L{l}_"
        wl_t, al_t, gl_t = AR_[2], AR_[3], AR_[4]
        W = AR_[5]

        def shift_mix(psb, ps_ap, rows, mu_name, mu_j, car_idx, out_t):
            k.act(W[0:rows, 1:513], ps_ap, AF.Copy, R=[psb], W=[W])
            k.copy("dve", W[0:rows, 0:1], carA[l][0:rows, car_idx:car_idx + 1], R=[carA[l]], W=[W])
            k.copy("pool", carA[l][0:rows, car_idx:car_idx + 1], W[0:rows, 512:513], R=[W], W=[carA[l]])
            mu = pc(Lp + mu_name, mu_j, rows)
            k.tt("dve", out_t[0:rows, 0:TT], W[0:rows, 0:512], W[0:rows, 1:513], ALU.subtract, R=[W], W=[out_t])
            k.stt("dve", out_t[0:rows, 0:TT], out_t[0:rows, 0:TT], mu, W[0:rows, 1:513], ALU.mult, ALU.add,
                  R=[out_t, W, par], W=[out_t])

        bufs, tap_, n = win_b[l]["A_lora"]
        wb = load_w(bufs, tap_, D, n)
        for (c0, M, nm, ci, dst) in ((0, 32, "mu_w", 12, wl_t), (32, 32, "mu_a", 13, al_t), (64, 96, "mu_g", 14, gl_t)):
            pb = next_bank()
            proj(wb, c0, M, pb[0:M, :], pb)
            shift_mix(pb, pb[0:M, :], M, nm, 0, ci, dst)
        k.act(wl_t[0:32, 0:TT], wl_t[0:32, 0:TT], AF.Tanh, R=[wl_t], W=[wl_t])
        k.act(gl_t[0:96, 0:TT], gl_t[0:96, 0:TT], AF.Sigmoid, R=[gl_t], W=[gl_t])
        chk("A1")
        if t == 0 or NL > 1:
            k.dma("sp", sm[:, :], sm_d[l], writes=[sm])

        for p in range(4):
            r_t, kx_t, v_t = AR_[6], AR_[7], AR_[8]
            bufs, tap_, n = win_b[l][f"A_pair{p}"]
            wb = load_w(bufs, tap_, D, n)
            for (c0, nm, ci, dst) in ((0, "mu_r", p, r_t), (128, "mu_k", 4 + p, kx_t), (256, "mu_v", 8 + p, v_t)):
                pb = next_bank()
                proj(wb, c0, 128, pb[:, :], pb)
                shift_mix(pb, pb[:, :], 128, nm, p, ci, dst)
            if p == 0:
                tap(f"A_r_l{l}", r_t[:, 0:TT], [128, TT], [r_t])
            chk("A1a")
            ld_t, ai_t, g_t = AR_[9], AR_[10], AR_[11]
            cs = slice(p * 128, (p + 1) * 128)
            pb = bank[2]
            k.mm(pb[:, :], sm[0:32, SM_W2 + p * 128:SM_W2 + (p + 1) * 128], wl_t[0:32, 0:TT], True, True,
                 R=[sm, wl_t], W=[pb])
            k.act(ld_t[:, 0:TT], pb[:, :], AF.Sigmoid, R=[pb, par], W=[ld_t], bias=pc(Lp + "w0", p))
            chk("A1b1")
            k.ts("pool", ld_t[:, 0:TT], ld_t[:, 0:TT], -float(np.exp(-0.5)), ALU.mult, R=[ld_t], W=[ld_t])
            chk("A1b2")
            pb = bank[3]
            k.mm(pb[:, :], sm[0:32, SM_A2 + p * 128:SM_A2 + (p + 1) * 128], al_t[0:32, 0:TT], True, True,
                 R=[sm, al_t], W=[pb])
            k.act(ai_t[:, 0:TT], pb[:, :], AF.Sigmoid, R=[pb, par], W=[ai_t], bias=pc(Lp + "a0", p))
            chk("A1b3")
            pb = bank[2]
            k.mm(pb[:, :], sm[0:96, SM_G2 + p * 128:SM_G2 + (p + 1) * 128], gl_t[0:96, 0:TT], True, True,
                 R=[sm, gl_t], W=[pb])
            k.act(g_t[:, 0:TT], pb[:, :], AF.Copy, R=[pb], W=[g_t])
            chk("A1b")
            kkn_t, tmp_t, rs_t = AR_[12], AR_[13], AR_[14]
            k.ts("pool", kkn_t[:, 0:TT], kx_t[:, 0:TT], pc(Lp + "kk", p), ALU.mult, R=[kx_t, par], W=[kkn_t])
            k.act(tmp_t[:, 0:TT], kkn_t[:, 0:TT], AF.Square, R=[kkn_t], W=[tmp_t])
            pb = bank[3]
            k.mm(pb[:, :], bd64, tmp_t[:, 0:TT], True, True, R=[cst, tmp_t], W=[pb])
            rstd_from(pb[:, :], pb, rs_t, 1e-6, 1.0)
            k.tt("dve", kkn_t[:, 0:TT], kkn_t[:, 0:TT], rs_t[:, 0:TT], ALU.mult, R=[kkn_t, rs_t], W=[kkn_t])
            chk("A1c")
            kp_t = AR_[15]
            k.ts("dve", tmp_t[:, 0:TT], ai_t[:, 0:TT], -1.0, ALU.add, R=[ai_t, par], W=[tmp_t],
                 s2=pc(Lp + "ka", p), op1=ALU.mult)
            k.stt("dve", kp_t[:, 0:TT], tmp_t[:, 0:TT], 1.0, kx_t[:, 0:TT], ALU.add, ALU.mult,
                  R=[tmp_t, kx_t], W=[kp_t])
            bon_t = AR_[16]
            k.stt("dve", tmp_t[:, 0:TT], r_t[:, 0:TT], pc(Lp + "rk", p), kp_t[:, 0:TT], ALU.mult, ALU.mult,
                  R=[r_t, kp_t, par], W=[tmp_t])
            pb = bank[2]
            k.mm(pb[:, :], bd64, tmp_t[:, 0:TT], True, True, R=[cst, tmp_t], W=[pb])
            k.tt("dve", bon_t[:, 0:TT], pb[:, :], v_t[:, 0:TT], ALU.mult, R=[pb, v_t], W=[bon_t])
            chk("A1d")
            bc_t, ep_t, en_t, ex_t = AR_[13], AR_[14], AR_[17], AR_[18]
            k.op("dve", lambda e, bc_t=bc_t, ld_t=ld_t: e.tensor_tensor_scan(
                bc_t[:, 0:TT], cst[:, C_RM:C_RM + TT], ld_t[:, 0:TT], 0.0, ALU.mult, ALU.add),
                [cst, ld_t], [bc_t])
            k.tt("pool", ex_t[:, 0:TT], bc_t[:, 0:TT], ld_t[:, 0:TT], ALU.subtract, R=[bc_t, ld_t], W=[ex_t])
            k.act(ex_t[:, 0:TT], ex_t[:, 0:TT], AF.Exp, R=[ex_t], W=[ex_t])
            k.act(ep_t[:, 0:TT], bc_t[:, 0:TT], AF.Exp, R=[bc_t], W=[ep_t])
            k.act(en_t[:, 0:TT], bc_t[:, 0:TT], AF.Exp, R=[bc_t], W=[en_t], scale=-1.0)
            chk("A1e")
            def bfv(tl):
                return tl[:, 0:256].bitcast(BF16)

            def v3(ap_):
                return ap_.rearrange("p (n c) -> p n c", c=CH)

            bA, bR, bB, bK = AR_[19], AR_[22], AR_[20], AR_[23]
            ARa, ARr, BKb, BKk = bfv(bA), bfv(bR), bfv(bB), bfv(bK)
            HBt, HBt2 = AR_[21], AR_[24]
            bt32, kt32 = AR_[9], AR_[25]
            k.stt("dve", ARa, kkn_t[:, 0:TT], -1.0, ex_t[:, 0:TT], ALU.mult, ALU.mult, R=[kkn_t, ex_t], W=[bA])
            k.tt("pool", ARr, r_t[:, 0:TT], ep_t[:, 0:TT], ALU.mult, R=[r_t, ep_t], W=[bR])
            k.tt("dve", bt32[:, 0:TT], kkn_t[:, 0:TT], ai_t[:, 0:TT], ALU.mult, R=[kkn_t, ai_t], W=[bt32])
            k.tt("dve", bt32[:, 0:TT], bt32[:, 0:TT], en_t[:, 0:TT], ALU.mult, R=[bt32, en_t], W=[bt32])
            k.copy("act", BKb, bt32[:, 0:TT], R=[bt32], W=[bB])
            k.tt("pool", kt32[:, 0:TT], kp_t[:, 0:TT], en_t[:, 0:TT], ALU.mult, R=[kp_t, en_t], W=[kt32])
            k.copy("act", BKk, kt32[:, 0:TT], R=[kt32], W=[bK])
            el_t = tiny[0]
            k.copy("dve", el_t[:, 0:NCH], ep_t[:, CH - 1:TT:CH], R=[ep_t], W=[el_t])
            elb = el_t[:, 0:NCH].unsqueeze(2).to_broadcast([128, NCH, CH])
            k.tt("dve", v3(HBt[:, 0:TT]), v3(bt32[:, 0:TT]), elb, ALU.mult, R=[bt32, el_t], W=[HBt])
            k.tt("pool", v3(HBt2[:, 0:TT]), v3(kt32[:, 0:TT]), elb, ALU.mult, R=[kt32, el_t], W=[HBt2])
            po = bank[7]
            st = stA[l][p]
            stb = stAb[l][p]
            chk("A2")
            Q32 = TK[2]
            Qb, Pb, Q2b, P2b, Accb = TB[0], TB[1], TB[2], TB[3], TB[4]
            su8 = cst[0:64, C_SU:C_SU + 64].unsqueeze(1).to_broadcast([64, 8, 64])
            id8 = ident[0:64, 0:64].unsqueeze(1).to_broadcast([64, 8, 64])

            def pre(n):
                cs_ = slice(n * CH, (n + 1) * CH)
                pt = bank[6]
                k.tr(pt[0:64, 0:128], v_t[:, cs_], ident, R=[v_t, cst], W=[pt])
                k.tr(pt[0:64, 128:256], HBt[:, cs_], ident, R=[HBt, cst], W=[pt])
                k.tr(pt[0:64, 256:384], HBt2[:, cs_], ident, R=[HBt2, cst], W=[pt])
                tk = TB[5 + n % 2]
                k.act(tk[0:64, 0:384], pt[0:64, 0:384], AF.Copy, R=[pt], W=[tk])
                pg = bank[4]
                for hh in range(2):
                    bs = slice(64 * hh, 64 * hh + 64)
                    o = hh * 192
                    k.mm(pg[0:64, o:o + 64], BKb[bs, cs_], ARr[bs, cs_], True, True, R=[bB, bR], W=[pg])
                    k.mm(pg[0:64, o + 64:o + 128], BKk[bs, cs_], ARa[bs, cs_], True, True, R=[bK, bA], W=[pg])
                    k.mm(pg[0:64, o + 128:o + 192], BKk[bs, cs_], ARr[bs, cs_], True, True, R=[bK, bR], W=[pg])
                gm = TB[7 + n % 2]
                k.tt("dve", gm[0:64, 0:384], pg[0:64, 0:384], cst[0:64, C_G3:C_G3 + 384], ALU.mult, R=[pg, cst], W=[gm])

            def dep(n, m0):
                cs_ = slice(n * CH, (n + 1) * CH)
                tk = TB[5 + n % 2]
                gm = TB[7 + n % 2]
                px = bank[5]
                for hh in range(2):
                    bs = slice(64 * hh, 64 * hh + 64)
                    hs = slice(hh * 64, hh * 64 + 64)
                    o = hh * 192
                    k.mm(px[0:64, hs], ARa[bs, cs_], stb[bs, :], True, False, R=[bA, stb], W=[px])
                    k.mm(px[0:64, hs], gm[0:64, o + 64:o + 128], tk[0:64, hs], False, True, R=[gm, tk], W=[px])
                X = TB[9]
                k.act(X[0:64, 0:128], px[0:64, 0:128], AF.Copy, R=[px], W=[X])
                pu = bank[3]
                for hh in range(2):
                    hs = slice(hh * 64, hh * 64 + 64)
                    ms = slice((m0 + hh) * 64, (m0 + hh) * 64 + 64)
                    accb_, acc_ = half_bufs[n // 4]["Acc"]
                    k.mm(pu[0:64, hs], acc_[:, ms], X[0:64, hs], True, True, R=[accb_, X], W=[pu])
                U = TB[10]
                k.copy("dve", U[0:64, 0:128], pu[0:64, 0:128], R=[pu], W=[U])
                for hh in range(2):
                    bs = slice(64 * hh, 64 * hh + 64)
                    hs = slice(hh * 64, hh * 64 + 64)
                    o = hh * 192
                    k.mm(po[bs, cs_], stb[bs, :], ARr[bs, cs_], True, False, R=[stb, bR], W=[po])
                    k.mm(po[bs, cs_], U[0:64, hs], gm[0:64, o:o + 64], False, False, R=[U, gm], W=[po])
                    k.mm(po[bs, cs_], tk[0:64, hs], gm[0:64, o + 128:o + 192], False, True, R=[tk, gm], W=[po])
                pst = bank[2]
                for hh in range(2):
                    bs = slice(64 * hh, 64 * hh + 64)
                    hs = slice(hh * 64, hh * 64 + 64)
                    k.mm(pst[bs, 0:64], tk[0:64, 128 + hh * 64:128 + hh * 64 + 64], U[0:64, hs], True, False,
                         R=[tk, U], W=[pst])
                    k.mm(pst[bs, 0:64], tk[0:64, 256 + hh * 64:256 + hh * 64 + 64], tk[0:64, hs], False, True,
                         R=[tk], W=[pst])
                P.dbg.append(("A", l, t, p, n, k.cnt["dve"]))
                k.stt("dve", st[:, :], st[:, :], el_t[:, n:n + 1], pst[:, 0:64], ALU.mult, ALU.add,
                      R=[st, el_t, pst], W=[st])
                k.act(stb[:, :], st[:, :], AF.Copy, R=[st], W=[stb])

            def bf64(tl):
                return tl[0:64, 0:256].bitcast(BF16)

            half_bufs = [
                dict(Q32=(TK[2], TK[2][0:64, :]), Qb=(TB[0], TB[0][0:64, :]), Pb=(TB[1], TB[1][0:64, :]),
                     Q2b=(TB[2], TB[2][0:64, :]), P2b=(TB[3], TB[3][0:64, :]), Acc=(TB[4], TB[4][0:64, :]),
                     banks=(bank[4], bank[5], bank[6])),
                dict(Q32=(AR_[7], AR_[7][0:64, 0:512]), Qb=(AR_[10], bf64(AR_[10])), Pb=(AR_[12], bf64(AR_[12])),
                     Q2b=(AR_[13], bf64(AR_[13])), P2b=(AR_[15], bf64(AR_[15])), Acc=(AR_[17], bf64(AR_[17])),
                     banks=(bank[2], bank[3], bank[7])),
            ]

            def phase1_steps(hf):
                hb = half_bufs[hf]
                bA_, bB_, bC_ = hb["banks"]
                Q32b, Q32 = hb["Q32"]
                Accb_, Acc = hb["Acc"]
                steps = []

                def s0():
                    for hh in range(2):
                        bs = slice(64 * hh, 64 * hh + 64)
                        for c4 in range(4):
                            n = hf * 4 + c4
                            cs_ = slice(n * CH, (n + 1) * CH)
                            o = (c4 * 2 + hh) * 64
                            k.mm(bA_[0:64, o:o + 64], BKb[bs, cs_], ARa[bs, cs_], True, True, R=[bB, bA], W=[bA_])
                steps.append(s0)

                def s1():
                    k.tt("dve", v3(Q32), v3(bA_[0:64, :]), su8, ALU.mult, R=[bA_, cst], W=[Q32b])
                steps.append(s1)

                def s2():
                    k.copy("act", hb["Qb"][1], Q32, R=[Q32b], W=[hb["Qb"][0]])
                    for m in range(8):
                        ms = slice(m * 64, m * 64 + 64)
                        k.tr(bB_[0:64, ms], Q32[:, ms], ident[0:64, 0:64], R=[Q32b, cst], W=[bB_])
                steps.append(s2)

                def s3():
                    k.act(hb["Pb"][1], bB_[0:64, :], AF.Copy, R=[bB_], W=[hb["Pb"][0]])
                    k.tt("pool", v3(Acc), v3(Q32), id8, ALU.add, R=[Q32b, cst], W=[Accb_])
                steps.append(s3)
                names = ["Qb", "Pb", "Q2b", "P2b"]
                cur = [0, 1, 2, 3]
                for step in range(5):
                    cqn, cpn, nqn, npn = (names[i] for i in cur)

                    def sa(step=step, cqn=cqn, cpn=cpn):
                        (cqb, cq), (cpb, cp) = hb[cqn], hb[cpn]
                        for m in range(8):
                            ms = slice(m * 64, m * 64 + 64)
                            k.mm(bC_[0:64, ms], cq[:, ms], cp[:, ms], True, True, R=[cqb, cpb], W=[bC_])
                        if step < 4:
                            for m in range(8):
                                ms = slice(m * 64, m * 64 + 64)
                                k.mm(bA_[0:64, ms], cp[:, ms], cq[:, ms], True, True, R=[cqb, cpb], W=[bA_])
                    steps.append(sa)

                    def sb(step=step, nqn=nqn, npn=npn):
                        k.act(hb[npn][1], bC_[0:64, :], AF.Copy, R=[bC_], W=[hb[npn][0]])
                        if step < 4:
                            k.copy("dve", hb[nqn][1], bA_[0:64, :], R=[bA_], W=[hb[nqn][0]])
                    steps.append(sb)

                    def sc(npn=npn):
                        npb, np_ = hb[npn]
                        for m in range(8):
                            ms = slice(m * 64, m * 64 + 64)
                            k.mm(bB_[0:64, ms], np_[:, ms], Acc[:, ms], True, True, R=[npb, Accb_], W=[bB_])
                    steps.append(sc)

                    def sd():
                        k.tt("dve", Acc, Acc, bB_[0:64, :], ALU.add, R=[Accb_, bB_], W=[Accb_])
                    steps.append(sd)
                    cur = [cur[2], cur[3], cur[0], cur[1]]
                return steps

            st0, st1 = phase1_steps(0), phase1_steps(1)
            for fa, fb in zip(st0, st1):
                fa()
                fb()
            chk("A5")
            pre(0)
            for n in range(NCH):
                if n + 1 < NCH:
                    pre(n + 1)
                dep(n, (n % 4) * 2)
            o_t, cen_t = AR_[9], AR_[10]
            k.act(o_t[:, 0:TT], po[:, :], AF.Copy, R=[po], W=[o_t])
            if p == 0:
                tap(f"A_o_l{l}", o_t[:, 0:TT], [128, TT], [o_t])
            pb = bank[2]
            k.mm(pb[:, :], bd64, o_t[:, 0:TT], True, True, R=[cst, o_t], W=[pb])
            k.stt("dve", cen_t[:, 0:TT], pb[:, :], -1.0 / 64, o_t[:, 0:TT], ALU.mult, ALU.add, R=[pb, o_t], W=[cen_t])
            k.act(tmp_t[:, 0:TT], cen_t[:, 0:TT], AF.Square, R=[cen_t], W=[tmp_t])
            pb = bank[3]
            k.mm(pb[:, :], bd64, tmp_t[:, 0:TT], True, True, R=[cst, tmp_t], W=[pb])
            rstd_from(pb[:, :], pb, rs_t, 64e-5, 1.0 / 64)
            k.tt("dve", cen_t[:, 0:TT], cen_t[:, 0:TT], rs_t[:, 0:TT], ALU.mult, R=[cen_t, rs_t], W=[cen_t])
            k.ts("dve", cen_t[:, 0:TT], cen_t[:, 0:TT], pc(Lp + "lnxw", p), ALU.mult, R=[cen_t, par], W=[cen_t],
                 s2=pc(Lp + "lnxb", p), op1=ALU.add)
            k.tt("pool", cen_t[:, 0:TT], cen_t[:, 0:TT], bon_t[:, 0:TT], ALU.add, R=[cen_t, bon_t], W=[cen_t])
            k.tt("dve", yT[0][:, p, :], cen_t[:, 0:TT], g_t[:, 0:TT], ALU.mult, R=[cen_t, g_t], W=[yT[0]])

    def mixer_B(l, t):
        Lp = f"L{l}_"
        gkl_t = AR_[2]
        bufs, tap_, n = win_b[l]["B_lora"]
        wb = load_w(bufs, tap_, D, n)
        pb = next_bank()
        proj(wb, 0, 16, pb[0:16, :], pb)
        k.act(gkl_t[0:16, 0:TT], pb[0:16, :], AF.Copy, R=[pb], W=[gkl_t])
        for p in range(2):
            q_t, k_t = AR_[3], AR_[4]
            v_t = [AR_[5], AR_[6]]
            g_t = [AR_[7], AR_[8]]
            bufs, tap_, n = win_b[l][f"B_pair{p}a"]
            wb = load_w(bufs, tap_, D, n)
            for (c0, dst) in ((0, q_t), (128, k_t)):
                pb = next_bank()
                proj(wb, c0, 128, pb[:, :], pb)
                k.act(dst[:, 0:TT], pb[:, :], AF.Copy, R=[pb], W=[dst])
            bufs, tap_, n = win_b[l][f"B_pair{p}b"]
            wb = load_w(bufs, tap_, D, n)
            for (c0, dst, fn) in ((0, v_t[0], AF.Copy), (128, v_t[1], AF.Copy), (256, g_t[0], AF.Silu), (384, g_t[1], AF.Silu)):
                pb = next_bank()
                proj(wb, c0, 128, pb[:, :], pb)
                k.act(dst[:, 0:TT], pb[:, :], fn, R=[pb], W=[dst])
            lg_t, bc_t, ep_t, en_t = AR_[9], AR_[10], AR_[11], AR_[12]
            pb = bank[2]
            k.mm(pb[:, :], sm[0:16, SM_GK + p * 128:SM_GK + (p + 1) * 128], gkl_t[0:16, 0:TT], True, True,
                 R=[sm, gkl_t], W=[pb])
            k.act(lg_t[:, 0:TT], pb[:, :], AF.Sigmoid, R=[pb, par], W=[lg_t], bias=pc(Lp + "gkb", p))
            k.act(lg_t[:, 0:TT], lg_t[:, 0:TT], AF.Ln, R=[lg_t], W=[lg_t])
            k.op("dve", lambda e, bc_t=bc_t, lg_t=lg_t: e.tensor_tensor_scan(
                bc_t[:, 0:TT], cst[:, C_RM:C_RM + TT], lg_t[:, 0:TT], 0.0, ALU.mult, ALU.add),
                [cst, lg_t], [bc_t])
            k.act(ep_t[:, 0:TT], bc_t[:, 0:TT], AF.Exp, R=[bc_t], W=[ep_t], scale=1.0 / 16)
            k.act(en_t[:, 0:TT], bc_t[:, 0:TT], AF.Exp, R=[bc_t], W=[en_t], scale=-1.0 / 16)
            def bfv(tl):
                return tl[:, 0:256].bitcast(BF16)

            bQd, bKd = AR_[13], AR_[16]
            qd_b, kd_b = bfv(bQd), bfv(bKd)
            kd_t, kdec_t = AR_[14], AR_[15]
            k.stt("dve", qd_b, q_t[:, 0:TT], 0.125, ep_t[:, 0:TT], ALU.mult, ALU.mult,
                  R=[q_t, ep_t], W=[bQd])
            k.tt("pool", kd_t[:, 0:TT], k_t[:, 0:TT], en_t[:, 0:TT], ALU.mult, R=[k_t, en_t], W=[kd_t])
            k.copy("act", kd_b, kd_t[:, 0:TT], R=[kd_t], W=[bKd])
            el_t = tiny[0]
            k.copy("dve", el_t[:, 0:NCH], ep_t[:, CH - 1:TT:CH], R=[ep_t], W=[el_t])
            elb = el_t[:, 0:NCH].unsqueeze(2).to_broadcast([128, NCH, CH])
            k.tt("dve", kdec_t[:, 0:TT].rearrange("p (n c) -> p n c", c=CH),
                 kd_t[:, 0:TT].rearrange("p (n c) -> p n c", c=CH), elb, ALU.mult, R=[kd_t, el_t], W=[kdec_t])
            st = stB[l][p]
            stb = stBb[l][p]
            po = [bank[6], bank[7]]
            iub = cst[0:64, C_IU:C_IU + 64].unsqueeze(1).to_broadcast([64, 2, 64])

            def pre(n):
                cs_ = slice(n * CH, (n + 1) * CH)
                pt = bank[4]
                k.tr(pt[0:64, 0:128], kdec_t[:, cs_], ident, R=[kdec_t, cst], W=[pt])
                k.tr(pt[0:64, 128:256], v_t[0][:, cs_], ident, R=[v_t[0], cst], W=[pt])
                k.tr(pt[0:64, 256:384], v_t[1][:, cs_], ident, R=[v_t[1], cst], W=[pt])
                tk = TB[5 + n % 2]
                k.act(tk[0:64, 0:384], pt[0:64, 0:384], AF.Copy, R=[pt], W=[tk])
                pa_ = bank[5]
                for hh in range(2):
                    bs = slice(64 * hh, 64 * hh + 64)
                    k.mm(pa_[0:64, hh * 64:hh * 64 + 64], kd_b[bs, cs_], qd_b[bs, cs_], True, True,
                         R=[bKd, bQd], W=[pa_])
                att = TB[7 + n % 2]
                k.tt("dve", att[0:64, 0:128].rearrange("p (h c) -> p h c", h=2),
                     pa_[0:64, 0:128].rearrange("p (h c) -> p h c", h=2), iub, ALU.mult, R=[pa_, cst], W=[att])
                pu = bank[3 - n % 2]
                for hh in range(2):
                    bs = slice(64 * hh, 64 * hh + 64)
                    vs = slice(128 + hh * 128, 256 + hh * 128)
                    k.mm(pu[bs, 0:128], tk[0:64, hh * 64:hh * 64 + 64], tk[0:64, vs], True, True, R=[tk], W=[pu])

            def dep(n):
                cs_ = slice(n * CH, (n + 1) * CH)
                tk = TB[5 + n % 2]
                att = TB[7 + n % 2]
                pu = bank[3 - n % 2]
                for hh in range(2):
                    bs = slice(64 * hh, 64 * hh + 64)
                    vs = slice(128 + hh * 128, 256 + hh * 128)
                    k.mm(po[hh][:, cs_], tk[0:64, vs], att[0:64, hh * 64:hh * 64 + 64], True, False,
                         R=[tk, att], W=[po[hh]])
                    k.mm(po[hh][:, cs_], stb[bs, :], qd_b[bs, cs_], False, True, R=[stb, bQd], W=[po[hh]])
                k.stt("dve", st[:, :], st[:, :], el_t[:, n:n + 1], pu[:, 0:128], ALU.mult, ALU.add,
                      R=[st, el_t, pu], W=[st])
                k.act(stb[:, :], st[:, :], AF.Copy, R=[st], W=[stb])

            pre(0)
            for n in range(NCH):
                if n + 1 < NCH:
                    pre(n + 1)
                dep(n)
            for hh in range(2):
                h = 2 * p + hh
                sq_t, rs_t = AR_[9], AR_[10]
                k.act(sq_t[:, 0:TT], po[hh][:, :], AF.Square, R=[po[hh]], W=[sq_t])
                pb = bank[2 + hh]
                k.mm(pb[:, :], ones, sq_t[:, 0:TT], True, True, R=[cst, sq_t], W=[pb])
                rstd_from(pb[:, :], pb, rs_t, 1e-5, 1.0 / 128)
                k.tt("dve", sq_t[:, 0:TT], po[hh][:, :], rs_t[:, 0:TT], ALU.mult, R=[po[hh], rs_t], W=[sq_t])
                if h == 0:
                    tap(f"B_on_l{l}", sq_t[:, 0:TT], [128, TT], [sq_t])
                k.stt("dve", yT[1][:, h, :], sq_t[:, 0:TT], pc(Lp + "bnw", 0), g_t[hh][:, 0:TT], ALU.mult, ALU.mult,
                      R=[sq_t, g_t[hh], par], W=[yT[1]])

    def mixer_C(l, t):
        Lp = f"L{l}_"
        bufs, tap_, n_ = win_b[l]["C_ab"]
        wb = load_w(bufs, tap_, D, n_)
        pab = bank[2]
        for n in range(NCH):
            for kc in range(KC):
                k.mm(pab[0:64, n * 8:n * 8 + 8], hTb[:, kc, n * CH:(n + 1) * CH], wb[:, kc, 0:8],
                     kc == 0, kc == KC - 1, R=[wb, hTb], W=[pab])
        g_tok, be_tok, gam, eg, egl, elast = tiny[0], tiny[1], tiny[2], tiny[3], tiny[4], tiny[5]
        ab3 = pab[0:64, 0:64].rearrange("p (n c) -> p n c", c=8)
        dtb = par[0:64, PM.off[Lp + "dtb"]:PM.off[Lp + "dtb"] + 4].unsqueeze(1).to_broadcast([64, NCH, 4])
        nea = lay[l][0:64, 0:4].unsqueeze(1).to_broadcast([64, NCH, 4])
        g3 = g_tok[0:64, 0:32].rearrange("p (n c) -> p n c", c=4)
        k.tt("dve", g3, ab3[:, :, 0:4], dtb, ALU.add, R=[pab, par], W=[g_tok])
        k.act(g_tok[0:64, 0:32], g_tok[0:64, 0:32], AF.Exp, R=[g_tok], W=[g_tok])
        k.act(g_tok[0:64, 0:32], g_tok[0:64, 0:32], AF.Ln, R=[g_tok], W=[g_tok], bias=1.0)
        k.tt("dve", g3, g3, nea, ALU.mult, R=[g_tok, lay[l]], W=[g_tok])
        k.act(be_tok[0:64, 0:32].rearrange("p (n c) -> p n c", c=4), ab3[:, :, 4:8], AF.Sigmoid, R=[pab], W=[be_tok])
        pg = bank[3]
        k.mm(pg[0:64, 0:32], cst[0:64, C_IU:C_IU + 64], g_tok[0:64, 0:32], True, True, R=[cst, g_tok], W=[pg])
        k.mm(pg[:, 32:64], ones[0:64, :], g_tok[0:64, 0:32], True, True, R=[cst, g_tok], W=[pg])
        k.act(gam[0:64, 0:32], pg[0:64, 0:32], AF.Copy, R=[pg], W=[gam])
        k.act(eg[0:64, 0:32], pg[0:64, 0:32], AF.Exp, R=[pg], W=[eg], scale=1.0)
        k.act(elast[:, 0:32], pg[:, 32:64], AF.Exp, R=[pg], W=[elast])
        k.tt("dve", egl[0:64, 0:32], pg[0:64, 32:64], gam[0:64, 0:32], ALU.subtract, R=[pg, gam], W=[egl])
        k.act(egl[0:64, 0:32], egl[0:64, 0:32], AF.Exp, R=[egl], W=[egl])
        nbeg = tiny[6]
        k.ts("pool", nbeg[0:64, 0:32], eg[0:64, 0:32], -1.0, ALU.mult, R=[eg], W=[nbeg])
        for h in range(4):
            bufs, tap_, n_ = win_b[l][f"C_head{h}"]
            wb = load_w(bufs, tap_, D, n_)
            Wk = AR_[2]
            q_t, k_t, v_t, z_t = AR_[3], AR_[4], AR_[5], AR_[6]
            for (c0, ci, dst) in ((0, h, q_t), (128, 4 + h, k_t), (256, 8 + h, v_t)):
                pb = next_bank()
                proj(wb, c0, 128, pb[:, :], pb)
                k.act(Wk[:, 3:515], pb[:, :], AF.Copy, R=[pb], W=[Wk])
                k.copy("dve", Wk[:, 0:3], carC[l][:, ci, :], R=[carC[l]], W=[Wk])
                k.copy("pool", carC[l][:, ci, :], Wk[:, 512:515], R=[Wk], W=[carC[l]])
                o = PM.off[Lp + "ccw"]
                k.ts("dve", dst[:, 0:TT], Wk[:, 0:512], par[:, o + ci:o + ci + 1], ALU.mult, R=[Wk, par], W=[dst])
                for j in range(1, 4):
                    k.stt("dve", dst[:, 0:TT], Wk[:, j:j + 512], par[:, o + 12 * j + ci:o + 12 * j + ci + 1],
                          dst[:, 0:TT], ALU.mult, ALU.add, R=[Wk, dst, par], W=[dst])
                k.act(dst[:, 0:TT], dst[:, 0:TT], AF.Silu, R=[dst], W=[dst])
            pb = next_bank()
            proj(wb, 384, 128, pb[:, :], pb)
            k.act(z_t[:, 0:TT], pb[:, :], AF.Silu, R=[pb], W=[z_t])
            sq_t, rs_t = AR_[7], AR_[8]
            for (src, scl) in ((q_t, 128.0 ** -0.5), (k_t, 1.0)):
                k.act(sq_t[:, 0:TT], src[:, 0:TT], AF.Square, R=[src], W=[sq_t])
                pb = bank[2]
                k.mm(pb[:, :], ones, sq_t[:, 0:TT], True, True, R=[cst, sq_t], W=[pb])
                rstd_from(pb[:, :], pb, rs_t, 1e-6, 1.0)
                k.stt("dve", src[:, 0:TT], src[:, 0:TT], scl, rs_t[:, 0:TT], ALU.mult, ALU.mult,
                      R=[src, rs_t], W=[src])
            if h == 0:
                tap(f"C_k_l{l}", k_t[:, 0:TT], [128, TT], [k_t])
                tap(f"C_v_l{l}", v_t[:, 0:TT], [128, TT], [v_t])
            st = stC[l][h]
            stb = stCb[l][h]
            po = bank[7]

            def bfv(tl):
                return tl[:, 0:256].bitcast(BF16)

            def v3(ap_):
                return ap_.rearrange("p (n c) -> p n c", c=CH)

            bKb, bQb, bQg = AR_[13], AR_[14], AR_[15]
            k_tb, q_tb, qg_b = bfv(bKb), bfv(bQb), bfv(bQg)
            k.copy("act", k_tb, k_t[:, 0:TT], R=[k_t], W=[bKb])
            k.copy("pool", q_tb, q_t[:, 0:TT], R=[q_t], W=[bQb])
            gcol = gam[0:64, h:32:4].unsqueeze(2).to_broadcast([64, NCH, CH])
            bcol = be_tok[0:64, h:32:4].unsqueeze(2).to_broadcast([64, NCH, CH])
            su8 = cst[0:64, C_SU:C_SU + 64].unsqueeze(1).to_broadcast([64, 8, 64])
            iu8 = cst[0:64, C_IU:C_IU + 64].unsqueeze(1).to_broadcast([64, 8, 64])
            id8 = ident[0:64, 0:64].unsqueeze(1).to_broadcast([64, 8, 64])
            dg, DT, Q32, qk32 = TK[1], TK[2], TK[3], TK[4]
            k.tt("pool", v3(dg[0:64, :]), id8, gcol, ALU.mult, R=[cst, gam], W=[dg])
            pgr = bank[5]
            k.mm(pgr[:, :], ones[0:64, :], dg[0:64, :], True, True, R=[cst, dg], W=[pgr])
            k.tt("dve", v3(DT[0:64, :]), v3(pgr[0:64, :]), gcol, ALU.subtract, R=[pgr, gam], W=[DT])
            k.ts("pool", DT[0:64, :], DT[0:64, :], 0.0, ALU.min, R=[DT], W=[DT])
            k.act(DT[0:64, :], DT[0:64, :], AF.Exp, R=[DT], W=[DT])
            eg_r = AR_[9]
            k.act(eg_r[:, 0:TT], pgr[:, :], AF.Exp, R=[pgr], W=[eg_r])
            k.tt("dve", qg_b, eg_r[:, 0:TT], q_t[:, 0:TT], ALU.mult, R=[eg_r, q_t], W=[bQg])
            pkk, pqk = bank[6], bank[4]
            for n in range(NCH):
                cs_ = slice(n * CH, (n + 1) * CH)
                k.mm(pkk[0:64, cs_], k_tb[:, cs_], k_tb[:, cs_], True, True, R=[bKb], W=[pkk])
            for n in range(NCH):
                cs_ = slice(n * CH, (n + 1) * CH)
                k.mm(pqk[0:64, cs_], k_tb[:, cs_], q_tb[:, cs_], True, True, R=[bKb, bQb], W=[pqk])
            k.tt("dve", Q32[0:64, :], pkk[0:64, :], DT[0:64, :], ALU.mult, R=[pkk, DT], W=[Q32])
            k.tt("dve", v3(Q32[0:64, :]), v3(Q32[0:64, :]), bcol, ALU.mult, R=[Q32, be_tok], W=[Q32])
            k.stt("dve", v3(Q32[0:64, :]), v3(Q32[0:64, :]), -1.0, su8, ALU.mult, ALU.mult, R=[Q32, cst], W=[Q32])
            Qb, Pb, Q2b, P2b, Accb, QKD = TB[0], TB[1], TB[2], TB[3], TB[4], TB[7]
            k.tt("dve", qk32[0:64, :], pqk[0:64, :], DT[0:64, :], ALU.mult, R=[pqk, DT], W=[qk32])
            k.tt("pool", v3(QKD[0:64, :]), v3(qk32[0:64, :]), iu8, ALU.mult, R=[qk32, cst], W=[QKD])
            k.copy("act", Qb[0:64, :], Q32[0:64, :], R=[Q32], W=[Qb])
            pp = bank[5]
            for m in range(8):
                ms = slice(m * 64, m * 64 + 64)
                k.tr(pp[0:64, ms], Q32[0:64, ms], ident[0:64, 0:64], R=[Q32, cst], W=[pp])
            k.act(Pb[0:64, :], pp[0:64, :], AF.Copy, R=[pp], W=[Pb])
            k.tt("pool", v3(Accb[0:64, :]), v3(Q32[0:64, :]), id8, ALU.add, R=[Q32, cst], W=[Accb])
            cq, cp, nq, np_ = Qb, Pb, Q2b, P2b
            for step in range(5):
                ps1 = bank[6]
                for m in range(8):
                    ms = slice(m * 64, m * 64 + 64)
                    k.mm(ps1[0:64, ms], cq[0:64, ms], cp[0:64, ms], True, True, R=[cq, cp], W=[ps1])
                k.act(np_[0:64, :], ps1[0:64, :], AF.Copy, R=[ps1], W=[np_])
                if step < 4:
                    ps2 = bank[4]
                    for m in range(8):
                        ms = slice(m * 64, m * 64 + 64)
                        k.mm(ps2[0:64, ms], cp[0:64, ms], cq[0:64, ms], True, True, R=[cq, cp], W=[ps2])
                    k.copy("dve", nq[0:64, :], ps2[0:64, :], R=[ps2], W=[nq])
                pa_ = bank[5]
                for m in range(8):
                    ms = slice(m * 64, m * 64 + 64)
                    k.mm(pa_[0:64, ms], np_[0:64, ms], Accb[0:64, ms], True, True, R=[np_, Accb], W=[pa_])
                k.tt("dve", Accb[0:64, :], Accb[0:64, :], pa_[0:64, :], ALU.add, R=[Accb, pa_], W=[Accb])
                cq, cp, nq, np_ = nq, np_, cq, cp

            def pre(n):
                cs_ = slice(n * CH, (n + 1) * CH)
                ci = n * 4 + h
                pt = bank[6]
                k.tr(pt[0:64, 0:128], k_t[:, cs_], ident, R=[k_t, cst], W=[pt])
                k.tr(pt[0:64, 128:256], v_t[:, cs_], ident, R=[v_t, cst], W=[pt])
                tk = TB[5 + n % 2]
                k.ts("dve", tk[0:64, 0:128], pt[0:64, 0:128], egl[0:64, ci:ci + 1], ALU.mult, R=[pt, egl], W=[tk])
                k.act(tk[0:64, 128:256], pt[0:64, 128:256], AF.Copy, R=[pt], W=[tk])

            def dep(n):
                cs_ = slice(n * CH, (n + 1) * CH)
                ci = n * 4 + h
                ms = slice(n * 64, n * 64 + 64)
                tk = TB[5 + n % 2]
                pks = bank[3]
                k.mm(pks[0:64, 0:128], k_tb[:, cs_], stb[:, :], True, True, R=[bKb, stb], W=[pks])
                Rp = TB[8]
                k.stt("dve", Rp[0:64, 0:128], pks[0:64, 0:128], nbeg[0:64, ci:ci + 1], tk[0:64, 128:256],
                      ALU.mult, ALU.add, R=[pks, nbeg, tk], W=[Rp])
                pv = bank[2]
                k.mm(pv[0:64, 0:128], Accb[0:64, ms], Rp[0:64, 0:128], True, True, R=[Accb, Rp], W=[pv])
                vn = TB[9]
                k.ts("dve", vn[0:64, 0:128], pv[0:64, 0:128], be_tok[0:64, ci:ci + 1], ALU.mult,
                     R=[pv, be_tok], W=[vn])
                k.mm(po[:, cs_], stb[:, :], qg_b[:, cs_], True, False, R=[stb, bQg], W=[po])
                k.mm(po[:, cs_], vn[0:64, 0:128], QKD[0:64, ms], False, True, R=[vn, QKD], W=[po])
                pst = bank[4]
                k.mm(pst[:, 0:128], tk[0:64, 0:128], vn[0:64, 0:128], True, True, R=[tk, vn], W=[pst])
                P.dbg.append(("C", l, t, h, n, k.cnt["dve"]))
                k.stt("dve", st[:, :], st[:, :], elast[:, ci:ci + 1], pst[:, 0:128], ALU.mult, ALU.add,
                      R=[st, elast, pst], W=[st])
                k.act(stb[:, :], st[:, :], AF.Copy, R=[st], W=[stb])

            pre(0)
            for n in range(NCH):
                if n + 1 < NCH:
                    pre(n + 1)
                dep(n)
            k.act(sq_t[:, 0:TT], po[:, :], AF.Square, R=[po], W=[sq_t])
            pb = bank[2]
            k.mm(pb[:, :], ones, sq_t[:, 0:TT], True, True, R=[cst, sq_t], W=[pb])
            rstd_from(pb[:, :], pb, rs_t, 1e-5, 1.0 / 128)
            k.tt("dve", sq_t[:, 0:TT], po[:, :], rs_t[:, 0:TT], ALU.mult, R=[po, rs_t], W=[sq_t])
            if h == 0:
                tap(f"C_on_l{l}", sq_t[:, 0:TT], [128, TT], [sq_t])
            k.stt("dve", yT[2][:, h, :], sq_t[:, 0:TT], pc(Lp + "cnw", 0), z_t[:, 0:TT], ALU.mult, ALU.mult,
                  R=[sq_t, z_t, par], W=[yT[2]])

    def merge_out(l, t):
        Lp = f"L{l}_"
        def mg(c):
            return AR_[2 + c // 2], AR_[2 + c // 2][:, 0:512].bitcast(BF16)[:, (c % 2) * 512:(c % 2) * 512 + 512]

        for cb in range(4):
            for i in range(3):
                b, tap_ = gate_b[l][i * 4 + cb]
                wg = load_w([b], tap_, D, 512)
                bb_, btap = branch_b[l][i]
                wbr = wbr_buf[wbr_sel[0] % 2]
                wbr_sel[0] += 1
                k.dma("sp", wbr[:, :, :], btap[:, cb * 512:(cb + 1) * 512].rearrange("(kc p) n -> p kc n", p=128),
                      reads=[bb_], writes=[wbr])
                for cc in range(4):
                    c = cb * 4 + cc
                    pg_ = next_bank()
                    proj(wg, cc * 128, 128, pg_[:, :], pg_)
                    pbr = bank[2 + (cc % 2)]
                    for kc in range(4):
                        k.mm(pbr[:, :], wbr[:, kc, cc * 128:(cc + 1) * 128], yT[i][:, kc, :], kc == 0, kc == 3,
                             R=[wbr, yT[i]], W=[pbr])
                    sg = AR_[14 + (cc % 2)]
                    k.act(sg[:, 0:TT], pg_[:, :], AF.Sigmoid, R=[pg_], W=[sg])
                    acc = AR_[16 + cc]
                    if i == 0:
                        k.tt("dve", acc[:, 0:TT], sg[:, 0:TT], pbr[:, :], ALU.mult, R=[sg, pbr], W=[acc])
                    else:
                        k.tt("dve", sg[:, 0:TT], sg[:, 0:TT], pbr[:, :], ALU.mult, R=[sg, pbr], W=[sg])
                        if i == 1:
                            k.tt("pool", acc[:, 0:TT], acc[:, 0:TT], sg[:, 0:TT], ALU.add, R=[acc, sg], W=[acc])
                        else:
                            mb, mv = mg(c)
                            k.tt("pool", mv, acc[:, 0:TT], sg[:, 0:TT], ALU.add, R=[acc, sg], W=[mb])
        if t == 0:
            mb, mv = mg(0)
            tmpf = AR_[20]
            k.copy("dve", tmpf[:, 0:TT], mv, R=[mb], W=[tmpf])
            tap(f"merged_l{l}", tmpf[:, 0:TT], [128, TT], [tmpf])
        for cb in range(4):
            b, tap_ = wout_b[l][cb]
            wo = load_w([b], tap_, D, 512)
            for cc in range(4):
                c = cb * 4 + cc
                pz = next_bank()
                for kc in range(KC):
                    mb, mv = mg(kc)
                    k.mm(pz[:, :], wo[:, kc, cc * 128:(cc + 1) * 128], mv, kc == 0, kc == KC - 1, R=[wo, mb], W=[pz])
                k.stt("dve", hT[c][:, :], hT[c][:, :], ALPHA, pz[:, :], ALU.mult, ALU.add, R=[hT[c], pz], W=[hT[c]])
        layer_norm(Lp + "ln1g", Lp + "ln1b")

    def ffn(l, t):
        Lp = f"L{l}_"
        ofw = PM.off[Lp + "fcw"]
        ofb = PM.off[Lp + "fcb"]

        def aT(c):
            return AR_[2 + c // 2], AR_[2 + c // 2][:, 0:512].bitcast(BF16)[:, (c % 2) * 512:(c % 2) * 512 + 512]

        Wk = [AR_[0], AR_[1]]
        cv = [AR_[24], AR_[25]]
        for j in range(22):
            bufs, tap_ = wup_b[l][j]
            wu = load_w(bufs, tap_, D, 512)
            for cc in range(2):
                c = j * 2 + cc
                res = []
                for half in range(2):
                    ci = half * 44 + c
                    pb = next_bank()
                    proj(wu, half * 256 + cc * 128, 128, pb[:, :], pb)
                    W_ = Wk[half]
                    k.act(W_[:, 2:514], pb[:, :], AF.Copy, R=[pb], W=[W_])
                    k.copy("dve", W_[:, 0:2], carF[l][:, ci, :], R=[carF[l]], W=[W_])
                    k.copy("pool", carF[l][:, ci, :], W_[:, 512:514], R=[W_], W=[carF[l]])
                    dst = cv[half]
                    k.ts("dve", dst[:, 0:TT], W_[:, 0:512], par[:, ofw + ci:ofw + ci + 1], ALU.mult,
                         R=[W_, par], W=[dst], s2=par[:, ofb + ci:ofb + ci + 1], op1=ALU.add)
                    for jj in range(1, 3):
                        k.stt("dve" if jj == 1 else "pool", dst[:, 0:TT], W_[:, jj:jj + 512],
                              par[:, ofw + 88 * jj + ci:ofw + 88 * jj + ci + 1], dst[:, 0:TT], ALU.mult, ALU.add,
                              R=[W_, dst, par], W=[dst])
                    res.append(dst)
                k.act(res[0][:, 0:TT], res[0][:, 0:TT], AF.Silu, R=[res[0]], W=[res[0]])
                ab, av = aT(c)
                k.tt("dve", av, res[0][:, 0:TT], res[1][:, 0:TT], ALU.mult, R=[res[0], res[1]], W=[ab])
        for c in range(16):
            b, tap_ = wdown_b[l][c]
            wd = wbuf[wsel[0] % 2]
            wsel[0] += 1
            wdv = wd[:, :, :].rearrange("p a b -> p (a b)")[:, 0:44 * 128].rearrange("p (a b) -> p a b", b=128)
            k.dma("sp", wdv, tap_.rearrange("(kc p) n -> p kc n", p=128), reads=[b], writes=[wd])
            pz = next_bank()
            for kc in range(44):
                ab, av = aT(kc)
                k.mm(pz[:, :], wdv[:, kc, :], av, kc == 0, kc == 43, R=[wd, ab], W=[pz])
            k.stt("dve", hT[c][:, :], hT[c][:, :], ALPHA, pz[:, :], ALU.mult, ALU.add, R=[hT[c], pz], W=[hT[c]])
        layer_norm(Lp + "ln2g", Lp + "ln2b")

    marks = []
    P.marks = marks

    def chk(name):
        if stop == name:
            raise _Stop()

    marks_act = []
    P.marks_act = marks_act
    P.dbg = []

    def mark(name):
        marks.append((name, k.cnt["pe"]))
        marks_act.append((name, k.cnt["dve"]))

    try:
      for t in range(NT):
          for s in range(4):
              for j in range(4):
                  k.dma("sp", AR_[s * 4 + j][:, 0:512], x_d[t * TT + s * 128:t * TT + (s + 1) * 128, j * 512:(j + 1) * 512],
                        writes=[AR_[s * 4 + j]])
          for kc in range(KC):
              pb = next_bank()
              for s in range(4):
                  src = AR_[s * 4 + kc // 4]
                  k.tr(pb[:, s * 128:(s + 1) * 128], src[:, (kc % 4) * 128:(kc % 4 + 1) * 128], ident,
                       R=[src, cst], W=[pb])
              k.copy("dve" if kc % 2 == 0 else "act", hT[kc][:, :], pb[:, :], R=[pb], W=[hT[kc]])
          layer_norm("lning", "lninb")
          if t == 0:
              tap("h0", hT[0][:, :], [128, TT], [hT[0]])
          chk("ln0")
          mark(f"t{t}_pre_end")
          for l in range(NL):
              mark(f"t{t}_l{l}_A")
              if SCHED and stop is None:
                  k.begin_sched()
              mixer_A(l, t)
              if SCHED and stop is None:
                  k.end_sched()
              if stop == "A":
                  tmpf = AR_[20]
                  k.copy("dve", tmpf[:, 0:TT], yT[0][:, 0, :], R=[yT[0]], W=[tmpf])
                  tap(f"y0_l{l}", tmpf[:, 0:TT], [128, TT], [tmpf])
              chk("A")
              mark(f"t{t}_l{l}_B")
              if SCHED and stop is None:
                  k.begin_sched()
              mixer_B(l, t)
              if SCHED and stop is None:
                  k.end_sched()
              if stop == "B":
                  tmpf = AR_[20]
                  k.copy("dve", tmpf[:, 0:TT], yT[1][:, 0, :], R=[yT[1]], W=[tmpf])
                  tap(f"y1_l{l}", tmpf[:, 0:TT], [128, TT], [tmpf])
              chk("B")
              mark(f"t{t}_l{l}_C")
              if SCHED and stop is None:
                  k.begin_sched()
              mixer_C(l, t)
              if SCHED and stop is None:
                  k.end_sched()
              if t == 0:
                  for i in range(3):
                      if i > {"A": 0, "B": 1}.get(stop, 2):
                          break
                      tmpf = AR_[20]
                      k.copy("dve", tmpf[:, 0:TT], yT[i][:, 0, :], R=[yT[i]], W=[tmpf])
                      tap(f"y{i}_l{l}", tmpf[:, 0:TT], [128, TT], [tmpf])
              mark(f"t{t}_l{l}_merge")
              merge_out(l, t)
              if t == 0:
                  tap(f"h1_l{l}", hT[0][:, :], [128, TT], [hT[0]])
              mark(f"t{t}_l{l}_ffn")
              ffn(l, t)
              mark(f"t{t}_l{l}_end")
              if t == 0:
                  tap(f"h2_l{l}", hT[0][:, :], [128, TT], [hT[0]])
          for s in range(4):
              for j in range(4):
                  pb = next_bank()
                  for q in range(4):
                      kc = j * 4 + q
                      k.tr(pb[:, q * 128:(q + 1) * 128], hT[kc][:, s * 128:(s + 1) * 128], ident, R=[hT[kc], cst], W=[pb])
                  ot = AR_[(s * 4 + j) % 8 + 2]
                  k.copy("act" if (s + j) % 2 else "dve", ot[:, 0:512], pb[:, :], R=[pb], W=[ot])
                  k.dma("sp", out_d[t * TT + s * 128:t * TT + (s + 1) * 128, j * 512:(j + 1) * 512], ot[:, 0:512],
                        reads=[ot], writes=[outb], is_output=True)
    except _Stop:
        pass
    k.emit()
    P.nc = nc
    P.k = k
    P.taps = taps
    return P


_CACHE = {}


def kernel(**inputs):
    inp = {kk_: np.asarray(v) for kk_, v in inputs.items()}
    x = inp["x"]
    B = x.shape[0]
    NT = x.shape[1] // TT
    if "prog" not in _CACHE:
        _CACHE["prog"] = build_program(NT=NT, NL=2)
    P = _CACHE["prog"]
    par = pack_params(inp)
    sm = pack_small(inp)
    cst = make_consts()
    shared = dict(
        w_in=np.ascontiguousarray(inp["w_in"], np.float32),
        w_gate=np.ascontiguousarray(inp["w_gate"], np.float32),
        w_branch=np.ascontiguousarray(inp["w_branch"], np.float32),
        w_out=np.ascontiguousarray(inp["w_out"], np.float32),
        w_up=np.ascontiguousarray(inp["w_up"], np.float32),
        w_down=np.ascontiguousarray(inp["w_down"], np.float32),
        par=par, sm=sm, cst=cst)
    in_maps = []
    for c in range(8):
        b = c % B
        m = dict(shared)
        m["x"] = np.ascontiguousarray(x[b], np.float32)
        in_maps.append(m)
    res = run_bass_kernel_spmd(P.nc, in_maps, core_ids=list(range(8)))
    out = np.stack([res.results[b]["out"] for b in range(B)], axis=0)
    return out.astype(np.float32)
```
